# Optimizing a Trainium2 kernel written in Bass

```python
import jax
import jax.numpy as jnp
from jax import lax
import numpy as np

D_MODEL = 1024
BATCH = 1
SEQ = 16384
DEPTH = 4

GRID_W = 64
CTX_LEN = 256
N_MIXERS = 3
RMS_EPS = 1e-6
ADA_CHUNKS = 6

D_RNN = D_MODEL
RG_BLOCK_W = 256
RG_BLOCKS = D_RNN // RG_BLOCK_W
CONV_WIDTH = 4
CONV_PAD = (2, 1)
LRU_C = 8.0

NA_HEAD_DIM = 64
NA_HEADS = D_MODEL // NA_HEAD_DIM
NA_ROWS_MAX = 8
NA_COLS = 16

FT_GROUPS = 4
FT_GROUP_W = D_MODEL // FT_GROUPS

D_FF = 7 * D_MODEL // 2
N_EXPERTS = 8
TOP_K = 2

kernel_name = "hybrid_rglru_natten_fnet_moe_dit"


def _layer_counts(depth):
    n_mix = [sum(1 for i in range(depth) if i % N_MIXERS == m) for m in range(N_MIXERS)]
    n_dense = sum(1 for i in range(depth) if i % 2 == 0)
    return n_mix, n_dense, depth - n_dense


def rmsnorm(x, g):
    xf = x.astype(jnp.float32)
    y = xf * lax.rsqrt(jnp.mean(xf * xf, axis=-1, keepdims=True) + RMS_EPS)
    return (y * g.astype(jnp.float32)).astype(x.dtype)


def dwconv_centred(z, w, b):
    y = lax.conv_general_dilated(z, w[:, None, :].astype(z.dtype), window_strides=(1,), padding=[CONV_PAD], dimension_numbers=("NWC", "WIO", "NWC"), feature_group_count=z.shape[-1])
    return y + b


def block_diag(z, w):
    bsz, length, _ = z.shape
    zb = z.reshape(bsz, length, RG_BLOCKS, RG_BLOCK_W)
    return jnp.einsum("blnc,ncd->blnd", zb, w).reshape(bsz, length, D_RNN)


def linear_scan(a, b, h0):
    def combine(p, q):
        return p[0] * q[0], q[0] * p[1] + q[1]
    a_cum, b_cum = lax.associative_scan(combine, (a, b), axis=1)
    return a_cum * h0[:, None, :] + b_cum


def rglru_coeffs(z, wa, ba, wi, bi, lam):
    zf = z.astype(jnp.float32)
    r = jax.nn.sigmoid(block_diag(zf, wa) + ba)
    i = jax.nn.sigmoid(block_diag(zf, wi) + bi)
    log_a = -LRU_C * r * jax.nn.softplus(-lam.astype(jnp.float32))
    a = jnp.exp(log_a)
    b = jnp.sqrt(-jnp.expm1(2.0 * log_a)) * (i * zf)
    return a, b


def rglru_mixer(h, hc, w_in, conv_w, conv_b, wa, ba, wi, bi, lam, w_out, need_ctx):
    def branches(z):
        xz, gz = jnp.split(z @ w_in, 2, axis=-1)
        return dwconv_centred(xz, conv_w, conv_b), gz
    xl, gl = branches(h)
    xc, gc = branches(hc)
    zeros = jnp.zeros((hc.shape[0], D_RNN), jnp.float32)
    hcf = linear_scan(*rglru_coeffs(xc, wa[0], ba[0], wi[0], bi[0], lam[0]), zeros)
    hlf = linear_scan(*rglru_coeffs(xl, wa[0], ba[0], wi[0], bi[0], lam[0]), hcf[:, -1])
    hcb = linear_scan(*rglru_coeffs(jnp.flip(xc, 1), wa[1], ba[1], wi[1], bi[1], lam[1]), zeros)
    hlb = jnp.flip(linear_scan(*rglru_coeffs(jnp.flip(xl, 1), wa[1], ba[1], wi[1], bi[1], lam[1]), hcb[:, -1]), 1)
    y = ((hlf + hlb).astype(h.dtype) * jax.nn.gelu(gl)) @ w_out
    yc = None
    if need_ctx:
        yc = ((hcf + jnp.flip(hcb, 1)).astype(h.dtype) * jax.nn.gelu(gc)) @ w_out
    return y, yc


def na_mixer(h, hc, w_qkv, q_g, k_g, rpb, w_o, need_ctx):
    bsz, seq, _ = h.shape
    rows = seq // GRID_W
    kr = min(NA_ROWS_MAX, rows)
    n_loc = kr * NA_COLS
    scale = NA_HEAD_DIM ** -0.5

    def proj(z):
        q, k, v = jnp.split(z @ w_qkv, 3, axis=-1)
        shp = z.shape[:2] + (NA_HEADS, NA_HEAD_DIM)
        return rmsnorm(q.reshape(shp), q_g), rmsnorm(k.reshape(shp), k_g), v.reshape(shp)

    q, k, v = proj(h)
    qc, kc, vc = proj(hc)
    grid = (bsz, rows, GRID_W, NA_HEADS, NA_HEAD_DIM)
    qg, kg, vg = q.reshape(grid), k.reshape(grid), v.reshape(grid)

    cols = np.arange(GRID_W)
    col_start = np.clip(cols - NA_COLS // 2, 0, GRID_W - NA_COLS)
    col_idx = col_start[:, None] + np.arange(NA_COLS)[None, :]
    col_bias_idx = col_idx - cols[:, None] + (NA_COLS - 1)

    def row_block(r):
        rs = jnp.clip(r - kr // 2, 0, rows - kr)
        qr = lax.dynamic_index_in_dim(qg, r, axis=1, keepdims=False)
        def gather(t):
            tw = lax.dynamic_slice_in_dim(t, rs, kr, axis=1)[:, :, col_idx]
            return tw.transpose(0, 2, 1, 3, 4, 5).reshape(bsz, GRID_W, n_loc, NA_HEADS, NA_HEAD_DIM)
        kw, vw = gather(kg), gather(vg)
        row_bias_idx = rs + jnp.arange(kr) - r + (NA_ROWS_MAX - 1)
        bias = rpb[:, row_bias_idx[:, None, None], col_bias_idx[None, :, :]]
        bias = bias.transpose(0, 2, 1, 3).reshape(NA_HEADS, GRID_W, n_loc).astype(jnp.float32)
        s_loc = jnp.einsum("bqhd,bqkhd->bhqk", qr, kw).astype(jnp.float32) * scale + bias
        s_ctx = jnp.einsum("bqhd,bkhd->bhqk", qr, kc).astype(jnp.float32) * scale
        p = jax.nn.softmax(jnp.concatenate([s_loc, s_ctx], axis=-1), axis=-1).astype(v.dtype)
        return (jnp.einsum("bhqk,bqkhd->bqhd", p[..., :n_loc], vw)
                + jnp.einsum("bhqk,bkhd->bqhd", p[..., n_loc:], vc))

    o = lax.map(row_block, jnp.arange(rows))
    y = o.transpose(1, 0, 2, 3, 4).reshape(bsz, seq, D_MODEL) @ w_o
    yc = None
    if need_ctx:
        s = jnp.einsum("bqhd,bkhd->bhqk", qc, kc).astype(jnp.float32) * scale
        p = jax.nn.softmax(s, axis=-1).astype(vc.dtype)
        oc = jnp.einsum("bhqk,bkhd->bqhd", p, vc)
        yc = oc.reshape(hc.shape[0], hc.shape[1], D_MODEL) @ w_o
    return y, yc


def fourier_mix(z, w_f):
    bsz, length, _ = z.shape
    zg = z.astype(jnp.float32).reshape(bsz, length, FT_GROUPS, FT_GROUP_W)
    f = jnp.fft.fft2(zg, axes=(1, 3), norm="ortho").real
    return f.reshape(bsz, length, D_MODEL).astype(z.dtype) @ w_f


def fourier_mixer(h, hc, w_f, need_ctx):
    return fourier_mix(h, w_f), (fourier_mix(hc, w_f) if need_ctx else None)


def swiglu(z, w_gu, w_down):
    g, u = jnp.split(z @ w_gu, 2, axis=-1)
    return (jax.nn.silu(g) * u) @ w_down


def moe_swiglu(z, router, w_gu, w_down):
    logits = (z @ router).astype(jnp.float32)
    top_v, top_i = lax.top_k(logits, TOP_K)
    wts = jax.nn.softmax(top_v, axis=-1)
    gates = jnp.sum(jax.nn.one_hot(top_i, N_EXPERTS, dtype=jnp.float32) * wts[..., None], axis=-2)
    out = jnp.zeros_like(z)
    for e in range(N_EXPERTS):
        out = out + gates[..., e:e + 1].astype(z.dtype) * swiglu(z, w_gu[e], w_down[e])
    return out


def setup_inputs(seed: int = 0) -> dict:
    key = jax.random.key(seed)
    ks = jax.random.split(key, 32)
    (n_a, n_b, n_c), n_dense, n_moe = _layer_counts(DEPTH)
    d = D_MODEL

    def nrm(k, shape, s):
        return jax.random.normal(k, shape, jnp.float32) * s

    u = jax.random.uniform(ks[11], (n_a, 2, D_RNN), jnp.float32, 0.9, 0.999)
    a0 = u ** (1.0 / LRU_C)
    rg_lambda = jnp.log(a0) - jnp.log1p(-a0)
    return {
        "x": nrm(ks[0], (BATCH, SEQ, d), 1.0),
        "c": nrm(ks[1], (BATCH, d), 1.0),
        "ctx": nrm(ks[2], (BATCH, CTX_LEN, d), 1.0),
        "c_ctx": nrm(ks[3], (d,), 1.0),
        "ada_w": nrm(ks[4], (DEPTH, d, ADA_CHUNKS * d), 0.5 * d ** -0.5),
        "ada_b": nrm(ks[5], (DEPTH, ADA_CHUNKS * d), 0.02),
        "norm_g": 1.0 + nrm(ks[6], (DEPTH, 2, d), 0.1),
        "rg_w_in": nrm(ks[7], (n_a, d, 2 * D_RNN), d ** -0.5),
        "rg_conv_w": nrm(ks[8], (n_a, CONV_WIDTH, D_RNN), CONV_WIDTH ** -0.5),
        "rg_conv_b": nrm(ks[9], (n_a, D_RNN), 0.02),
        "rg_wa": nrm(ks[10], (n_a, 2, RG_BLOCKS, RG_BLOCK_W, RG_BLOCK_W), RG_BLOCK_W ** -0.5),
        "rg_ba": nrm(ks[12], (n_a, 2, D_RNN), 0.02),
        "rg_wi": nrm(ks[13], (n_a, 2, RG_BLOCKS, RG_BLOCK_W, RG_BLOCK_W), RG_BLOCK_W ** -0.5),
        "rg_bi": nrm(ks[14], (n_a, 2, D_RNN), 0.02),
        "rg_lambda": rg_lambda,
        "rg_w_out": nrm(ks[15], (n_a, D_RNN, d), D_RNN ** -0.5),
        "na_w_qkv": nrm(ks[16], (n_b, d, 3 * d), d ** -0.5),
        "na_q_g": 1.0 + nrm(ks[17], (n_b, NA_HEAD_DIM), 0.1),
        "na_k_g": 1.0 + nrm(ks[18], (n_b, NA_HEAD_DIM), 0.1),
        "na_rpb": nrm(ks[19], (n_b, NA_HEADS, 2 * NA_ROWS_MAX - 1, 2 * NA_COLS - 1), 0.1),
        "na_w_o": nrm(ks[20], (n_b, d, d), d ** -0.5),
        "ft_w_out": nrm(ks[21], (n_c, d, d), d ** -0.5),
        "ffn_w_gu": nrm(ks[22], (n_dense, d, 2 * D_FF), d ** -0.5),
        "ffn_w_down": nrm(ks[23], (n_dense, D_FF, d), D_FF ** -0.5),
        "moe_router": nrm(ks[24], (n_moe, d, N_EXPERTS), d ** -0.5),
        "moe_w_gu": nrm(ks[25], (n_moe, N_EXPERTS, d, 2 * D_FF), d ** -0.5),
        "moe_w_down": nrm(ks[26], (n_moe, N_EXPERTS, D_FF, d), D_FF ** -0.5),
    }


def reference(x, c, ctx, c_ctx, ada_w, ada_b, norm_g, rg_w_in, rg_conv_w, rg_conv_b, rg_wa, rg_ba, rg_wi, rg_bi, rg_lambda, rg_w_out, na_w_qkv, na_q_g, na_k_g, na_rpb, na_w_o, ft_w_out, ffn_w_gu, ffn_w_down, moe_router, moe_w_gu, moe_w_down):
    xc = ctx
    silu_c = jax.nn.silu(c)
    silu_cc = jax.nn.silu(c_ctx)
    mix_idx = [0] * N_MIXERS
    dense_idx = 0
    moe_idx = 0
    for layer in range(DEPTH):
        last = layer == DEPTH - 1
        need_ctx = not last
        m = [t[:, None, :] for t in jnp.split(silu_c @ ada_w[layer] + ada_b[layer], ADA_CHUNKS, axis=-1)]
        mc = jnp.split(silu_cc @ ada_w[layer] + ada_b[layer], ADA_CHUNKS, axis=-1)
        h = rmsnorm(x, norm_g[layer, 0]) * (1.0 + m[1]) + m[0]
        hc = rmsnorm(xc, norm_g[layer, 0]) * (1.0 + mc[1]) + mc[0]
        kind = layer % N_MIXERS
        j = mix_idx[kind]
        mix_idx[kind] += 1
        if kind == 0:
            y, yc = rglru_mixer(h, hc, rg_w_in[j], rg_conv_w[j], rg_conv_b[j], rg_wa[j], rg_ba[j], rg_wi[j], rg_bi[j], rg_lambda[j], rg_w_out[j], need_ctx)
        elif kind == 1:
            y, yc = na_mixer(h, hc, na_w_qkv[j], na_q_g[j], na_k_g[j], na_rpb[j], na_w_o[j], need_ctx)
        else:
            y, yc = fourier_mixer(h, hc, ft_w_out[j], need_ctx)
        x = x + m[2] * y
        if need_ctx:
            xc = xc + mc[2] * yc
        h = rmsnorm(x, norm_g[layer, 1]) * (1.0 + m[4]) + m[3]
        if layer % 2 == 0:
            f = swiglu(h, ffn_w_gu[dense_idx], ffn_w_down[dense_idx])
            if need_ctx:
                hc = rmsnorm(xc, norm_g[layer, 1]) * (1.0 + mc[4]) + mc[3]
                xc = xc + mc[5] * swiglu(hc, ffn_w_gu[dense_idx], ffn_w_down[dense_idx])
            dense_idx += 1
        else:
            f = moe_swiglu(h, moe_router[moe_idx], moe_w_gu[moe_idx], moe_w_down[moe_idx])
            if need_ctx:
                hc = rmsnorm(xc, norm_g[layer, 1]) * (1.0 + mc[4]) + mc[3]
                xc = xc + mc[5] * moe_swiglu(hc, moe_router[moe_idx], moe_w_gu[moe_idx], moe_w_down[moe_idx])
            moe_idx += 1
        x = x + m[5] * f
    return x
```

```python
import contextlib
import numpy as np
import concourse.bass as bass
import concourse.mybir as mybir
from concourse.bass_utils import run_bass_kernel_spmd

F32 = mybir.dt.float32
BF16 = mybir.dt.bfloat16
AF = mybir.ActivationFunctionType
ALU = mybir.AluOpType
AX = mybir.AxisListType


class Buf:
    __slots__ = ("w", "r", "name")

    def __init__(self, name=""):
        self.w = None
        self.r = []
        self.name = name


class Prog:
    NDSEM = 24

    def __init__(self):
        self.nc = bass.Bass("TRN2", target_bir_lowering=False)
        self.es = contextlib.ExitStack()
        self.recs = {k: [] for k in ("pe", "act", "dve", "pool", "sp")}
        self.cnt = {k: 0 for k in self.recs}
        self.waited = {k: {} for k in self.recs}
        self.sems = {}
        for i in range(self.NDSEM):
            self._sem("d%d" % i)
        self.ndma = 0
        self.dma_hist = []
        self.nbuf = 0

    EPOCH = 30000

    def _sem(self, key):
        if key not in self.sems:
            self.sems[key] = self.es.enter_context(self.nc.semaphore("s_" + key.replace("#", "_")))
        return self.sems[key]

    def sbuf(self, shape, dtype, name=None):
        self.nbuf += 1
        t = self.es.enter_context(self.nc.sbuf_tensor((name + "_sb") if name else ("sb%d" % self.nbuf), list(shape), dtype))
        return t

    def psum(self, shape, dtype=F32, name=None):
        self.nbuf += 1
        t = self.es.enter_context(self.nc.psum_tensor((name + "_ps") if name else ("ps%d" % self.nbuf), list(shape), dtype))
        return t

    def dram_in(self, name, shape, dtype=F32):
        return self.nc.dram_tensor(name, list(shape), dtype, kind="ExternalInput").ap()

    def dram_out(self, name, shape, dtype=F32):
        return self.nc.dram_tensor(name, list(shape), dtype, kind="ExternalOutput").ap()

    def _need(self, eng, ev, waits):
        if ev is None:
            return
        key, val = ev
        if self.waited[eng].get(key, 0) >= val:
            return
        if eng == "pe" and key.startswith("pe#"):
            return
        self.waited[eng][key] = val
        waits[key] = max(waits.get(key, 0), val)

    def _deps(self, eng, reads, writes):
        waits = {}
        for b in reads:
            self._need(eng, b.w, waits)
        for b in writes:
            self._need(eng, b.w, waits)
            for ev in b.r:
                self._need(eng, ev, waits)
        return waits

    def _commit(self, ev, reads, writes):
        for b in reads:
            b.r.append(ev)
        for b in writes:
            b.w = ev
            b.r = []

    def op(self, eng, fn, reads=(), writes=()):
        waits = self._deps(eng, reads, writes)
        c = self.cnt[eng]
        self.cnt[eng] += 1
        key = "%s#%d" % (eng, c // self.EPOCH)
        self._sem(key)
        ev = (key, c % self.EPOCH + 1)
        self.recs[eng].append((list(waits.items()), fn, (key, 1)))
        self._commit(ev, reads, writes)
        return ev

    def I(self, eng, name, *args, reads=(), writes=(), **kw):
        return self.op(eng, lambda e: getattr(e, name)(*args, **kw), reads, writes)

    def dma(self, q, out, in_, reads=(), writes=(), **kw):
        waits = self._deps(q, reads, writes)
        n = self.ndma
        self.ndma += 1
        key = "d%d" % (n % self.NDSEM)
        val = 16 * (n // self.NDSEM + 1)
        if n >= self.NDSEM:
            pk, pv = self.dma_hist[n - self.NDSEM]
            if self.waited[q].get(pk, 0) < pv:
                self.waited[q][pk] = pv
                waits[pk] = max(waits.get(pk, 0), pv)
        ev = (key, val)
        self.dma_hist.append(ev)

        def fn(e, out=out, in_=in_, kw=kw):
            return e.dma_start(out=out, in_=in_, **kw)
        self.recs[q].append((list(waits.items()), fn, (key, 16)))
        self._commit(ev, reads, writes)
        return ev

    def coll(self, kind, ins, outs, reads=(), writes=(), groups=None):
        q = "pool"
        waits = self._deps(q, reads, writes)
        n = self.ndma
        self.ndma += 1
        key = "d%d" % (n % self.NDSEM)
        val = 16 * (n // self.NDSEM + 1)
        if n >= self.NDSEM:
            pk, pv = self.dma_hist[n - self.NDSEM]
            if self.waited[q].get(pk, 0) < pv:
                self.waited[q][pk] = pv
                waits[pk] = max(waits.get(pk, 0), pv)
        ev = (key, val)
        self.dma_hist.append(ev)
        groups = groups or [list(range(8))]

        def fn(e):
            return e.collective_compute(kind, ALU.bypass, replica_groups=groups, ins=list(ins), outs=list(outs))
        self.recs[q].append((list(waits.items()), fn, (key, 16)))
        self._commit(ev, reads, writes)
        return ev

    def wait_all(self, eng, bufs):
        waits = {}
        for b in bufs:
            self._need(eng, b.w, waits)
        if waits:
            self.recs[eng].append((list(waits.items()), None, None))

    def finalize(self):
        nc = self.nc
        sems = self.sems
        recs = self.recs

        def replay(e, lst):
            for waits, fn, inc in lst:
                for key, val in waits:
                    e.wait_ge(sems[key], val)
                if fn is not None:
                    ins = fn(e)
                    ins.then_inc(sems[inc[0]], inc[1])

        with nc.Block() as block:
            @block.sync
            def _(e):
                replay(e, recs["sp"])

            @block.tensor
            def _(e):
                replay(e, recs["pe"])

            @block.scalar
            def _(e):
                replay(e, recs["act"])

            @block.vector
            def _(e):
                replay(e, recs["dve"])

            @block.gpsimd
            def _(e):
                replay(e, recs["pool"])
        self.es.close()
        return nc
NT = 2304
TT = [(0, 512, 0), (512, 512, 0), (1024, 512, 0), (1536, 512, 0), (2048, 256, 1)]
RMS_EPS = 1e-6


def fm(ap):
    return ap.rearrange("(c p) t -> p c t", p=128)


class Common:
    def __init__(self, p, need_ident=False):
        self.p = p
        self.mod_d = p.dram_in("mod", [128, 48, 2])
        self.mod = p.sbuf([128, 48, 2], F32, "mod_sb")
        self.Bmod = Buf()
        p.dma("sp", self.mod[:], self.mod_d, writes=[self.Bmod])
        self.ones = p.sbuf([128, 128], BF16, "ones_bf")
        self.Bones = Buf()
        p.op("dve", lambda e: e.memset(self.ones[:], 1.0), writes=[self.Bones])

    def scale_vec(self, g_d, j, name):
        p = self.p
        g = p.sbuf([128, 8], F32, name + "_g")
        Bg = Buf()
        p.dma("sp", g[:], g_d, writes=[Bg])
        s = p.sbuf([128, 8, 2], F32, name + "_s")
        Bs = Buf()
        p.op("dve", lambda e: e.tensor_scalar(s[:], self.mod[:, j * 8:(j + 1) * 8, :], 1.0, None, ALU.add),
             reads=[self.Bmod], writes=[Bs])
        for st in range(2):
            p.op("dve", lambda e, st=st: e.tensor_tensor(s[:, :, st], s[:, :, st], g[:], ALU.mult),
                 reads=[Bg, Bs], writes=[Bs])
        return s, Bs


class NormMod:
    def __init__(self, p, cm, s, Bs, jshift, ps_bank, Bps, width=512):
        self.p, self.cm, self.s, self.Bs, self.jshift = p, cm, s, Bs, jshift
        self.ps, self.Bps = ps_bank, Bps
        self.sq = [p.sbuf([128, width], BF16, "nm_sq%d" % i) for i in range(2)]
        self.Bsq = [Buf() for _ in range(2)]
        self.rstd = [p.sbuf([128, width], F32, "nm_rstd%d" % i) for i in range(2)]
        self.Brstd = [Buf() for _ in range(2)]
        self.tmp = [p.sbuf([128, width], F32, "nm_tmp%d" % i) for i in range(2)]
        self.Btmp = [Buf() for _ in range(2)]
        self.k = 0

    def run(self, xs, Bxs, n, stream, outs, Bouts):
        p, cm = self.p, self.cm
        ps = self.ps
        for c in range(8):
            i = self.k % 2
            self.k += 1
            p.op("act", lambda e, i=i, c=c: e.activation(self.sq[i][:, :n], xs[c], AF.Square),
                 reads=[Bxs[c]], writes=[self.Bsq[i]])
            p.op("pe", lambda e, i=i, c=c: e.matmul(ps[:, :n], cm.ones[:], self.sq[i][:, :n], start=(c == 0), stop=(c == 7)),
                 reads=[cm.Bones, self.Bsq[i]], writes=[self.Bps])
        r = self.k % 2
        p.op("act", lambda e: e.activation(self.rstd[r][:, :n], ps[:, :n], AF.Sqrt, bias=RMS_EPS, scale=1.0 / 1024.0),
             reads=[self.Bps], writes=[self.Brstd[r]])
        p.op("dve", lambda e: e.reciprocal(self.rstd[r][:, :n], self.rstd[r][:, :n]),
             reads=[self.Brstd[r]], writes=[self.Brstd[r]])
        js = self.jshift
        for c in range(8):
            i = self.k % 2
            self.k += 1
            p.op("dve", lambda e, i=i, c=c: e.tensor_tensor(self.tmp[i][:, :n], xs[c], self.rstd[r][:, :n], ALU.mult),
                 reads=[Bxs[c], self.Brstd[r]], writes=[self.Btmp[i]])
            for oi, (o, Bo) in enumerate(zip(outs, Bouts)):
                p.op("act", lambda e, i=i, c=c, o=o: e.activation(
                    o[c], self.tmp[i][:, :n], AF.Identity,
                    bias=cm.mod[:, js * 8 + c, stream:stream + 1], scale=self.s[:, c, stream:stream + 1]),
                    reads=[self.Btmp[i], cm.Bmod, self.Bs], writes=[Bo[c]])
def build_ada():
    p = Prog()
    w_d = p.dram_in("w", [4, 1024, 768])
    b_d = p.dram_in("b", [4, 128, 6])
    c_d = p.dram_in("cv", [128, 8, 2])
    o_d = p.dram_out("o", [4, 128, 6, 2])
    cv = p.sbuf([128, 8, 2], F32); Bcv = Buf()
    p.dma("sp", cv[:], c_d, writes=[Bcv])
    p.op("act", lambda e: e.activation(cv[:], cv[:], AF.Silu), reads=[Bcv], writes=[Bcv])
    bb = p.sbuf([128, 4, 6], F32); Bbb = Buf()
    p.dma("sp", bb[:], b_d.rearrange("l p i -> p l i"), writes=[Bbb])
    ob = p.sbuf([128, 4, 6, 2], F32); Bob = Buf()
    ws = [p.sbuf([128, 8, 768], F32, "adaw%d" % l) for l in range(4)]
    Bws = [Buf() for _ in range(4)]
    ps = [p.psum([128, 512], F32, "adaps%d" % i) for i in range(2)]
    Bps = [Buf() for _ in range(2)]
    for l in range(4):
        p.dma("sp" if l % 2 == 0 else "act", ws[l][:], w_d[l].rearrange("(k p) m -> p k m", p=128), writes=[Bws[l]])
    n = 0
    for l in range(4):
        for i in range(6):
            b = n % 2
            n += 1
            for k in range(8):
                p.op("pe", lambda e, l=l, i=i, k=k, b=b: e.matmul(
                    ps[b][:, 0:2], ws[l][:, k, i * 128:(i + 1) * 128], cv[:, k, :], start=(k == 0), stop=(k == 7)),
                    reads=[Bws[l], Bcv], writes=[Bps[b]])
            p.op("act", lambda e, l=l, i=i, b=b: e.activation(ob[:, l, i, :], ps[b][:, 0:2], AF.Identity,
                                                           bias=bb[:, l, i:i + 1], scale=1.0),
                 reads=[Bps[b], Bbb], writes=[Bob])
    Bo = Buf()
    p.dma("sp", o_d.rearrange("l p i s -> p l i s"), ob[:], reads=[Bob], writes=[Bo])
    p.wait_all("sp", [Bo])
    return p.finalize()


DEV_NG = 99


def build_ffn(E):
    p = Prog()
    xT = p.dram_in("xT", [1024, NT])
    g_d = p.dram_in("g", [128, 8])
    wgu = p.dram_in("wgu", [E, 1024, 7168])
    wdn = p.dram_in("wdn", [E, 3584, 1024])
    xo = p.dram_out("xo", [1024, NT])
    cm = Common(p)
    s4, Bs4 = cm.scale_vec(g_d, 4, "s4")
    moe = E > 1
    if moe:
        rt_d = p.dram_in("router", [1024, 8])
        id_d = p.dram_in("ident", [128, 128])
        sel_d = p.dram_in("sel", [8, 8, 128])
        rt = p.sbuf([128, 8, 8], F32, "rt"); Brt = Buf()
        p.dma("sp", rt[:], rt_d.rearrange("(k p) e -> p k e", p=128), writes=[Brt])
        ident = p.sbuf([128, 128], F32, "ident_sb"); Bid = Buf()
        p.dma("sp", ident[:], id_d, writes=[Bid])
        sel = p.sbuf([8, 8, 128], F32, "sel_sb"); Bsel = Buf()
        p.dma("sp", sel[:], sel_d, writes=[Bsel])
        GT = p.sbuf([8, NT], F32, "GT"); BGT = [Buf() for _ in TT]
        h32 = p.sbuf([128, 8, 512], F32, "h32"); Bh32 = [Buf() for _ in range(8)]
        gbc = [p.sbuf([128, NT], F32, "gbc%d" % i) for i in range(1)]
        Bgbc = [[Buf() for _ in TT] for _ in range(1)]
        swt = [p.sbuf([128, 512], F32, "swt%d" % i) for i in range(2)]
        Bswt = [Buf() for _ in range(2)]
        sm = {k: p.sbuf([128, 8], F32, "sm_" + k) for k in ("l", "eq1", "l2", "eq2", "G")}
        sv = {k: p.sbuf([128, 1], F32, "sv_" + k) for k in ("m1", "m2", "d", "w1", "w2")}
        Bsm = Buf()
    x = p.sbuf([128, 8, NT], F32, "x")
    Bx = [[Buf() for _ in TT] for _ in range(8)]
    hb = p.sbuf([128, 8, NT], BF16, "hb")
    Bhb = [[Buf() for _ in TT] for _ in range(8)]
    ps = [p.psum([128, 512], F32, "bank%d" % i) for i in range(8)]
    Bps = [Buf() for _ in range(8)]
    xv = fm(xT)
    for ti, (t0, n, st) in enumerate(TT):
        for c in range(8):
            p.dma("sp", x[:, c, t0:t0 + n], xv[:, c, t0:t0 + n], writes=[Bx[c][ti]])
    nm = NormMod(p, cm, s4, Bs4, 3, ps[7], Bps[7])
    for ti, (t0, n, st) in enumerate(TT):
        xs = [x[:, c, t0:t0 + n] for c in range(8)]
        Bxs = [Bx[c][ti] for c in range(8)]
        outs = [[hb[:, c, t0:t0 + n] for c in range(8)]]
        Bouts = [[Bhb[c][ti] for c in range(8)]]
        if moe:
            outs.append([h32[:, c, :n] for c in range(8)])
            Bouts.append(Bh32)
        nm.run(xs, Bxs, n, st, outs, Bouts)
        if moe:
            for sub in range(n // 128):
                lg = ps[6][:, 0:8]
                for k in range(8):
                    p.op("pe", lambda e, k=k, sub=sub: e.matmul(lg, h32[:, k, sub * 128:(sub + 1) * 128], rt[:, k, :],
                                                               start=(k == 0), stop=(k == 7)),
                         reads=[Bh32[k], Brt], writes=[Bps[6]])
                D = lambda fn, rd=(), wr=(): p.op("dve", fn, reads=[Bsm] + list(rd), writes=[Bsm] + list(wr))
                D(lambda e: e.tensor_copy(sm["l"][:], lg), rd=[Bps[6]])
                D(lambda e: e.reduce_max(sv["m1"][:], sm["l"][:], AX.X))
                D(lambda e: e.tensor_scalar(sm["eq1"][:], sm["l"][:], sv["m1"][:, 0:1], None, ALU.is_equal))
                D(lambda e: e.scalar_tensor_tensor(sm["l2"][:], sm["eq1"][:], -1e30, sm["l"][:], ALU.mult, ALU.add))
                D(lambda e: e.reduce_max(sv["m2"][:], sm["l2"][:], AX.X))
                D(lambda e: e.tensor_scalar(sm["eq2"][:], sm["l2"][:], sv["m2"][:, 0:1], None, ALU.is_equal))
                D(lambda e: e.tensor_tensor(sv["d"][:], sv["m2"][:], sv["m1"][:], ALU.subtract))
                p.op("act", lambda e: e.activation(sv["d"][:], sv["d"][:], AF.Exp), reads=[Bsm], writes=[Bsm])
                D(lambda e: e.tensor_scalar(sv["w1"][:], sv["d"][:], 1.0, None, ALU.add))
                D(lambda e: e.reciprocal(sv["w1"][:], sv["w1"][:]))
                D(lambda e: e.tensor_tensor(sv["w2"][:], sv["d"][:], sv["w1"][:], ALU.mult))
                D(lambda e: e.tensor_scalar(sm["G"][:], sm["eq1"][:], sv["w1"][:, 0:1], None, ALU.mult))
                D(lambda e: e.scalar_tensor_tensor(sm["G"][:], sm["eq2"][:], sv["w2"][:, 0:1], sm["G"][:], ALU.mult, ALU.add))
                tp = ps[6][0:8, 128:256]
                p.op("pe", lambda e: e.transpose(tp, sm["G"][:], ident[:]), reads=[Bsm, Bid], writes=[Bps[6]])
                p.op("act", lambda e, sub=sub, t0=t0: e.activation(GT[:, t0 + sub * 128:t0 + (sub + 1) * 128], tp, AF.Identity),
                     reads=[Bps[6]], writes=[BGT[ti]])
    GH = 2 if moe else 4
    GW = GH * 128
    NG = min(28 // GH, DEV_NG)
    wg_sb = [p.sbuf([128, 8, 2, GW], BF16, "wgu%d" % i) for i in range(2)]
    wd_sb = [p.sbuf([128, GH, 1024], BF16, "wdn%d" % i) for i in range(2)]
    Bwg = [[Buf() for _ in range(8)] for _ in range(2)]
    Bwd = [[Buf() for _ in range(GH)] for _ in range(2)]
    act = [p.sbuf([128, GH, 512], BF16, "act%d" % i) for i in range(2)]
    Bact = [[Buf() for _ in range(GH)] for _ in range(2)]
    sg = [p.sbuf([128, 512], F32, "sg%d" % i) for i in range(2)]
    Bsg = [Buf() for _ in range(2)]
    NSTG = 3 if moe else 6
    stg = [p.sbuf([128, 1024], F32, "stg%d" % i) for i in range(NSTG)]
    Bstg = [Buf() for _ in range(NSTG)]
    cnt = {"si": 0, "hi": 0, "yi": 0}

    def load_group(ex, j, wb):
        wv = wgu[ex].rearrange("(k p) (h m) -> p k h m", p=128, h=2)
        dv = wdn[ex].rearrange("(k p) m -> p k m", p=128)
        c0 = j * GW
        for k in range(8):
            sb_ = cnt["si"] % NSTG
            cnt["si"] += 1
            sv_ = stg[sb_][:, 0:2 * GW].rearrange("p (a b) -> p a b", a=2)
            p.dma("sp", sv_, wv[:, k, :, c0:c0 + GW], writes=[Bstg[sb_]])
            p.I("pool", "tensor_copy", wg_sb[wb][:, k, :, :], sv_, reads=[Bstg[sb_]], writes=[Bwg[wb][k]])
        for hc in range(GH):
            sb_ = cnt["si"] % NSTG
            cnt["si"] += 1
            p.dma("sp", stg[sb_][:], dv[:, j * GH + hc, :], writes=[Bstg[sb_]])
            p.I("pool", "tensor_copy", wd_sb[wb][:, hc, :], stg[sb_][:], reads=[Bstg[sb_]], writes=[Bwd[wb][hc]])

    def gate_bc(ex):
        for ti, (t0, n, st) in enumerate(TT):
            p.I("pe", "matmul", ps[7][:, :n], sel[:, ex, :], GT[:, t0:t0 + n], start=True, stop=True,
                reads=[Bsel, BGT[ti]], writes=[Bps[7]])
            p.I("act", "activation", gbc[0][:, t0:t0 + n], ps[7][:, :n], AF.Identity, reads=[Bps[7]], writes=[Bgbc[0][ti]])

    def stage1(item):
        ex, j, wb, ti, ab = item
        t0, n, st = TT[ti]
        for hc in range(GH):
            s_ = cnt["hi"] % 2
            cnt["hi"] += 1
            pg, pu = ps[2 * s_], ps[2 * s_ + 1]
            for half, pp, Bpp in ((0, pg, Bps[2 * s_]), (1, pu, Bps[2 * s_ + 1])):
                for k in range(8):
                    p.I("pe", "matmul", pp[:, :n], wg_sb[wb][:, k, half, hc * 128:(hc + 1) * 128], hb[:, k, t0:t0 + n],
                        start=(k == 0), stop=(k == 7), reads=[Bwg[wb][k], Bhb[k][ti]], writes=[Bpp])
            p.I("act", "activation", sg[s_][:, :n], pg[:, :n], AF.Silu, reads=[Bps[2 * s_]], writes=[Bsg[s_]])
            if moe:
                p.I("dve", "tensor_tensor", swt[s_][:, :n], sg[s_][:, :n], pu[:, :n], ALU.mult,
                    reads=[Bsg[s_], Bps[2 * s_ + 1]], writes=[Bswt[s_]])
                p.I("pool", "tensor_tensor", act[ab][:, hc, :n], swt[s_][:, :n], gbc[0][:, t0:t0 + n], ALU.mult,
                    reads=[Bswt[s_], Bgbc[0][ti]], writes=[Bact[ab][hc]])
            else:
                p.I("dve", "tensor_tensor", act[ab][:, hc, :n], sg[s_][:, :n], pu[:, :n], ALU.mult,
                    reads=[Bsg[s_], Bps[2 * s_ + 1]], writes=[Bact[ab][hc]])

    def stage2(item):
        ex, j, wb, ti, ab = item
        t0, n, st = TT[ti]
        for oc in range(8):
            yb = 4 + (cnt["yi"] % 3)
            cnt["yi"] += 1
            for hc in range(GH):
                p.I("pe", "matmul", ps[yb][:, :n], wd_sb[wb][:, hc, oc * 128:(oc + 1) * 128], act[ab][:, hc, :n],
                    start=(hc == 0), stop=(hc == GH - 1), reads=[Bwd[wb][hc], Bact[ab][hc]], writes=[Bps[yb]])
            xs = x[:, oc, t0:t0 + n]
            p.I("dve", "scalar_tensor_tensor", xs, ps[yb][:, :n], cm.mod[:, 40 + oc, st:st + 1], xs, ALU.mult, ALU.add,
                reads=[Bps[yb], cm.Bmod, Bx[oc][ti]], writes=[Bx[oc][ti]])

    items = []
    gi = 0
    for ex in range(E):
        for j in range(NG):
            for ti in range(len(TT)):
                items.append((ex, j, gi % 2, ti, len(items) % 2))
            gi += 1
    prev = None
    for idx, item in enumerate(items):
        ex, j, wb, ti, ab = item
        if ti == 0:
            if moe and j == 0:
                if prev is not None:
                    stage2(prev)
                    prev = None
                gate_bc(ex)
            load_group(ex, j, wb)
        stage1(item)
        if prev is not None:
            stage2(prev)
        prev = item
    stage2(prev)
    ov = fm(xo)
    Bo = []
    for ti, (t0, n, st) in enumerate(TT):
        for c in range(8):
            b = Buf()
            p.dma("sp", ov[:, c, t0:t0 + n], x[:, c, t0:t0 + n], reads=[Bx[c][ti]], writes=[b])
            Bo.append(b)
    p.wait_all("sp", Bo)
    return p.finalize()
class Caster:
    def __init__(self, p, n=3, width=1024):
        self.p = p
        self.w = width
        self.stg = [p.sbuf([128, width], F32, "cst%d" % i) for i in range(n)]
        self.B = [Buf() for _ in range(n)]
        self.i = 0

    def load(self, dst, src, Bdst, eng="pool"):
        p = self.p
        m = dst.shape[-1]
        k = self.i % len(self.stg)
        self.i += 1
        p.dma("sp", self.stg[k][:, :m], src, writes=[self.B[k]])
        if eng == "pool":
            p.op("pool", lambda e: e.tensor_copy(dst, self.stg[k][:, :m]), reads=[self.B[k]], writes=[Bdst])
        else:
            p.op("act", lambda e: e.activation(dst, self.stg[k][:, :m], AF.Identity), reads=[self.B[k]], writes=[Bdst])


XZW = 2310


def build_rga():
    p = Prog()
    xT = p.dram_in("xT", [1024, NT])
    xh = p.dram_in("xh", [1024, 3])
    hm_d = p.dram_in("hmask", [128, 3])
    g_d = p.dram_in("g", [128, 8])
    win = p.dram_in("w_in", [1024, 2048])
    cw_d = p.dram_in("conv_w", [128, 8, 4])
    cb_d = p.dram_in("conv_b", [128, 8])
    XC = p.dram_out("XC", [1024, NT])
    GO = p.dram_out("G", [1024, NT])
    cm = Common(p)
    s1, Bs1 = cm.scale_vec(g_d, 1, "s1")
    cw = p.sbuf([128, 8, 4], F32, "cw"); cb = p.sbuf([128, 8], F32, "cb"); hm = p.sbuf([128, 3], F32, "hm")
    Bc = Buf()
    p.dma("sp", cw[:], cw_d, writes=[Bc]); p.dma("sp", cb[:], cb_d, writes=[Bc]); p.dma("sp", hm[:], hm_d, writes=[Bc])
    ps = [p.psum([128, 512], F32, "bank%d" % i) for i in range(8)]
    Bps = [Buf() for _ in range(8)]
    NH = NT + 3
    hb = p.sbuf([128, 8, NH], BF16, "hb")
    tiles = TT + [(NT, 3, 0)]
    Bhb = [[Buf() for _ in tiles] for _ in range(8)]
    wsb = p.sbuf([128, 8, 2048], BF16, "w_in_sb")
    Bw = [[Buf() for _ in range(2)] for _ in range(8)]
    cst = Caster(p, 3, 1024)
    wv = win.rearrange("(k p) m -> p k m", p=128)
    xt = [p.sbuf([128, 8, 512], F32, "xt%d" % i) for i in range(2)]
    Bxt = [[Buf() for _ in range(8)] for _ in range(2)]
    nm = NormMod(p, cm, s1, Bs1, 0, ps[7], Bps[7])
    xv = fm(xT)
    xhv = fm(xh)
    for ti, (t0, n, st) in enumerate(tiles):
        b = ti % 2
        for c in range(8):
            src = xv[:, c, t0:t0 + n] if ti < 5 else xhv[:, c, :]
            p.dma("sp", xt[b][:, c, :n], src, writes=[Bxt[b][c]])
        nm.run([xt[b][:, c, :n] for c in range(8)], Bxt[b], n, st,
               [[hb[:, c, t0:t0 + n] for c in range(8)]], [[Bhb[c][ti] for c in range(8)]])
        if ti == 0:
            for k in range(8):
                for hh in range(2):
                    cst.load(wsb[:, k, hh * 1024:(hh + 1) * 1024], wv[:, k, hh * 1024:(hh + 1) * 1024], Bw[k][hh])
    xz = [p.sbuf([128, XZW], F32, "xz%d" % i) for i in range(2)]
    Bxz = [Buf() for _ in range(2)]
    xco = [p.sbuf([128, NT], F32, "xco%d" % i) for i in range(2)]
    Bxco = [Buf() for _ in range(2)]
    gsb = [p.sbuf([128, 512], F32, "gsb%d" % i) for i in range(2)]
    Bgsb = [Buf() for _ in range(2)]
    for i in range(2):
        p.op("dve", lambda e, i=i: e.memset(xz[i][:], 0.0), writes=[Bxz[i]])
    Bout = []
    pi = 0
    gi = 0
    for oc in range(16):
        hh = oc // 8
        zb = oc % 2
        for ti, (t0, n, st) in enumerate(tiles):
            bk = pi % 6
            pi += 1
            for k in range(8):
                p.op("pe", lambda e, k=k, oc=oc, bk=bk, t0=t0, n=n: e.matmul(
                    ps[bk][:, :n], wsb[:, k, oc * 128:(oc + 1) * 128], hb[:, k, t0:t0 + n], start=(k == 0), stop=(k == 7)),
                    reads=[Bw[k][hh], Bhb[k][ti]], writes=[Bps[bk]])
            if hh == 0:
                if ti < 4:
                    p.op("act", lambda e, zb=zb, bk=bk, t0=t0, n=n: e.activation(xz[zb][:, 2 + t0:2 + t0 + n], ps[bk][:, :n], AF.Identity),
                         reads=[Bps[bk]], writes=[Bxz[zb]])
                elif ti == 4:
                    p.op("act", lambda e, zb=zb, bk=bk, n=n: e.activation(xz[zb][:, 2053:2053 + n], ps[bk][:, :n], AF.Identity),
                         reads=[Bps[bk]], writes=[Bxz[zb]])
                else:
                    p.op("dve", lambda e, zb=zb, bk=bk: e.tensor_tensor(xz[zb][:, 0:2], ps[bk][:, 0:2], hm[:, 0:2], ALU.mult),
                         reads=[Bps[bk], Bc], writes=[Bxz[zb]])
                    p.op("dve", lambda e, zb=zb, bk=bk: e.tensor_tensor(xz[zb][:, 2050:2051], ps[bk][:, 2:3], hm[:, 2:3], ALU.mult),
                         reads=[Bps[bk], Bc], writes=[Bxz[zb]])
            elif ti < 5:
                gb = gi % 2
                gi += 1
                p.op("act", lambda e, gb=gb, bk=bk, n=n: e.activation(gsb[gb][:, :n], ps[bk][:, :n], AF.Gelu_apprx_tanh),
                     reads=[Bps[bk]], writes=[Bgsb[gb]])
                b_ = Buf()
                p.dma("sp", fm(GO)[:, oc - 8, t0:t0 + n], gsb[gb][:, :n], reads=[Bgsb[gb]], writes=[b_])
                Bout.append(b_)
        if hh == 0:
            for (o0, ln, z0) in ((0, 2048, 0), (2048, 256, 2051)):
                p.op("dve", lambda e, zb=zb, oc=oc, o0=o0, ln=ln, z0=z0: e.tensor_scalar(
                    xco[zb][:, o0:o0 + ln], xz[zb][:, z0:z0 + ln], cw[:, oc, 0:1], cb[:, oc:oc + 1], ALU.mult, ALU.add),
                    reads=[Bxz[zb], Bc], writes=[Bxco[zb]])
                for j in range(1, 4):
                    p.op("dve", lambda e, zb=zb, oc=oc, o0=o0, ln=ln, z0=z0, j=j: e.scalar_tensor_tensor(
                        xco[zb][:, o0:o0 + ln], xz[zb][:, z0 + j:z0 + j + ln], cw[:, oc, j:j + 1], xco[zb][:, o0:o0 + ln],
                        ALU.mult, ALU.add),
                        reads=[Bxz[zb], Bc, Bxco[zb]], writes=[Bxco[zb]])
            b_ = Buf()
            p.dma("sp", fm(XC)[:, oc, :], xco[zb][:], reads=[Bxco[zb]], writes=[b_])
            Bout.append(b_)
    p.wait_all("sp", Bout)
    return p.finalize()
def build_rgs(full):
    p = Prog()
    XC = p.dram_in("XC", [1024, NT])
    wa_d = p.dram_in("wa", [2, 4, 256, 256])
    wi_d = p.dram_in("wi", [2, 4, 256, 256])
    gb_d = p.dram_in("gbias", [128, 3, 2, 8])
    ps = [p.psum([128, 512], F32, "bank%d" % i) for i in range(8)]
    Bps = [Buf() for _ in range(8)]
    gbs = p.sbuf([128, 3, 2, 8], F32, "gbs"); Bgb = Buf()
    p.dma("sp", gbs[:], gb_d, writes=[Bgb])
    cl = p.sbuf([128, 2, 2, 8], F32, "cl"); Bcl = Buf()
    p.op("act", lambda e: e.activation(cl[:, 0], gbs[:, 2], AF.Exp, scale=-1.0), reads=[Bgb], writes=[Bcl])
    p.op("act", lambda e: e.activation(cl[:, 0], cl[:, 0], AF.Ln, bias=1.0), reads=[Bcl], writes=[Bcl])
    p.op("dve", lambda e: e.tensor_scalar(cl[:, 1], cl[:, 0], -16.0, None, ALU.mult), reads=[Bcl], writes=[Bcl])
    p.op("dve", lambda e: e.tensor_scalar(cl[:, 0], cl[:, 0], -8.0, None, ALU.mult), reads=[Bcl], writes=[Bcl])
    cst = Caster(p, 3, 1152)
    gw = p.sbuf([128, 2, 2, 4, 2, 256], BF16, "gw"); Bgw = Buf()
    for d in range(2):
        for gi_, src in enumerate((wa_d, wi_d)):
            for n_ in range(4):
                cst.load(gw[:, d, gi_, n_].rearrange("p a b -> p (a b)"),
                         src[d, n_].rearrange("(kk p) m -> p kk m", p=128), Bgw)
    xcb = p.sbuf([128, 8, NT], BF16, "xcb"); Bxcb = [Buf() for _ in range(8)]
    xcv = fm(XC)
    for c in range(8):
        for hh in range(2):
            cst.load(xcb[:, c, hh * 1152:(hh + 1) * 1152], xcv[:, c, hh * 1152:(hh + 1) * 1152], Bxcb[c], eng="act" if c % 2 else "pool")
    xcf = [p.sbuf([128, NT], F32, "xcf%d" % i) for i in range(2)]; Bxcf = [Buf() for _ in range(2)]
    T = {k: [p.sbuf([128, 512], F32, "t_%s%d" % (k, i)) for i in range(2)] for k in ("r", "i", "a", "s", "b")}
    BT = {k: [Buf() for _ in range(2)] for k in T}
    zeros = p.sbuf([128, 512], F32, "zeros"); Bz = Buf()
    p.op("dve", lambda e: e.memset(zeros[:], 0.0), writes=[Bz])
    hbuf = [p.sbuf([128, NT], F32, "hf%d" % i) for i in range(2)]
    Bh = [[Buf() for _ in TT] for _ in range(2)]
    junk = p.sbuf([128, 512], F32, "junk"); Bjunk = Buf()
    comp = p.sbuf([128, 8, 4], F32, "comp"); Bcomp = Buf()
    if full:
        ca_d = p.dram_in("comp_all", [128, 8, 8, 4])
        oh_d = p.dram_in("onehot", [128, 2, 9])
        G_d = p.dram_in("G", [1024, NT])
        wo_d = p.dram_in("w_out", [1024, 1024])
        xT = p.dram_in("xT", [1024, NT])
        xo = p.dram_out("xo", [1024, NT])
        cm = Common(p)
        ca = p.sbuf([128, 8, 8, 4], F32, "ca"); oh = p.sbuf([128, 2, 9], F32, "oh"); Bca = Buf()
        p.dma("sp", ca[:], ca_d, writes=[Bca]); p.dma("sp", oh[:], oh_d, writes=[Bca])
        ext = p.sbuf([128, 9], F32, "ext"); ext2 = p.sbuf([128, 9], F32, "ext2"); car = p.sbuf([128, 1], F32, "car")
        Bext = Buf()
        wo = p.sbuf([128, 8, 1024], BF16, "wo"); Bwo = [Buf() for _ in range(8)]
        for k in range(8):
            cst.load(wo[:, k, :], wo_d.rearrange("(k p) m -> p k m", p=128)[:, k, :], Bwo[k])
        yin = p.sbuf([128, 8, NT], BF16, "yin"); Byin = [[Buf() for _ in TT] for _ in range(8)]
        gch = [p.sbuf([128, NT], F32, "gch%d" % i) for i in range(2)]; Bgch = [Buf() for _ in range(2)]
    else:
        comp_o = p.dram_out("comp", [128, 8, 4])
    st_ = {"ki": 0, "pi": 0}

    def coeffs(ti, d, oc):
        blk, xb = oc // 2, oc % 2
        t0, n, st = TT[ti]
        kb = st_["ki"] % 2
        st_["ki"] += 1
        q = st_["pi"] % 3
        st_["pi"] += 1
        pr, pq = ps[q * 2], ps[q * 2 + 1]
        Bpr, Bpq = Bps[q * 2], Bps[q * 2 + 1]
        for gi_, pp, Bpp in ((0, pr, Bpr), (1, pq, Bpq)):
            for kk in range(2):
                p.I("pe", "matmul", pp[:, :n], gw[:, d, gi_, blk, kk, (oc % 2) * 128:(oc % 2 + 1) * 128],
                    xcb[:, 2 * blk + kk, t0:t0 + n], start=(kk == 0), stop=(kk == 1),
                    reads=[Bgw, Bxcb[2 * blk + kk]], writes=[Bpp])
        r_, i_, a_, s_, b_ = (T[k][kb][:, :n] for k in ("r", "i", "a", "s", "b"))
        p.I("act", "activation", r_, pr[:, :n], AF.Sigmoid, bias=gbs[:, 0, d, oc:oc + 1], reads=[Bpr, Bgb], writes=[BT["r"][kb]])
        p.I("act", "activation", i_, pq[:, :n], AF.Sigmoid, bias=gbs[:, 1, d, oc:oc + 1], reads=[Bpq, Bgb], writes=[BT["i"][kb]])
        p.I("act", "activation", a_, r_, AF.Exp, scale=cl[:, 0, d, oc:oc + 1], reads=[BT["r"][kb], Bcl], writes=[BT["a"][kb]])
        p.I("act", "activation", s_, r_, AF.Exp, scale=cl[:, 1, d, oc:oc + 1], reads=[BT["r"][kb], Bcl], writes=[BT["s"][kb]])
        p.I("act", "activation", s_, s_, AF.Sqrt, bias=1.0, scale=-1.0, reads=[BT["s"][kb]], writes=[BT["s"][kb]])
        p.I("dve", "tensor_tensor", b_, i_, xcf[xb][:, t0:t0 + n], ALU.mult, reads=[BT["i"][kb], Bxcf[xb]], writes=[BT["b"][kb]])
        p.I("dve", "tensor_tensor", b_, b_, s_, ALU.mult, reads=[BT["b"][kb], BT["s"][kb]], writes=[BT["b"][kb]])
        return kb

    def scan(ti, kb, init, Binit, d):
        t0, n, st = TT[ti]
        o = hbuf[d][:, t0:t0 + n]
        a_, b_ = T["a"][kb][:, :n], T["b"][kb][:, :n]
        if d == 1:
            o, a_, b_ = o[:, ::-1], a_[:, ::-1], b_[:, ::-1]
        p.I("dve", "tensor_tensor_scan", o, a_, b_, init, ALU.mult, ALU.add,
            reads=[BT["a"][kb], BT["b"][kb]] + Binit, writes=[Bh[d][ti]])

    def endcol(ti, d):
        t0, n, st = TT[ti]
        c = t0 + n - 1 if d == 0 else t0
        return hbuf[d][:, c:c + 1]

    for oc in range(8):
        xb = oc % 2
        p.dma("sp", xcf[xb][:], xcv[:, oc, :], writes=[Bxcf[xb]])
        if full:
            p.dma("sp", gch[xb][:], fm(G_d)[:, oc, :], writes=[Bgch[xb]])
        for d in range(2):
            order = [0, 1, 2, 3] if d == 0 else [3, 2, 1, 0]
            if full:
                kb = coeffs(4, d, oc)
                scan(4, kb, 0.0, [], d)
                e0 = endcol(4, d)
                p.I("dve", "tensor_copy", ext[:, 0:1], e0, reads=[Bh[d][4]], writes=[Bext])
                Aall, Ball = ca[:, oc, :, 2 * d], ca[:, oc, :, 2 * d + 1]
                if d == 1:
                    Aall, Ball = Aall[:, ::-1], Ball[:, ::-1]
                p.I("dve", "tensor_tensor_scan", ext[:, 1:9], Aall, Ball, e0, ALU.mult, ALU.add,
                    reads=[Bca, Bh[d][4], Bext], writes=[Bext])
                p.I("dve", "tensor_tensor", ext2[:], ext[:], oh[:, d, :], ALU.mult, reads=[Bext, Bca], writes=[Bext])
                p.I("dve", "reduce_sum", car[:], ext2[:], AX.X, reads=[Bext], writes=[Bext])
                init, Binit = car[:, 0:1], [Bext]
            else:
                init, Binit = 0.0, []
            prevA, BprevA = 1.0, []
            for ti in order:
                kb = coeffs(ti, d, oc)
                scan(ti, kb, init, Binit, d)
                init, Binit = endcol(ti, d), [Bh[d][ti]]
                if not full:
                    n = TT[ti][1]
                    p.I("dve", "tensor_tensor_scan", junk[:, :n], T["a"][kb][:, :n], zeros[:, :n], prevA, ALU.mult, ALU.add,
                        reads=[BT["a"][kb], Bz, Bjunk] + BprevA, writes=[Bjunk])
                    p.I("dve", "tensor_copy", comp[:, oc, 2 * d:2 * d + 1], junk[:, n - 1:n], reads=[Bjunk], writes=[Bcomp])
                    prevA, BprevA = comp[:, oc, 2 * d:2 * d + 1], [Bcomp]
            if not full:
                p.I("dve", "tensor_copy", comp[:, oc, 2 * d + 1:2 * d + 2], init, reads=Binit, writes=[Bcomp])
        if full:
            for ti, (t0, n, st) in enumerate(TT):
                p.I("dve", "tensor_tensor", hbuf[0][:, t0:t0 + n], hbuf[0][:, t0:t0 + n], hbuf[1][:, t0:t0 + n], ALU.add,
                    reads=[Bh[0][ti], Bh[1][ti]], writes=[Bh[0][ti]])
                p.I("dve", "tensor_tensor", yin[:, oc, t0:t0 + n], hbuf[0][:, t0:t0 + n], gch[xb][:, t0:t0 + n], ALU.mult,
                    reads=[Bh[0][ti], Bgch[xb]], writes=[Byin[oc][ti]])
    Bout = []
    if full:
        xt = [p.sbuf([128, 512], F32, "xt%d" % i) for i in range(3)]; Bxt = [Buf() for _ in range(3)]
        xi = 0
        for ti, (t0, n, st) in enumerate(TT):
            for oc in range(8):
                b = xi % 3
                bk = 6 + xi % 2
                xi += 1
                p.dma("sp", xt[b][:, :n], fm(xT)[:, oc, t0:t0 + n], writes=[Bxt[b]])
                for k in range(8):
                    p.op("pe", lambda e, k=k, oc=oc, bk=bk, t0=t0, n=n: e.matmul(
                        ps[bk][:, :n], wo[:, k, oc * 128:(oc + 1) * 128], yin[:, k, t0:t0 + n], start=(k == 0), stop=(k == 7)),
                        reads=[Bwo[k], Byin[k][ti]], writes=[Bps[bk]])
                p.op("dve", lambda e, b=b, bk=bk, oc=oc, n=n, st=st: e.scalar_tensor_tensor(
                    xt[b][:, :n], ps[bk][:, :n], cm.mod[:, 16 + oc, st:st + 1], xt[b][:, :n], ALU.mult, ALU.add),
                    reads=[Bps[bk], cm.Bmod, Bxt[b]], writes=[Bxt[b]])
                b_ = Buf()
                p.dma("sp", fm(xo)[:, oc, t0:t0 + n], xt[b][:, :n], reads=[Bxt[b]], writes=[b_])
                Bout.append(b_)
    else:
        b_ = Buf()
        p.dma("sp", comp_o, comp[:], reads=[Bcomp], writes=[b_])
        Bout.append(b_)
    p.wait_all("sp", Bout)
    return p.finalize()
def build_naa():
    p = Prog()
    xT = p.dram_in("xT", [1024, NT])
    g_d = p.dram_in("g", [128, 8])
    w_d = p.dram_in("w_qkv", [1024, 3072])
    qkg_d = p.dram_in("qkg", [128, 2])
    QT = p.dram_out("QT", [1024, NT], BF16)
    KT = p.dram_out("KT", [1024, NT], BF16)
    V = p.dram_out("V", [NT, 1024], BF16)
    cm = Common(p)
    s1, Bs1 = cm.scale_vec(g_d, 1, "s1")
    qkg = p.sbuf([128, 2], F32, "qkg"); Bqkg = Buf()
    p.dma("sp", qkg[:], qkg_d, writes=[Bqkg])
    p.I("dve", "tensor_scalar", qkg[:, 0:1], qkg[:, 0:1], 0.125, None, ALU.mult, reads=[Bqkg], writes=[Bqkg])
    bones = p.sbuf([128, 128], BF16, "bones"); Bbo = Buf()
    p.I("dve", "memset", bones[:], 0.0, writes=[Bbo])
    p.I("dve", "memset", bones[0:64, 0:64], 1.0, writes=[Bbo])
    p.I("dve", "memset", bones[64:128, 64:128], 1.0, writes=[Bbo])
    ps = [p.psum([128, 512], F32, "bank%d" % i) for i in range(8)]
    Bps = [Buf() for _ in range(8)]
    hb = p.sbuf([128, 8, NT], BF16, "hb")
    Bhb = [[Buf() for _ in TT] for _ in range(8)]
    wsb = p.sbuf([128, 8, 3072], BF16, "wqkv")
    Bw = [[Buf() for _ in range(3)] for _ in range(8)]
    cst = Caster(p, 3, 1024)
    wv = w_d.rearrange("(k p) m -> p k m", p=128)
    xt = [p.sbuf([128, 8, 512], F32, "xt%d" % i) for i in range(2)]
    Bxt = [[Buf() for _ in range(8)] for _ in range(2)]
    nm = NormMod(p, cm, s1, Bs1, 0, ps[7], Bps[7])
    xv = fm(xT)
    for ti, (t0, n, st) in enumerate(TT):
        b = ti % 2
        for c in range(8):
            p.dma("sp", xt[b][:, c, :n], xv[:, c, t0:t0 + n], writes=[Bxt[b][c]])
        nm.run([xt[b][:, c, :n] for c in range(8)], Bxt[b], n, st,
               [[hb[:, c, t0:t0 + n] for c in range(8)]], [[Bhb[c][ti] for c in range(8)]])
        if ti == 0:
            for k in range(8):
                for hh in range(3):
                    cst.load(wsb[:, k, hh * 1024:(hh + 1) * 1024], wv[:, k, hh * 1024:(hh + 1) * 1024], Bw[k][hh])
    sq = [p.sbuf([128, 512], BF16, "sq%d" % i) for i in range(2)]; Bsq = [Buf() for _ in range(2)]
    rs = [p.sbuf([128, 512], F32, "rs%d" % i) for i in range(2)]; Brs = [Buf() for _ in range(2)]
    ob = [p.sbuf([128, 512], BF16, "ob%d" % i) for i in range(3)]; Bob = [Buf() for _ in range(3)]
    Bout = []
    i2 = 0
    i3 = 0
    for oc in range(16):
        which = oc // 8
        dst = fm(QT if which == 0 else KT)
        for ti, (t0, n, st) in enumerate(TT):
            a = i2 % 2
            i2 += 1
            pq, pm = ps[a * 2], ps[a * 2 + 1]
            Bpq, Bpm = Bps[a * 2], Bps[a * 2 + 1]
            for k in range(8):
                p.I("pe", "matmul", pq[:, :n], wsb[:, k, oc * 128:(oc + 1) * 128], hb[:, k, t0:t0 + n], start=(k == 0), stop=(k == 7),
                    reads=[Bw[k][which], Bhb[k][ti]], writes=[Bpq])
            p.I("act", "activation", sq[a][:, :n], pq[:, :n], AF.Square, reads=[Bpq], writes=[Bsq[a]])
            p.I("pe", "matmul", pm[:, :n], bones[:], sq[a][:, :n], start=True, stop=True, reads=[Bbo, Bsq[a]], writes=[Bpm])
            p.I("act", "activation", rs[a][:, :n], pm[:, :n], AF.Sqrt, bias=RMS_EPS, scale=1.0 / 64.0, reads=[Bpm], writes=[Brs[a]])
            p.I("dve", "reciprocal", rs[a][:, :n], rs[a][:, :n], reads=[Brs[a]], writes=[Brs[a]])
            o = i3 % 3
            i3 += 1
            p.I("dve", "scalar_tensor_tensor", ob[o][:, :n], pq[:, :n], qkg[:, which:which + 1], rs[a][:, :n], ALU.mult, ALU.mult,
                reads=[Bpq, Bqkg, Brs[a]], writes=[Bob[o]])
            b_ = Buf()
            p.dma("sp", dst[:, oc % 8, t0:t0 + n], ob[o][:, :n], reads=[Bob[o]], writes=[b_])
            Bout.append(b_)
    vb = [p.sbuf([128, 512], BF16, "vb%d" % i) for i in range(3)]; Bvb = [Buf() for _ in range(3)]
    i4 = 0
    for sub in range(NT // 128):
        ti = min(sub // 4, 4)
        for half in range(2):
            a = 4 + i4 % 2
            o = i4 % 3
            i4 += 1
            for k in range(8):
                p.I("pe", "matmul", ps[a][:, :], hb[:, k, sub * 128:(sub + 1) * 128], wsb[:, k, 2048 + half * 512:2048 + (half + 1) * 512],
                    start=(k == 0), stop=(k == 7), reads=[Bw[k][2], Bhb[k][ti]], writes=[Bps[a]])
            p.I("act", "activation", vb[o][:], ps[a][:], AF.Identity, reads=[Bps[a]], writes=[Bvb[o]])
            b_ = Buf()
            p.dma("sp", V[sub * 128:(sub + 1) * 128, half * 512:(half + 1) * 512], vb[o][:], reads=[Bvb[o]], writes=[b_])
            Bout.append(b_)
    p.wait_all("sp", Bout)
    return p.finalize()


NSLOT = 20
NKT = NSLOT + 2
KW = NKT * 128


def build_nab():
    p = Prog()
    QT = p.dram_in("QT", [1024, NT], BF16)
    KT = p.dram_in("KText", [1024, KW], BF16)
    Vp = p.dram_in("Vp", [8, 128, NKT, 128], BF16)
    tab = p.dram_in("tab", [16, 128, 25, 128])
    wo_d = p.dram_in("w_o", [1024, 1024])
    xT = p.dram_in("xT", [1024, NT])
    xo = p.dram_out("xo", [1024, NT])
    cm = Common(p)
    ones = cm.ones
    cst = Caster(p, 3, 1024)
    wo = p.sbuf([128, 8, 1024], BF16, "wo"); Bwo = [Buf() for _ in range(8)]
    for k in range(8):
        cst.load(wo[:, k, :], wo_d.rearrange("(k p) m -> p k m", p=128)[:, k, :], Bwo[k])
    oT = p.sbuf([128, 8, NT], BF16, "oT"); BoT = [[Buf() for _ in range(18)] for _ in range(8)]
    qs = [p.sbuf([128, NT], BF16, "qs%d" % i) for i in range(2)]; Bqs = [Buf() for _ in range(2)]
    ks = [p.sbuf([128, KW], BF16, "ks%d" % i) for i in range(2)]; Bks = [Buf() for _ in range(2)]
    vs = [p.sbuf([128, NKT, 128], BF16, "vs%d" % i) for i in range(2)]; Bvs = [Buf() for _ in range(2)]
    tf = [p.sbuf([128, 25 * 128], F32, "tf%d" % i) for i in range(1)]; Btf = [Buf() for _ in range(1)]
    Eh = [p.sbuf([128, 25 * 128], BF16, "Eh%d" % i) for i in range(2)]; BEh = [Buf() for _ in range(2)]
    pe_ = [p.sbuf([128, 640], F32, "pexp%d" % i) for i in range(2)]; Bpe = [Buf() for _ in range(2)]
    pb = [p.sbuf([128, 896], BF16, "pbf%d" % i) for i in range(2)]; Bpb = [Buf() for _ in range(2)]
    rd = [p.sbuf([128, 128], F32, "rd%d" % i) for i in range(2)]; Brd = [Buf() for _ in range(2)]
    S = [p.psum([128, 1024], F32, "S%d" % i) for i in range(2)]; BS = [Buf() for _ in range(2)]
    O = [p.psum([128, 512], F32, "O%d" % i) for i in range(2)]; BO = [Buf() for _ in range(2)]
    Y = [p.psum([128, 512], F32, "Y%d" % i) for i in range(2)]; BY = [Buf() for _ in range(2)]
    it = 0
    for hc in range(8):
        b = hc % 2
        p.dma("sp", qs[b][:], fm(QT)[:, hc, :], writes=[Bqs[b]])
        p.dma("sp", ks[b][:], fm(KT)[:, hc, :], writes=[Bks[b]])
        p.dma("sp", vs[b][:], Vp[hc], writes=[Bvs[b]])
        for hh in range(2):
            h = 2 * hc + hh
            hp = 64 * hh
            eb = h % 2
            p.dma("sp", tf[0][:], tab[h].rearrange("p a b -> p (a b)"), writes=[Btf[0]])
            p.I("act", "activation", Eh[eb][:], tf[0][:], AF.Exp, reads=[Btf[0]], writes=[BEh[eb]])
            for blk in range(18):
                a = it % 2
                it += 1
                q0 = blk * 128
                if blk < 16:
                    st_ = 0 if blk == 0 else (1 if blk == 1 else (3 if blk == 14 else (4 if blk == 15 else 2)))
                    tiles = [blk + d for d in range(5)] + [NSLOT, NSLOT + 1]
                else:
                    tiles = [NSLOT, NSLOT + 1]
                nt = len(tiles)
                for j, tl in enumerate(tiles):
                    p.I("pe", "matmul", S[a][:, j * 128:(j + 1) * 128], ks[b][hp:hp + 64, tl * 128:(tl + 1) * 128],
                        qs[b][hp:hp + 64, q0:q0 + 128], start=True, stop=True, reads=[Bks[b], Bqs[b]], writes=[BS[a]])
                if blk < 16:
                    p.I("act", "activation", pe_[a][:], S[a][:, 0:640], AF.Exp, reads=[BS[a]], writes=[Bpe[a]])
                    p.I("act", "activation", pb[a][:, 640:896], S[a][:, 640:896], AF.Exp, reads=[BS[a]], writes=[Bpb[a]])
                    p.I("dve", "tensor_tensor", pb[a][:, 0:640], pe_[a][:], Eh[eb][:, st_ * 640:(st_ + 1) * 640], ALU.mult,
                        reads=[Bpe[a], BEh[eb]], writes=[Bpb[a]])
                else:
                    p.I("act", "activation", pb[a][:, 0:256], S[a][:, 0:256], AF.Exp, reads=[BS[a]], writes=[Bpb[a]])
                for j, tl in enumerate(tiles):
                    p.I("pe", "matmul", O[a][:, 0:128], vs[b][:, tl, :], pb[a][:, j * 128:(j + 1) * 128], start=(j == 0), stop=(j == nt - 1),
                        reads=[Bvs[b], Bpb[a]], writes=[BO[a]])
                for j, tl in enumerate(tiles):
                    p.I("pe", "matmul", O[a][:, 128:256], ones[:], pb[a][:, j * 128:(j + 1) * 128], start=(j == 0), stop=(j == nt - 1),
                        reads=[cm.Bones, Bpb[a]], writes=[BO[a]])
                p.I("dve", "reciprocal", rd[a][hp:hp + 64, :], O[a][hp:hp + 64, 128:256], reads=[BO[a]], writes=[Brd[a]])
                p.I("dve", "tensor_tensor", oT[hp:hp + 64, hc, q0:q0 + 128], O[a][hp:hp + 64, 0:128], rd[a][hp:hp + 64, :], ALU.mult,
                    reads=[BO[a], Brd[a]], writes=[BoT[hc][blk]])
    xt = [p.sbuf([128, 512], F32, "xt%d" % i) for i in range(3)]; Bxt = [Buf() for _ in range(3)]
    Bout = []
    xi = 0
    for ti, (t0, n, st) in enumerate(TT):
        for oc in range(8):
            b = xi % 3
            a = xi % 2
            xi += 1
            p.dma("sp", xt[b][:, :n], fm(xT)[:, oc, t0:t0 + n], writes=[Bxt[b]])
            for k in range(8):
                p.I("pe", "matmul", Y[a][:, :n], wo[:, k, oc * 128:(oc + 1) * 128], oT[:, k, t0:t0 + n], start=(k == 0), stop=(k == 7),
                    reads=[Bwo[k]] + BoT[k][t0 // 128:(t0 + n) // 128], writes=[BY[a]])
            p.I("dve", "scalar_tensor_tensor", xt[b][:, :n], Y[a][:, :n], cm.mod[:, 16 + oc, st:st + 1], xt[b][:, :n], ALU.mult, ALU.add,
                reads=[BY[a], cm.Bmod, Bxt[b]], writes=[Bxt[b]])
            b_ = Buf()
            p.dma("sp", fm(xo)[:, oc, t0:t0 + n], xt[b][:, :n], reads=[Bxt[b]], writes=[b_])
            Bout.append(b_)
    p.wait_all("sp", Bout)
    return p.finalize()
def build_fta():
    p = Prog()
    xT = p.dram_in("xT", [1024, NT])
    g_d = p.dram_in("g", [128, 8])
    cw_d = p.dram_in("cwsw", [2, 256, 256])
    ZR = p.dram_out("ZR", [1024, NT])
    ZI = p.dram_out("ZI", [1024, NT])
    cm = Common(p)
    s1, Bs1 = cm.scale_vec(g_d, 1, "s1")
    tb = p.sbuf([128, 2, 2, 256], F32, "tb"); Btb = Buf()
    for i in range(2):
        p.dma("sp", tb[:, i], cw_d[i].rearrange("(kk p) m -> p kk m", p=128), writes=[Btb])
    ps = [p.psum([128, 512], F32, "bank%d" % i) for i in range(8)]
    Bps = [Buf() for _ in range(8)]
    xt = [p.sbuf([128, 8, 512], F32, "xt%d" % i) for i in range(2)]
    Bxt = [[Buf() for _ in range(8)] for _ in range(2)]
    h32 = [p.sbuf([128, 8, 512], F32, "h32_%d" % i) for i in range(2)]
    Bh = [[Buf() for _ in range(8)] for _ in range(2)]
    ob = [p.sbuf([128, 512], F32, "ob%d" % i) for i in range(3)]; Bob = [Buf() for _ in range(3)]
    nm = NormMod(p, cm, s1, Bs1, 0, ps[7], Bps[7])
    xv = fm(xT)
    Bout = []
    i2 = 0
    for ti, (t0, n, st) in enumerate(TT):
        b = ti % 2
        for c in range(8):
            p.dma("sp", xt[b][:, c, :n], xv[:, c, t0:t0 + n], writes=[Bxt[b][c]])
        nm.run([xt[b][:, c, :n] for c in range(8)], Bxt[b], n, st, [[h32[b][:, c, :n] for c in range(8)]], [Bh[b]])
        for oc in range(8):
            gI = oc // 2
            for ri, dst in enumerate((ZR, ZI)):
                a = i2 % 6
                o = i2 % 3
                i2 += 1
                for kk in range(2):
                    p.I("pe", "matmul", ps[a][:, :n], tb[:, ri, kk, (oc % 2) * 128:(oc % 2 + 1) * 128], h32[b][:, 2 * gI + kk, :n],
                        start=(kk == 0), stop=(kk == 1), reads=[Btb, Bh[b][2 * gI + kk]], writes=[Bps[a]])
                p.I("act", "activation", ob[o][:, :n], ps[a][:, :n], AF.Identity, reads=[Bps[a]], writes=[Bob[o]])
                b_ = Buf()
                p.dma("sp", fm(dst)[:, oc, t0:t0 + n], ob[o][:, :n], reads=[Bob[o]], writes=[b_])
                Bout.append(b_)
    p.wait_all("sp", Bout)
    return p.finalize()


def build_ftb():
    p = Prog()
    Zr_d = p.dram_in("Zr", [128, 128, 128])
    Zi_d = p.dram_in("Zi", [128, 128, 128])
    Zc_d = p.dram_in("Zc", [2, 256, 128])
    R_d = p.dram_in("R12", [2, 128, 256])
    CS_d = p.dram_in("CS1", [2, 128, 128])
    TW_d = p.dram_in("TW", [2, 128, 128])
    F256_d = p.dram_in("F256", [2, 256, 256])
    F_o = p.dram_out("F", [128, 128, 128])
    Fc_o = p.dram_out("Fc", [256, 128])
    R = p.sbuf([128, 2, 256], F32, "R"); CS = p.sbuf([128, 2, 128], F32, "CS"); TW = p.sbuf([128, 2, 128], F32, "TW")
    Bt = Buf()
    for i in range(2):
        p.dma("sp", R[:, i], R_d[i], writes=[Bt]); p.dma("sp", CS[:, i], CS_d[i], writes=[Bt]); p.dma("sp", TW[:, i], TW_d[i], writes=[Bt])
    F2 = p.sbuf([128, 2, 2, 256], F32, "F2")
    for i in range(2):
        p.dma("sp", F2[:, i], F256_d[i].rearrange("(tc p) k -> p tc k", p=128), writes=[Bt])
    zc = p.sbuf([128, 2, 2, 128], F32, "zc")
    for i in range(2):
        p.dma("sp", zc[:, i], Zc_d[i].rearrange("(tc p) c -> p tc c", p=128), writes=[Bt])
    CG = 32
    zr = [p.sbuf([128, CG, 128], F32, "zr%d" % i) for i in range(2)]; zi = [p.sbuf([128, CG, 128], F32, "zi%d" % i) for i in range(2)]
    Bz = [Buf() for _ in range(2)]
    ar = [p.sbuf([128, CG, 128], F32, "ar%d" % i) for i in range(2)]; ai = [p.sbuf([128, CG, 128], F32, "ai%d" % i) for i in range(2)]
    Ba = [[Buf() for _ in range(CG // 4)] for _ in range(2)]
    tmp = [p.sbuf([128, 128], F32, "tw_t%d" % i) for i in range(4)]; Btmp = [Buf() for _ in range(4)]
    ob = [p.sbuf([128, 512], F32, "ob%d" % i) for i in range(3)]; Bob = [Buf() for _ in range(3)]
    ps = [p.psum([128, 512], F32, "bank%d" % i) for i in range(8)]
    Bps = [Buf() for _ in range(8)]
    Bout = []
    i1 = 0
    i3 = 0
    for gi in range(128 // CG):
        b = gi % 2
        c0 = gi * CG
        p.dma("sp", zr[b][:], Zr_d[:, c0:c0 + CG, :], writes=[Bz[b]])
        p.dma("sp", zi[b][:], Zi_d[:, c0:c0 + CG, :], writes=[Bz[b]])
        for c in range(CG):
            a = i1 % 4
            i1 += 1
            A = ps[a][:, 0:256]
            p.I("pe", "matmul", A, zr[b][:, c, :], R[:, 0, :], start=True, stop=False, reads=[Bz[b], Bt], writes=[Bps[a]])
            p.I("pe", "matmul", A, zi[b][:, c, :], R[:, 1, :], start=False, stop=True, reads=[Bz[b], Bt], writes=[Bps[a]])
            Ar, Ai = ps[a][:, 0:128], ps[a][:, 128:256]
            Tc, Ts = TW[:, 0, :], TW[:, 1, :]
            Bq = Ba[b][c // 4]
            p.I("dve", "tensor_tensor", tmp[0][:], Ar, Tc, ALU.mult, reads=[Bps[a], Bt], writes=[Btmp[0]])
            p.I("dve", "tensor_tensor", tmp[1][:], Ai, Ts, ALU.mult, reads=[Bps[a], Bt], writes=[Btmp[1]])
            p.I("pool", "tensor_tensor", ar[b][:, c, :], tmp[0][:], tmp[1][:], ALU.add, reads=[Btmp[0], Btmp[1]], writes=[Bq])
            p.I("dve", "tensor_tensor", tmp[2][:], Ai, Tc, ALU.mult, reads=[Bps[a], Bt], writes=[Btmp[2]])
            p.I("dve", "tensor_tensor", tmp[3][:], Ar, Ts, ALU.mult, reads=[Bps[a], Bt], writes=[Btmp[3]])
            p.I("pool", "tensor_tensor", ai[b][:, c, :], tmp[2][:], tmp[3][:], ALU.subtract, reads=[Btmp[2], Btmp[3]], writes=[Bq])
        for q in range(CG // 4):
            a = 4 + i3 % 3
            o = i3 % 3
            i3 += 1
            rr = ar[b][:, 4 * q:4 * q + 4, :].rearrange("p c k -> p (c k)")
            ii = ai[b][:, 4 * q:4 * q + 4, :].rearrange("p c k -> p (c k)")
            p.I("pe", "matmul", ps[a][:], CS[:, 0, :], rr, start=True, stop=False, reads=[Bt, Ba[b][q]], writes=[Bps[a]])
            p.I("pe", "matmul", ps[a][:], CS[:, 1, :], ii, start=False, stop=True, reads=[Bt, Ba[b][q]], writes=[Bps[a]])
            p.I("act", "activation", ob[o][:], ps[a][:], AF.Identity, reads=[Bps[a]], writes=[Bob[o]])
            b_ = Buf()
            cc = c0 + 4 * q
            p.dma("sp", F_o[:, cc:cc + 4, :], ob[o][:].rearrange("p (c k) -> p c k", c=4), reads=[Bob[o]], writes=[b_])
            Bout.append(b_)
    for kc in range(2):
        a = 7
        first = True
        for tc in range(2):
            for ri in range(2):
                p.I("pe", "matmul", ps[a][:, 0:128], F2[:, ri, tc, kc * 128:(kc + 1) * 128], zc[:, ri, tc, :],
                    start=first, stop=(tc == 1 and ri == 1), reads=[Bt], writes=[Bps[a]])
                first = False
        o = i3 % 3
        i3 += 1
        p.I("act", "activation", ob[o][:, 0:128], ps[a][:, 0:128], AF.Identity, reads=[Bps[a]], writes=[Bob[o]])
        b_ = Buf()
        p.dma("sp", Fc_o[kc * 128:(kc + 1) * 128, :], ob[o][:, 0:128], reads=[Bob[o]], writes=[b_])
        Bout.append(b_)
    p.wait_all("sp", Bout)
    return p.finalize()


def build_proj():
    p = Prog()
    fT = p.dram_in("fT", [1024, NT])
    w_d = p.dram_in("w", [1024, 1024])
    xT = p.dram_in("xT", [1024, NT])
    xo = p.dram_out("xo", [1024, NT])
    cm = Common(p)
    cst = Caster(p, 3, 1152)
    wo = p.sbuf([128, 8, 1024], BF16, "wo"); Bwo = [Buf() for _ in range(8)]
    for k in range(8):
        cst.load(wo[:, k, :], w_d.rearrange("(k p) m -> p k m", p=128)[:, k, :], Bwo[k])
    fb = p.sbuf([128, 8, NT], BF16, "fb"); Bfb = [Buf() for _ in range(8)]
    for c in range(8):
        for hh in range(2):
            cst.load(fb[:, c, hh * 1152:(hh + 1) * 1152], fm(fT)[:, c, hh * 1152:(hh + 1) * 1152], Bfb[c], eng="act" if c % 2 else "pool")
    Y = [p.psum([128, 512], F32, "Y%d" % i) for i in range(2)]; BY = [Buf() for _ in range(2)]
    xt = [p.sbuf([128, 512], F32, "xt%d" % i) for i in range(3)]; Bxt = [Buf() for _ in range(3)]
    Bout = []
    xi = 0
    for ti, (t0, n, st) in enumerate(TT):
        for oc in range(8):
            b = xi % 3
            a = xi % 2
            xi += 1
            p.dma("sp", xt[b][:, :n], fm(xT)[:, oc, t0:t0 + n], writes=[Bxt[b]])
            for k in range(8):
                p.I("pe", "matmul", Y[a][:, :n], wo[:, k, oc * 128:(oc + 1) * 128], fb[:, k, t0:t0 + n], start=(k == 0), stop=(k == 7),
                    reads=[Bwo[k], Bfb[k]], writes=[BY[a]])
            p.I("dve", "scalar_tensor_tensor", xt[b][:, :n], Y[a][:, :n], cm.mod[:, 16 + oc, st:st + 1], xt[b][:, :n], ALU.mult, ALU.add,
                reads=[BY[a], cm.Bmod, Bxt[b]], writes=[Bxt[b]])
            b_ = Buf()
            p.dma("sp", fm(xo)[:, oc, t0:t0 + n], xt[b][:, :n], reads=[Bxt[b]], writes=[b_])
            Bout.append(b_)
    p.wait_all("sp", Bout)
    return p.finalize()
NCORES = 8
_PROGS = {}
DEBUG = {}


def _prog(name, fn, *a):
    key = (name,) + a
    if key not in _PROGS:
        _PROGS[key] = fn(*a)
    return _PROGS[key]


def _run(nc, maps):
    if DEBUG.get("trace"):
        res = run_bass_kernel_spmd(nc, maps, core_ids=list(range(NCORES)), trace=True)
        print("EXEC_NS", res.exec_time_ns, flush=True)
        return res.results
    res = run_bass_kernel_spmd(nc, maps, core_ids=list(range(NCORES)))
    return res.results


def to_fm(v):
    return np.ascontiguousarray(np.asarray(v, np.float32).reshape(-1, 128).T)


def _split_x(xTs):
    lat = np.concatenate([r[:, :2048].T for r in xTs], 0)
    return np.ascontiguousarray(lat), np.ascontiguousarray(xTs[0][:, 2048:].T)


def _core_xT(lat, ctx):
    return [np.ascontiguousarray(np.concatenate([lat[2048 * k:2048 * (k + 1)], ctx], 0).T) for k in range(NCORES)]


def run_ada(inp):
    cv = np.stack([to_fm(inp["c"][0]), to_fm(inp["c_ctx"])], -1)
    maps = []
    for k in range(NCORES):
        w = np.ascontiguousarray(inp["ada_w"][:, :, 768 * k:768 * (k + 1)])
        b = np.ascontiguousarray(inp["ada_b"][:, 768 * k:768 * (k + 1)].reshape(4, 6, 128).transpose(0, 2, 1))
        maps.append({"w": w, "b": b, "cv": cv})
    res = _run(_prog("ada", build_ada), maps)
    return [np.ascontiguousarray(np.concatenate([r["o"][l] for r in res], 1)) for l in range(4)]


def run_ffn(xTs, mod, g, wgu, wdn, router=None):
    E = wgu.shape[0]
    maps = []
    for k in range(NCORES):
        m = {"xT": xTs[k], "mod": mod, "g": to_fm(g), "wgu": wgu, "wdn": wdn}
        if E > 1:
            sel = np.zeros((8, 8, 128), np.float32)
            for e in range(8):
                sel[e, e, :] = 1
            m.update({"router": router, "ident": np.eye(128, dtype=np.float32), "sel": sel})
        maps.append(m)
    res = _run(_prog("ffn", build_ffn, E), maps)
    return [r["xo"] for r in res]


def run_rg(xTs, mod, g, inp, j):
    lat, ctx = _split_x(xTs)
    cw = np.ascontiguousarray(inp["rg_conv_w"][j].reshape(4, 8, 128).transpose(2, 1, 0))
    maps = []
    for k in range(NCORES):
        xh = np.zeros((3, 1024), np.float32)
        hm = np.zeros((128, 3), np.float32)
        for i, t in enumerate((2048 * k - 2, 2048 * k - 1, 2048 * (k + 1))):
            if 0 <= t < 16384:
                xh[i] = lat[t]
                hm[:, i] = 1.0
        maps.append({"xT": xTs[k], "xh": np.ascontiguousarray(xh.T), "hmask": hm, "g": to_fm(g), "mod": mod,
                     "w_in": inp["rg_w_in"][j], "conv_w": cw, "conv_b": to_fm(inp["rg_conv_b"][j])})
    ra = _run(_prog("rga", build_rga), maps)
    gb = np.zeros((128, 3, 2, 8), np.float32)
    for i, nm_ in enumerate(("rg_ba", "rg_bi", "rg_lambda")):
        for d in range(2):
            gb[:, i, d, :] = to_fm(inp[nm_][j, d])
    maps = [{"XC": ra[k]["XC"], "wa": inp["rg_wa"][j], "wi": inp["rg_wi"][j], "gbias": gb} for k in range(NCORES)]
    rb = _run(_prog("rgs", build_rgs, False), maps)
    comp_all = np.ascontiguousarray(np.stack([r["comp"] for r in rb], 2))
    DEBUG["comp_all"] = comp_all
    maps2 = []
    for k in range(NCORES):
        oh = np.zeros((128, 2, 9), np.float32)
        oh[:, 0, k] = 1.0
        oh[:, 1, 7 - k] = 1.0
        m = dict(maps[k])
        m.update({"comp_all": comp_all, "onehot": oh, "G": ra[k]["G"], "w_out": inp["rg_w_out"][j], "xT": xTs[k], "mod": mod})
        maps2.append(m)
    rc = _run(_prog("rgs", build_rgs, True), maps2)
    return [r["xo"] for r in rc]
import ml_dtypes


def _na_slot_pairs(k):
    pairs = []
    for lp in range(NSLOT):
        P = 16 * k - 2 + lp
        pairs.append(P if 0 <= P < 128 else None)
    sub = {}
    if k == 0:
        pairs[1] = 3
        sub[1] = 0
    if k == NCORES - 1:
        pairs[18] = 124
        sub[18] = 15
    return pairs, sub


def _na_tables(rpb, k):
    pairs, sub = _na_slot_pairs(k)
    tab = np.full((16, 128, 25, 128), -30000.0, np.float32)
    ki = np.arange(128)
    qi = np.arange(128)
    for st_, jl in enumerate((0, 1, 5, 14, 15)):
        for d in range(5):
            lp = jl + d
            P = pairs[lp]
            if P is None or (lp in sub and sub[lp] != jl):
                continue
            kr = 2 * P + ki // 64
            kc = ki % 64
            r = 2 * (16 * k + jl) + qi // 64
            qc = qi % 64
            rs = np.clip(r - 4, 0, 248)
            cs = np.clip(qc - 8, 0, 48)
            ok = ((kr[:, None] >= rs[None, :]) & (kr[:, None] < rs[None, :] + 8) &
                  (kc[:, None] >= cs[None, :]) & (kc[:, None] < cs[None, :] + 16))
            ri = np.clip(kr[:, None] - r[None, :] + 7, 0, 14)
            ci = np.clip(kc[:, None] - qc[None, :] + 15, 0, 30)
            vals = rpb[:, ri, ci]
            tab[:, :, st_ * 5 + d, :] = np.where(ok[None], vals, np.float32(-30000.0))
    return tab


def run_na(xTs, mod, g, inp):
    qkg = np.zeros((128, 2), np.float32)
    qkg[:, 0] = np.tile(inp["na_q_g"][0], 2)
    qkg[:, 1] = np.tile(inp["na_k_g"][0], 2)
    sc = np.zeros((128, 2), np.float32)
    maps = [{"xT": xTs[k], "mod": mod, "g": to_fm(g), "w_qkv": inp["na_w_qkv"][0], "qkg": qkg, "qscale": sc} for k in range(NCORES)]
    for m in maps:
        m.pop("qscale")
    ra = _run(_prog("naa", build_naa), maps)
    Kfull = np.concatenate([r["KT"][:, :2048] for r in ra], 1)
    Kctx = ra[0]["KT"][:, 2048:]
    Vfull = np.concatenate([r["V"][:2048] for r in ra], 0)
    Vctx = ra[0]["V"][2048:]
    maps2 = []
    for k in range(NCORES):
        pairs, _ = _na_slot_pairs(k)
        kt = np.zeros((1024, KW), Kfull.dtype)
        vt = np.zeros((KW, 1024), Vfull.dtype)
        for lp, P in enumerate(pairs):
            if P is not None:
                kt[:, lp * 128:(lp + 1) * 128] = Kfull[:, P * 128:(P + 1) * 128]
                vt[lp * 128:(lp + 1) * 128] = Vfull[P * 128:(P + 1) * 128]
        kt[:, NSLOT * 128:] = Kctx
        vt[NSLOT * 128:] = Vctx
        vp = np.ascontiguousarray(vt.reshape(NKT, 128, 8, 128).transpose(2, 1, 0, 3))
        maps2.append({"QT": ra[k]["QT"], "KText": kt, "Vp": vp, "tab": _na_tables(inp["na_rpb"][0], k),
                      "w_o": inp["na_w_o"][0], "xT": xTs[k], "mod": mod})
    rb = _run(_prog("nab", build_nab), maps2)
    return [r["xo"] for r in rb]
def _dft_tables():
    k = np.arange(256)
    ang = 2 * np.pi * np.outer(k, k) / 256.0
    cwsw = np.stack([np.cos(ang) / 16.0, -np.sin(ang) / 16.0]).astype(np.float32)
    F256 = np.stack([np.cos(ang) / 16.0, np.sin(ang) / 16.0]).astype(np.float32)
    j = np.arange(128)
    a1 = 2 * np.pi * np.outer(j, j) / 128.0
    s = 1.0 / np.sqrt(128.0)
    C1, S1 = np.cos(a1) * s, np.sin(a1) * s
    R12 = np.stack([np.concatenate([C1, -S1], 1), np.concatenate([S1, C1], 1)]).astype(np.float32)
    CS1 = np.stack([C1, S1]).astype(np.float32)
    at = 2 * np.pi * np.outer(j, j) / 16384.0
    TW = np.stack([np.cos(at), np.sin(at)]).astype(np.float32)
    return cwsw, F256, R12, CS1, TW


def run_ft(xTs, mod, g, inp):
    cwsw, F256, R12, CS1, TW = _dft_tables()
    maps = [{"xT": xTs[k], "mod": mod, "g": to_fm(g), "cwsw": cwsw} for k in range(NCORES)]
    ra = _run(_prog("fta", build_fta), maps)
    Z = []
    Zc = []
    for nm_ in ("ZR", "ZI"):
        Z.append(np.concatenate([r[nm_][:, :2048].T for r in ra], 0))
        Zc.append(ra[0][nm_][:, 2048:].T)
    maps2 = []
    for k in range(NCORES):
        sl = slice(128 * k, 128 * (k + 1))
        m = {"R12": R12, "CS1": CS1, "TW": TW, "F256": F256,
             "Zc": np.ascontiguousarray(np.stack([Zc[0][:, sl], Zc[1][:, sl]]))}
        for nm_, z in (("Zr", Z[0]), ("Zi", Z[1])):
            m[nm_] = np.ascontiguousarray(z[:, sl].reshape(128, 128, 128).transpose(0, 2, 1))
        maps2.append(m)
    rb = _run(_prog("ftb", build_ftb), maps2)
    f = np.concatenate([r["F"].transpose(0, 2, 1).reshape(16384, 128) for r in rb], 1)
    fc = np.concatenate([r["Fc"] for r in rb], 1)
    DEBUG["f"] = f
    fTs = _core_xT(f, fc)
    maps3 = [{"fT": fTs[k], "w": inp["ft_w_out"][0], "xT": xTs[k], "mod": mod} for k in range(NCORES)]
    rc = _run(_prog("proj", build_proj), maps3)
    return [r["xo"] for r in rc]
def kernel(**inp):
    inp = {k: np.asarray(v) for k, v in inp.items()}
    mods = run_ada(inp)
    xTs = _core_xT(inp["x"][0], inp["ctx"][0])
    ng = inp["norm_g"]
    xTs = run_rg(xTs, mods[0], ng[0, 0], inp, 0)
    xTs = run_ffn(xTs, mods[0], ng[0, 1], inp["ffn_w_gu"][0:1], inp["ffn_w_down"][0:1])
    xTs = run_na(xTs, mods[1], ng[1, 0], inp)
    xTs = run_ffn(xTs, mods[1], ng[1, 1], inp["moe_w_gu"][0], inp["moe_w_down"][0], inp["moe_router"][0])
    xTs = run_ft(xTs, mods[2], ng[2, 0], inp)
    xTs = run_ffn(xTs, mods[2], ng[2, 1], inp["ffn_w_gu"][1:2], inp["ffn_w_down"][1:2])
    xTs = run_rg(xTs, mods[3], ng[3, 0], inp, 1)
    xTs = run_ffn(xTs, mods[3], ng[3, 1], inp["moe_w_gu"][1], inp["moe_w_down"][1], inp["moe_router"][1])
    lat, _ = _split_x(xTs)
    return np.ascontiguousarray(lat[None].astype(np.float32))
```

```python
import contextlib
import numpy as np
import concourse.bass as bass
import concourse.mybir as mybir
from concourse.bass_utils import run_bass_kernel_spmd

F32 = mybir.dt.float32
BF16 = mybir.dt.bfloat16
AF = mybir.ActivationFunctionType
ALU = mybir.AluOpType
AX = mybir.AxisListType


class Buf:
    __slots__ = ("w", "r", "name")

    def __init__(self, name=""):
        self.w = None
        self.r = []
        self.name = name


class Prog:
    NDSEM = 24

    def __init__(self):
        self.nc = bass.Bass("TRN2", target_bir_lowering=False)
        self.es = contextlib.ExitStack()
        self.recs = {k: [] for k in ("pe", "act", "dve", "pool", "sp")}
        self.cnt = {k: 0 for k in self.recs}
        self.waited = {k: {} for k in self.recs}
        self.sems = {}
        for i in range(self.NDSEM):
            self._sem("d%d" % i)
        self.ndma = 0
        self.dma_hist = []
        self.nbuf = 0

    EPOCH = 30000

    def _sem(self, key):
        if key not in self.sems:
            self.sems[key] = self.es.enter_context(self.nc.semaphore("s_" + key.replace("#", "_")))
        return self.sems[key]

    def sbuf(self, shape, dtype, name=None):
        self.nbuf += 1
        t = self.es.enter_context(self.nc.sbuf_tensor((name + "_sb") if name else ("sb%d" % self.nbuf), list(shape), dtype))
        return t

    def psum(self, shape, dtype=F32, name=None):
        self.nbuf += 1
        t = self.es.enter_context(self.nc.psum_tensor((name + "_ps") if name else ("ps%d" % self.nbuf), list(shape), dtype))
        return t

    def dram_in(self, name, shape, dtype=F32):
        return self.nc.dram_tensor(name, list(shape), dtype, kind="ExternalInput").ap()

    def dram_out(self, name, shape, dtype=F32):
        return self.nc.dram_tensor(name, list(shape), dtype, kind="ExternalOutput").ap()

    def _need(self, eng, ev, waits):
        if ev is None:
            return
        key, val = ev
        if self.waited[eng].get(key, 0) >= val:
            return
        if eng == "pe" and key.startswith("pe#"):
            return
        self.waited[eng][key] = val
        waits[key] = max(waits.get(key, 0), val)

    def _deps(self, eng, reads, writes):
        waits = {}
        for b in reads:
            self._need(eng, b.w, waits)
        for b in writes:
            self._need(eng, b.w, waits)
            for ev in b.r:
                self._need(eng, ev, waits)
        return waits

    def _commit(self, ev, reads, writes):
        for b in reads:
            b.r.append(ev)
        for b in writes:
            b.w = ev
            b.r = []

    def op(self, eng, fn, reads=(), writes=()):
        waits = self._deps(eng, reads, writes)
        c = self.cnt[eng]
        self.cnt[eng] += 1
        key = "%s#%d" % (eng, c // self.EPOCH)
        self._sem(key)
        ev = (key, c % self.EPOCH + 1)
        self.recs[eng].append((list(waits.items()), fn, (key, 1)))
        self._commit(ev, reads, writes)
        return ev

    def I(self, eng, name, *args, reads=(), writes=(), **kw):
        return self.op(eng, lambda e: getattr(e, name)(*args, **kw), reads, writes)

    def dma(self, q, out, in_, reads=(), writes=(), **kw):
        waits = self._deps(q, reads, writes)
        n = self.ndma
        self.ndma += 1
        key = "d%d" % (n % self.NDSEM)
        val = 16 * (n // self.NDSEM + 1)
        if n >= self.NDSEM:
            pk, pv = self.dma_hist[n - self.NDSEM]
            if self.waited[q].get(pk, 0) < pv:
                self.waited[q][pk] = pv
                waits[pk] = max(waits.get(pk, 0), pv)
        ev = (key, val)
        self.dma_hist.append(ev)

        def fn(e, out=out, in_=in_, kw=kw):
            return e.dma_start(out=out, in_=in_, **kw)
        self.recs[q].append((list(waits.items()), fn, (key, 16)))
        self._commit(ev, reads, writes)
        return ev

    def coll(self, kind, ins, outs, reads=(), writes=(), groups=None):
        q = "pool"
        waits = self._deps(q, reads, writes)
        n = self.ndma
        self.ndma += 1
        key = "d%d" % (n % self.NDSEM)
        val = 16 * (n // self.NDSEM + 1)
        if n >= self.NDSEM:
            pk, pv = self.dma_hist[n - self.NDSEM]
            if self.waited[q].get(pk, 0) < pv:
                self.waited[q][pk] = pv
                waits[pk] = max(waits.get(pk, 0), pv)
        ev = (key, val)
        self.dma_hist.append(ev)
        groups = groups or [list(range(8))]

        def fn(e):
            return e.collective_compute(kind, ALU.bypass, replica_groups=groups, ins=list(ins), outs=list(outs))
        self.recs[q].append((list(waits.items()), fn, (key, 16)))
        self._commit(ev, reads, writes)
        return ev

    def wait_all(self, eng, bufs):
        waits = {}
        for b in bufs:
            self._need(eng, b.w, waits)
        if waits:
            self.recs[eng].append((list(waits.items()), None, None))

    def finalize(self):
        nc = self.nc
        sems = self.sems
        recs = self.recs

        def replay(e, lst):
            for waits, fn, inc in lst:
                for key, val in waits:
                    e.wait_ge(sems[key], val)
                if fn is not None:
                    ins = fn(e)
                    ins.then_inc(sems[inc[0]], inc[1])

        with nc.Block() as block:
            @block.sync
            def _(e):
                replay(e, recs["sp"])

            @block.tensor
            def _(e):
                replay(e, recs["pe"])

            @block.scalar
            def _(e):
                replay(e, recs["act"])

            @block.vector
            def _(e):
                replay(e, recs["dve"])

            @block.gpsimd
            def _(e):
                replay(e, recs["pool"])
        self.es.close()
        return nc
NT = 2304
TT = [(0, 512, 0), (512, 512, 0), (1024, 512, 0), (1536, 512, 0), (2048, 256, 1)]
RMS_EPS = 1e-6


def fm(ap):
    return ap.rearrange("(c p) t -> p c t", p=128)


class Common:
    def __init__(self, p, need_ident=False):
        self.p = p
        self.mod_d = p.dram_in("mod", [128, 48, 2])
        self.mod = p.sbuf([128, 48, 2], F32, "mod_sb")
        self.Bmod = Buf()
        p.dma("sp", self.mod[:], self.mod_d, writes=[self.Bmod])
        self.ones = p.sbuf([128, 128], BF16, "ones_bf")
        self.Bones = Buf()
        p.op("dve", lambda e: e.memset(self.ones[:], 1.0), writes=[self.Bones])

    def scale_vec(self, g_d, j, name):
        p = self.p
        g = p.sbuf([128, 8], F32, name + "_g")
        Bg = Buf()
        p.dma("sp", g[:], g_d, writes=[Bg])
        s = p.sbuf([128, 8, 2], F32, name + "_s")
        Bs = Buf()
        p.op("dve", lambda e: e.tensor_scalar(s[:], self.mod[:, j * 8:(j + 1) * 8, :], 1.0, None, ALU.add),
             reads=[self.Bmod], writes=[Bs])
        for st in range(2):
            p.op("dve", lambda e, st=st: e.tensor_tensor(s[:, :, st], s[:, :, st], g[:], ALU.mult),
                 reads=[Bg, Bs], writes=[Bs])
        return s, Bs


class NormMod:
    def __init__(self, p, cm, s, Bs, jshift, ps_bank, Bps, width=512):
        self.p, self.cm, self.s, self.Bs, self.jshift = p, cm, s, Bs, jshift
        self.ps, self.Bps = ps_bank, Bps
        self.sq = [p.sbuf([128, width], BF16, "nm_sq%d" % i) for i in range(2)]
        self.Bsq = [Buf() for _ in range(2)]
        self.rstd = [p.sbuf([128, width], F32, "nm_rstd%d" % i) for i in range(2)]
        self.Brstd = [Buf() for _ in range(2)]
        self.tmp = [p.sbuf([128, width], F32, "nm_tmp%d" % i) for i in range(2)]
        self.Btmp = [Buf() for _ in range(2)]
        self.k = 0

    def run(self, xs, Bxs, n, stream, outs, Bouts):
        p, cm = self.p, self.cm
        ps = self.ps
        for c in range(8):
            i = self.k % 2
            self.k += 1
            p.op("act", lambda e, i=i, c=c: e.activation(self.sq[i][:, :n], xs[c], AF.Square),
                 reads=[Bxs[c]], writes=[self.Bsq[i]])
            p.op("pe", lambda e, i=i, c=c: e.matmul(ps[:, :n], cm.ones[:], self.sq[i][:, :n], start=(c == 0), stop=(c == 7)),
                 reads=[cm.Bones, self.Bsq[i]], writes=[self.Bps])
        r = self.k % 2
        p.op("act", lambda e: e.activation(self.rstd[r][:, :n], ps[:, :n], AF.Sqrt, bias=RMS_EPS, scale=1.0 / 1024.0),
             reads=[self.Bps], writes=[self.Brstd[r]])
        p.op("dve", lambda e: e.reciprocal(self.rstd[r][:, :n], self.rstd[r][:, :n]),
             reads=[self.Brstd[r]], writes=[self.Brstd[r]])
        js = self.jshift
        for c in range(8):
            i = self.k % 2
            self.k += 1
            p.op("dve", lambda e, i=i, c=c: e.tensor_tensor(self.tmp[i][:, :n], xs[c], self.rstd[r][:, :n], ALU.mult),
                 reads=[Bxs[c], self.Brstd[r]], writes=[self.Btmp[i]])
            for oi, (o, Bo) in enumerate(zip(outs, Bouts)):
                p.op("act", lambda e, i=i, c=c, o=o: e.activation(
                    o[c], self.tmp[i][:, :n], AF.Identity,
                    bias=cm.mod[:, js * 8 + c, stream:stream + 1], scale=self.s[:, c, stream:stream + 1]),
                    reads=[self.Btmp[i], cm.Bmod, self.Bs], writes=[Bo[c]])
def build_ada():
    p = Prog()
    w_d = p.dram_in("w", [4, 1024, 768])
    b_d = p.dram_in("b", [4, 128, 6])
    c_d = p.dram_in("cv", [128, 8, 2])
    o_d = p.dram_out("o", [4, 128, 6, 2])
    cv = p.sbuf([128, 8, 2], F32); Bcv = Buf()
    p.dma("sp", cv[:], c_d, writes=[Bcv])
    p.op("act", lambda e: e.activation(cv[:], cv[:], AF.Silu), reads=[Bcv], writes=[Bcv])
    bb = p.sbuf([128, 4, 6], F32); Bbb = Buf()
    p.dma("sp", bb[:], b_d.rearrange("l p i -> p l i"), writes=[Bbb])
    ob = p.sbuf([128, 4, 6, 2], F32); Bob = Buf()
    ws = [p.sbuf([128, 8, 768], F32, "adaw%d" % l) for l in range(4)]
    Bws = [Buf() for _ in range(4)]
    ps = [p.psum([128, 512], F32, "adaps%d" % i) for i in range(2)]
    Bps = [Buf() for _ in range(2)]
    for l in range(4):
        p.dma("sp" if l % 2 == 0 else "act", ws[l][:], w_d[l].rearrange("(k p) m -> p k m", p=128), writes=[Bws[l]])
    n = 0
    for l in range(4):
        for i in range(6):
            b = n % 2
            n += 1
            for k in range(8):
                p.op("pe", lambda e, l=l, i=i, k=k, b=b: e.matmul(
                    ps[b][:, 0:2], ws[l][:, k, i * 128:(i + 1) * 128], cv[:, k, :], start=(k == 0), stop=(k == 7)),
                    reads=[Bws[l], Bcv], writes=[Bps[b]])
            p.op("act", lambda e, l=l, i=i, b=b: e.activation(ob[:, l, i, :], ps[b][:, 0:2], AF.Identity,
                                                           bias=bb[:, l, i:i + 1], scale=1.0),
                 reads=[Bps[b], Bbb], writes=[Bob])
    Bo = Buf()
    p.dma("sp", o_d.rearrange("l p i s -> p l i s"), ob[:], reads=[Bob], writes=[Bo])
    p.wait_all("sp", [Bo])
    return p.finalize()


DEV_NG = 99
FFN_DMACAST = True
NYB = 4


def build_ffn(E):
    p = Prog()
    xT = p.dram_in("xT", [1024, NT])
    g_d = p.dram_in("g", [128, 8])
    wgu = p.dram_in("wgu", [E, 1024, 7168])
    wdn = p.dram_in("wdn", [E, 3584, 1024])
    xo = p.dram_out("xo", [1024, NT])
    cm = Common(p)
    s4, Bs4 = cm.scale_vec(g_d, 4, "s4")
    moe = E > 1
    if moe:
        rt_d = p.dram_in("router", [1024, 8])
        id_d = p.dram_in("ident", [128, 128])
        sel_d = p.dram_in("sel", [8, 8, 128])
        rt = p.sbuf([128, 8, 8], F32, "rt"); Brt = Buf()
        p.dma("sp", rt[:], rt_d.rearrange("(k p) e -> p k e", p=128), writes=[Brt])
        ident = p.sbuf([128, 128], F32, "ident_sb"); Bid = Buf()
        p.dma("sp", ident[:], id_d, writes=[Bid])
        sel = p.sbuf([8, 8, 128], F32, "sel_sb"); Bsel = Buf()
        p.dma("sp", sel[:], sel_d, writes=[Bsel])
        GT = p.sbuf([8, NT], F32, "GT"); BGT = [Buf() for _ in TT]
        h32 = p.sbuf([128, 8, 512], F32, "h32"); Bh32 = [Buf() for _ in range(8)]
        gbc = [p.sbuf([128, NT], F32, "gbc%d" % i) for i in range(1)]
        Bgbc = [[Buf() for _ in TT] for _ in range(1)]
        swt = [p.sbuf([128, 512], F32, "swt%d" % i) for i in range(2)]
        Bswt = [Buf() for _ in range(2)]
        sm = {k: p.sbuf([128, 8], F32, "sm_" + k) for k in ("l", "eq1", "l2", "eq2", "G")}
        sv = {k: p.sbuf([128, 1], F32, "sv_" + k) for k in ("m1", "m2", "d", "w1", "w2")}
        Bsm = Buf()
    x = p.sbuf([128, 8, NT], F32, "x")
    Bx = [[Buf() for _ in TT] for _ in range(8)]
    hb = p.sbuf([128, 8, NT], BF16, "hb")
    Bhb = [[Buf() for _ in TT] for _ in range(8)]
    ps = [p.psum([128, 512], F32, "bank%d" % i) for i in range(8)]
    Bps = [Buf() for _ in range(8)]
    xv = fm(xT)
    for ti, (t0, n, st) in enumerate(TT):
        for c in range(8):
            p.dma("sp", x[:, c, t0:t0 + n], xv[:, c, t0:t0 + n], writes=[Bx[c][ti]])
    nm = NormMod(p, cm, s4, Bs4, 3, ps[7], Bps[7])
    for ti, (t0, n, st) in enumerate(TT):
        xs = [x[:, c, t0:t0 + n] for c in range(8)]
        Bxs = [Bx[c][ti] for c in range(8)]
        outs = [[hb[:, c, t0:t0 + n] for c in range(8)]]
        Bouts = [[Bhb[c][ti] for c in range(8)]]
        if moe:
            outs.append([h32[:, c, :n] for c in range(8)])
            Bouts.append(Bh32)
        nm.run(xs, Bxs, n, st, outs, Bouts)
        if moe:
            for sub in range(n // 128):
                lg = ps[6][:, 0:8]
                for k in range(8):
                    p.op("pe", lambda e, k=k, sub=sub: e.matmul(lg, h32[:, k, sub * 128:(sub + 1) * 128], rt[:, k, :],
                                                               start=(k == 0), stop=(k == 7)),
                         reads=[Bh32[k], Brt], writes=[Bps[6]])
                D = lambda fn, rd=(), wr=(): p.op("dve", fn, reads=[Bsm] + list(rd), writes=[Bsm] + list(wr))
                D(lambda e: e.tensor_copy(sm["l"][:], lg), rd=[Bps[6]])
                D(lambda e: e.reduce_max(sv["m1"][:], sm["l"][:], AX.X))
                D(lambda e: e.tensor_scalar(sm["eq1"][:], sm["l"][:], sv["m1"][:, 0:1], None, ALU.is_equal))
                D(lambda e: e.scalar_tensor_tensor(sm["l2"][:], sm["eq1"][:], -1e30, sm["l"][:], ALU.mult, ALU.add))
                D(lambda e: e.reduce_max(sv["m2"][:], sm["l2"][:], AX.X))
                D(lambda e: e.tensor_scalar(sm["eq2"][:], sm["l2"][:], sv["m2"][:, 0:1], None, ALU.is_equal))
                D(lambda e: e.tensor_tensor(sv["d"][:], sv["m2"][:], sv["m1"][:], ALU.subtract))
                p.op("act", lambda e: e.activation(sv["d"][:], sv["d"][:], AF.Exp), reads=[Bsm], writes=[Bsm])
                D(lambda e: e.tensor_scalar(sv["w1"][:], sv["d"][:], 1.0, None, ALU.add))
                D(lambda e: e.reciprocal(sv["w1"][:], sv["w1"][:]))
                D(lambda e: e.tensor_tensor(sv["w2"][:], sv["d"][:], sv["w1"][:], ALU.mult))
                D(lambda e: e.tensor_scalar(sm["G"][:], sm["eq1"][:], sv["w1"][:, 0:1], None, ALU.mult))
                D(lambda e: e.scalar_tensor_tensor(sm["G"][:], sm["eq2"][:], sv["w2"][:, 0:1], sm["G"][:], ALU.mult, ALU.add))
                tp = ps[6][0:8, 128:256]
                p.op("pe", lambda e: e.transpose(tp, sm["G"][:], ident[:]), reads=[Bsm, Bid], writes=[Bps[6]])
                p.op("act", lambda e, sub=sub, t0=t0: e.activation(GT[:, t0 + sub * 128:t0 + (sub + 1) * 128], tp, AF.Identity),
                     reads=[Bps[6]], writes=[BGT[ti]])
    GH = 2 if moe else 4
    GW = GH * 128
    NG = min(28 // GH, DEV_NG)
    wg_sb = [p.sbuf([128, 8, 2, GW], BF16, "wgu%d" % i) for i in range(2)]
    wd_sb = [p.sbuf([128, GH, 1024], BF16, "wdn%d" % i) for i in range(2)]
    Bwg = [[Buf() for _ in range(8)] for _ in range(2)]
    Bwd = [[Buf() for _ in range(GH)] for _ in range(2)]
    act = [p.sbuf([128, GH, 512], BF16, "act%d" % i) for i in range(2)]
    Bact = [[Buf() for _ in range(GH)] for _ in range(2)]
    sg = [p.sbuf([128, 512], F32, "sg%d" % i) for i in range(2)]
    Bsg = [Buf() for _ in range(2)]
    NSTG = 3 if moe else 6
    stg = [p.sbuf([128, 1024], F32, "stg%d" % i) for i in range(NSTG)]
    Bstg = [Buf() for _ in range(NSTG)]
    cnt = {"si": 0, "hi": 0, "yi": 0}

    def load_parts(ex, j, wb):
        wv = wgu[ex].rearrange("(k p) (h m) -> p k h m", p=128, h=2)
        dv = wdn[ex].rearrange("(k p) m -> p k m", p=128)
        c0 = j * GW
        steps = []

        def gu(k):
            if FFN_DMACAST:
                p.dma("pool", wg_sb[wb][:, k, :, :], wv[:, k, :, c0:c0 + GW], writes=[Bwg[wb][k]])
                return
            sb_ = cnt["si"] % NSTG
            cnt["si"] += 1
            sv_ = stg[sb_][:, 0:2 * GW].rearrange("p (a b) -> p a b", a=2)
            p.dma("sp", sv_, wv[:, k, :, c0:c0 + GW], writes=[Bstg[sb_]])
            p.I("pool", "tensor_copy", wg_sb[wb][:, k, :, :], sv_, reads=[Bstg[sb_]], writes=[Bwg[wb][k]])

        def dn(hc):
            if FFN_DMACAST:
                p.dma("pool", wd_sb[wb][:, hc, :], dv[:, j * GH + hc, :], writes=[Bwd[wb][hc]])
                return
            sb_ = cnt["si"] % NSTG
            cnt["si"] += 1
            p.dma("sp", stg[sb_][:], dv[:, j * GH + hc, :], writes=[Bstg[sb_]])
            p.I("pool", "tensor_copy", wd_sb[wb][:, hc, :], stg[sb_][:], reads=[Bstg[sb_]], writes=[Bwd[wb][hc]])

        for k in range(8):
            steps.append(lambda k=k: gu(k))
        for hc in range(GH):
            steps.append(lambda hc=hc: dn(hc))
        return steps

    def gate_bc(ex):
        for ti, (t0, n, st) in enumerate(TT):
            p.I("pe", "matmul", ps[7][:, :n], sel[:, ex, :], GT[:, t0:t0 + n], start=True, stop=True,
                reads=[Bsel, BGT[ti]], writes=[Bps[7]])
            p.I("act", "activation", gbc[0][:, t0:t0 + n], ps[7][:, :n], AF.Identity, reads=[Bps[7]], writes=[Bgbc[0][ti]])

    def stage1(item, hcs=None):
        ex, j, wb, ti, ab = item
        t0, n, st = TT[ti]
        for hc in (range(GH) if hcs is None else hcs):
            s_ = cnt["hi"] % 2
            cnt["hi"] += 1
            pg, pu = ps[2 * s_], ps[2 * s_ + 1]
            for half, pp, Bpp in ((0, pg, Bps[2 * s_]), (1, pu, Bps[2 * s_ + 1])):
                for k in range(8):
                    p.I("pe", "matmul", pp[:, :n], wg_sb[wb][:, k, half, hc * 128:(hc + 1) * 128], hb[:, k, t0:t0 + n],
                        start=(k == 0), stop=(k == 7), reads=[Bwg[wb][k], Bhb[k][ti]], writes=[Bpp])
            p.I("act", "activation", sg[s_][:, :n], pg[:, :n], AF.Silu, reads=[Bps[2 * s_]], writes=[Bsg[s_]])
            if moe:
                p.I("dve", "tensor_tensor", swt[s_][:, :n], sg[s_][:, :n], pu[:, :n], ALU.mult,
                    reads=[Bsg[s_], Bps[2 * s_ + 1]], writes=[Bswt[s_]])
                p.I("pool", "tensor_tensor", act[ab][:, hc, :n], swt[s_][:, :n], gbc[0][:, t0:t0 + n], ALU.mult,
                    reads=[Bswt[s_], Bgbc[0][ti]], writes=[Bact[ab][hc]])
            else:
                p.I("dve", "tensor_tensor", act[ab][:, hc, :n], sg[s_][:, :n], pu[:, :n], ALU.mult,
                    reads=[Bsg[s_], Bps[2 * s_ + 1]], writes=[Bact[ab][hc]])

    def stage2(item, ocs=None):
        ex, j, wb, ti, ab = item
        t0, n, st = TT[ti]
        for oc in (range(8) if ocs is None else ocs):
            yb = 4 + (cnt["yi"] % NYB)
            cnt["yi"] += 1
            for hc in range(GH):
                p.I("pe", "matmul", ps[yb][:, :n], wd_sb[wb][:, hc, oc * 128:(oc + 1) * 128], act[ab][:, hc, :n],
                    start=(hc == 0), stop=(hc == GH - 1), reads=[Bwd[wb][hc], Bact[ab][hc]], writes=[Bps[yb]])
            xs = x[:, oc, t0:t0 + n]
            p.I("dve", "scalar_tensor_tensor", xs, ps[yb][:, :n], cm.mod[:, 40 + oc, st:st + 1], xs, ALU.mult, ALU.add,
                reads=[Bps[yb], cm.Bmod, Bx[oc][ti]], writes=[Bx[oc][ti]])

    groups = [(ex, j) for ex in range(E) for j in range(NG)]
    nT = len(TT)
    for st_ in load_parts(groups[0][0], groups[0][1], 0):
        st_()
    prev = None
    n_items = 0
    for gidx, (ex, j) in enumerate(groups):
        wb = gidx % 2
        nxt = load_parts(groups[gidx + 1][0], groups[gidx + 1][1], (gidx + 1) % 2) if gidx + 1 < len(groups) else []
        per = -(-len(nxt) // nT)
        for ti in range(nT):
            item = (ex, j, wb, ti, n_items % 2)
            n_items += 1
            if ti == 0 and moe and j == 0:
                if prev is not None:
                    stage2(prev)
                    prev = None
                gate_bc(ex)
            for hcs, ocs in ((range(0, GH // 2), range(0, 4)), (range(GH // 2, GH), range(4, 8))):
                stage1(item, hcs)
                if prev is not None:
                    stage2(prev, ocs)
            prev = item
            for st_ in nxt[ti * per:(ti + 1) * per]:
                st_()
    stage2(prev)
    ov = fm(xo)
    Bo = []
    for ti, (t0, n, st) in enumerate(TT):
        for c in range(8):
            b = Buf()
            p.dma("sp", ov[:, c, t0:t0 + n], x[:, c, t0:t0 + n], reads=[Bx[c][ti]], writes=[b])
            Bo.append(b)
    p.wait_all("sp", Bo)
    return p.finalize()
class Caster:
    def __init__(self, p, n=3, width=1024):
        self.p = p
        self.w = width
        self.stg = [p.sbuf([128, width], F32, "cst%d" % i) for i in range(n)]
        self.B = [Buf() for _ in range(n)]
        self.i = 0

    def load(self, dst, src, Bdst, eng="pool"):
        p = self.p
        m = dst.shape[-1]
        k = self.i % len(self.stg)
        self.i += 1
        p.dma("sp", self.stg[k][:, :m], src, writes=[self.B[k]])
        if eng == "pool":
            p.op("pool", lambda e: e.tensor_copy(dst, self.stg[k][:, :m]), reads=[self.B[k]], writes=[Bdst])
        else:
            p.op("act", lambda e: e.activation(dst, self.stg[k][:, :m], AF.Identity), reads=[self.B[k]], writes=[Bdst])


XZW = 2310


def build_rga():
    p = Prog()
    xT = p.dram_in("xT", [1024, NT])
    xh = p.dram_in("xh", [1024, 3])
    hm_d = p.dram_in("hmask", [128, 3])
    g_d = p.dram_in("g", [128, 8])
    win = p.dram_in("w_in", [1024, 2048])
    cw_d = p.dram_in("conv_w", [128, 8, 4])
    cb_d = p.dram_in("conv_b", [128, 8])
    XC = p.dram_out("XC", [1024, NT])
    GO = p.dram_out("G", [1024, NT])
    cm = Common(p)
    s1, Bs1 = cm.scale_vec(g_d, 1, "s1")
    cw = p.sbuf([128, 8, 4], F32, "cw"); cb = p.sbuf([128, 8], F32, "cb"); hm = p.sbuf([128, 3], F32, "hm")
    Bc = Buf()
    p.dma("sp", cw[:], cw_d, writes=[Bc]); p.dma("sp", cb[:], cb_d, writes=[Bc]); p.dma("sp", hm[:], hm_d, writes=[Bc])
    ps = [p.psum([128, 512], F32, "bank%d" % i) for i in range(8)]
    Bps = [Buf() for _ in range(8)]
    NH = NT + 3
    hb = p.sbuf([128, 8, NH], BF16, "hb")
    tiles = TT + [(NT, 3, 0)]
    Bhb = [[Buf() for _ in tiles] for _ in range(8)]
    wsb = p.sbuf([128, 8, 2048], BF16, "w_in_sb")
    Bw = [[Buf() for _ in range(2)] for _ in range(8)]
    cst = Caster(p, 3, 1024)
    wv = win.rearrange("(k p) m -> p k m", p=128)
    xt = [p.sbuf([128, 8, 512], F32, "xt%d" % i) for i in range(2)]
    Bxt = [[Buf() for _ in range(8)] for _ in range(2)]
    nm = NormMod(p, cm, s1, Bs1, 0, ps[7], Bps[7])
    xv = fm(xT)
    xhv = fm(xh)
    for ti, (t0, n, st) in enumerate(tiles):
        b = ti % 2
        for c in range(8):
            src = xv[:, c, t0:t0 + n] if ti < 5 else xhv[:, c, :]
            p.dma("sp", xt[b][:, c, :n], src, writes=[Bxt[b][c]])
        nm.run([xt[b][:, c, :n] for c in range(8)], Bxt[b], n, st,
               [[hb[:, c, t0:t0 + n] for c in range(8)]], [[Bhb[c][ti] for c in range(8)]])
        if ti == 0:
            for k in range(8):
                for hh in range(2):
                    cst.load(wsb[:, k, hh * 1024:(hh + 1) * 1024], wv[:, k, hh * 1024:(hh + 1) * 1024], Bw[k][hh])
    xz = [p.sbuf([128, XZW], F32, "xz%d" % i) for i in range(2)]
    Bxz = [Buf() for _ in range(2)]
    xco = [p.sbuf([128, NT], F32, "xco%d" % i) for i in range(2)]
    Bxco = [Buf() for _ in range(2)]
    gsb = [p.sbuf([128, 512], F32, "gsb%d" % i) for i in range(2)]
    Bgsb = [Buf() for _ in range(2)]
    for i in range(2):
        p.op("dve", lambda e, i=i: e.memset(xz[i][:], 0.0), writes=[Bxz[i]])
    Bout = []
    pi = 0
    gi = 0
    for oc in range(16):
        hh = oc // 8
        zb = oc % 2
        for ti, (t0, n, st) in enumerate(tiles):
            bk = pi % 6
            pi += 1
            for k in range(8):
                p.op("pe", lambda e, k=k, oc=oc, bk=bk, t0=t0, n=n: e.matmul(
                    ps[bk][:, :n], wsb[:, k, oc * 128:(oc + 1) * 128], hb[:, k, t0:t0 + n], start=(k == 0), stop=(k == 7)),
                    reads=[Bw[k][hh], Bhb[k][ti]], writes=[Bps[bk]])
            if hh == 0:
                if ti < 4:
                    p.op("act", lambda e, zb=zb, bk=bk, t0=t0, n=n: e.activation(xz[zb][:, 2 + t0:2 + t0 + n], ps[bk][:, :n], AF.Identity),
                         reads=[Bps[bk]], writes=[Bxz[zb]])
                elif ti == 4:
                    p.op("act", lambda e, zb=zb, bk=bk, n=n: e.activation(xz[zb][:, 2053:2053 + n], ps[bk][:, :n], AF.Identity),
                         reads=[Bps[bk]], writes=[Bxz[zb]])
                else:
                    p.op("dve", lambda e, zb=zb, bk=bk: e.tensor_tensor(xz[zb][:, 0:2], ps[bk][:, 0:2], hm[:, 0:2], ALU.mult),
                         reads=[Bps[bk], Bc], writes=[Bxz[zb]])
                    p.op("dve", lambda e, zb=zb, bk=bk: e.tensor_tensor(xz[zb][:, 2050:2051], ps[bk][:, 2:3], hm[:, 2:3], ALU.mult),
                         reads=[Bps[bk], Bc], writes=[Bxz[zb]])
            elif ti < 5:
                gb = gi % 2
                gi += 1
                p.op("act", lambda e, gb=gb, bk=bk, n=n: e.activation(gsb[gb][:, :n], ps[bk][:, :n], AF.Gelu_apprx_tanh),
                     reads=[Bps[bk]], writes=[Bgsb[gb]])
                b_ = Buf()
                p.dma("sp", fm(GO)[:, oc - 8, t0:t0 + n], gsb[gb][:, :n], reads=[Bgsb[gb]], writes=[b_])
                Bout.append(b_)
        if hh == 0:
            for (o0, ln, z0) in ((0, 2048, 0), (2048, 256, 2051)):
                p.op("dve", lambda e, zb=zb, oc=oc, o0=o0, ln=ln, z0=z0: e.tensor_scalar(
                    xco[zb][:, o0:o0 + ln], xz[zb][:, z0:z0 + ln], cw[:, oc, 0:1], cb[:, oc:oc + 1], ALU.mult, ALU.add),
                    reads=[Bxz[zb], Bc], writes=[Bxco[zb]])
                for j in range(1, 4):
                    p.op("dve", lambda e, zb=zb, oc=oc, o0=o0, ln=ln, z0=z0, j=j: e.scalar_tensor_tensor(
                        xco[zb][:, o0:o0 + ln], xz[zb][:, z0 + j:z0 + j + ln], cw[:, oc, j:j + 1], xco[zb][:, o0:o0 + ln],
                        ALU.mult, ALU.add),
                        reads=[Bxz[zb], Bc, Bxco[zb]], writes=[Bxco[zb]])
            b_ = Buf()
            p.dma("sp", fm(XC)[:, oc, :], xco[zb][:], reads=[Bxco[zb]], writes=[b_])
            Bout.append(b_)
    p.wait_all("sp", Bout)
    return p.finalize()
def build_rgs(full):
    p = Prog()
    XC = p.dram_in("XC", [1024, NT])
    wa_d = p.dram_in("wa", [2, 4, 256, 256])
    wi_d = p.dram_in("wi", [2, 4, 256, 256])
    gb_d = p.dram_in("gbias", [128, 3, 2, 8])
    ps = [p.psum([128, 512], F32, "bank%d" % i) for i in range(8)]
    Bps = [Buf() for _ in range(8)]
    gbs = p.sbuf([128, 3, 2, 8], F32, "gbs"); Bgb = Buf()
    p.dma("sp", gbs[:], gb_d, writes=[Bgb])
    cl = p.sbuf([128, 2, 2, 8], F32, "cl"); Bcl = Buf()
    p.op("act", lambda e: e.activation(cl[:, 0], gbs[:, 2], AF.Exp, scale=-1.0), reads=[Bgb], writes=[Bcl])
    p.op("act", lambda e: e.activation(cl[:, 0], cl[:, 0], AF.Ln, bias=1.0), reads=[Bcl], writes=[Bcl])
    p.op("dve", lambda e: e.tensor_scalar(cl[:, 1], cl[:, 0], -16.0, None, ALU.mult), reads=[Bcl], writes=[Bcl])
    p.op("dve", lambda e: e.tensor_scalar(cl[:, 0], cl[:, 0], -8.0, None, ALU.mult), reads=[Bcl], writes=[Bcl])
    cst = Caster(p, 3, 1152)
    gw = p.sbuf([128, 2, 2, 4, 2, 256], BF16, "gw"); Bgw = Buf()
    for d in range(2):
        for gi_, src in enumerate((wa_d, wi_d)):
            for n_ in range(4):
                cst.load(gw[:, d, gi_, n_].rearrange("p a b -> p (a b)"),
                         src[d, n_].rearrange("(kk p) m -> p kk m", p=128), Bgw)
    xcb = p.sbuf([128, 8, NT], BF16, "xcb"); Bxcb = [Buf() for _ in range(8)]
    xcv = fm(XC)
    for c in range(8):
        for hh in range(2):
            cst.load(xcb[:, c, hh * 1152:(hh + 1) * 1152], xcv[:, c, hh * 1152:(hh + 1) * 1152], Bxcb[c], eng="act" if c % 2 else "pool")
    xcf = [p.sbuf([128, NT], F32, "xcf%d" % i) for i in range(2)]; Bxcf = [Buf() for _ in range(2)]
    T = {k: [p.sbuf([128, 512], F32, "t_%s%d" % (k, i)) for i in range(2)] for k in ("r", "i", "a", "s", "b")}
    BT = {k: [Buf() for _ in range(2)] for k in T}
    zeros = p.sbuf([128, 512], F32, "zeros"); Bz = Buf()
    p.op("dve", lambda e: e.memset(zeros[:], 0.0), writes=[Bz])
    hbuf = [p.sbuf([128, NT], F32, "hf%d" % i) for i in range(2)]
    Bh = [[Buf() for _ in TT] for _ in range(2)]
    junk = p.sbuf([128, 512], F32, "junk"); Bjunk = Buf()
    comp = p.sbuf([128, 8, 4], F32, "comp"); Bcomp = Buf()
    if full:
        ca_d = p.dram_in("comp_all", [128, 8, 8, 4])
        oh_d = p.dram_in("onehot", [128, 2, 9])
        G_d = p.dram_in("G", [1024, NT])
        wo_d = p.dram_in("w_out", [1024, 1024])
        xT = p.dram_in("xT", [1024, NT])
        xo = p.dram_out("xo", [1024, NT])
        cm = Common(p)
        ca = p.sbuf([128, 8, 8, 4], F32, "ca"); oh = p.sbuf([128, 2, 9], F32, "oh"); Bca = Buf()
        p.dma("sp", ca[:], ca_d, writes=[Bca]); p.dma("sp", oh[:], oh_d, writes=[Bca])
        ext = p.sbuf([128, 9], F32, "ext"); ext2 = p.sbuf([128, 9], F32, "ext2"); car = p.sbuf([128, 1], F32, "car")
        Bext = Buf()
        wo = p.sbuf([128, 8, 1024], BF16, "wo"); Bwo = [Buf() for _ in range(8)]
        for k in range(8):
            cst.load(wo[:, k, :], wo_d.rearrange("(k p) m -> p k m", p=128)[:, k, :], Bwo[k])
        yin = p.sbuf([128, 8, NT], BF16, "yin"); Byin = [[Buf() for _ in TT] for _ in range(8)]
        gch = [p.sbuf([128, NT], F32, "gch%d" % i) for i in range(2)]; Bgch = [Buf() for _ in range(2)]
    else:
        comp_o = p.dram_out("comp", [128, 8, 4])
    st_ = {"ki": 0, "pi": 0}

    def coeffs(ti, d, oc):
        blk, xb = oc // 2, oc % 2
        t0, n, st = TT[ti]
        kb = st_["ki"] % 2
        st_["ki"] += 1
        q = st_["pi"] % 3
        st_["pi"] += 1
        pr, pq = ps[q * 2], ps[q * 2 + 1]
        Bpr, Bpq = Bps[q * 2], Bps[q * 2 + 1]
        for gi_, pp, Bpp in ((0, pr, Bpr), (1, pq, Bpq)):
            for kk in range(2):
                p.I("pe", "matmul", pp[:, :n], gw[:, d, gi_, blk, kk, (oc % 2) * 128:(oc % 2 + 1) * 128],
                    xcb[:, 2 * blk + kk, t0:t0 + n], start=(kk == 0), stop=(kk == 1),
                    reads=[Bgw, Bxcb[2 * blk + kk]], writes=[Bpp])
        r_, i_, a_, s_, b_ = (T[k][kb][:, :n] for k in ("r", "i", "a", "s", "b"))
        p.I("act", "activation", r_, pr[:, :n], AF.Sigmoid, bias=gbs[:, 0, d, oc:oc + 1], reads=[Bpr, Bgb], writes=[BT["r"][kb]])
        p.I("act", "activation", i_, pq[:, :n], AF.Sigmoid, bias=gbs[:, 1, d, oc:oc + 1], reads=[Bpq, Bgb], writes=[BT["i"][kb]])
        p.I("act", "activation", a_, r_, AF.Exp, scale=cl[:, 0, d, oc:oc + 1], reads=[BT["r"][kb], Bcl], writes=[BT["a"][kb]])
        p.I("act", "activation", s_, r_, AF.Exp, scale=cl[:, 1, d, oc:oc + 1], reads=[BT["r"][kb], Bcl], writes=[BT["s"][kb]])
        p.I("act", "activation", s_, s_, AF.Sqrt, bias=1.0, scale=-1.0, reads=[BT["s"][kb]], writes=[BT["s"][kb]])
        p.I("dve", "tensor_tensor", b_, i_, xcf[xb][:, t0:t0 + n], ALU.mult, reads=[BT["i"][kb], Bxcf[xb]], writes=[BT["b"][kb]])
        p.I("dve", "tensor_tensor", b_, b_, s_, ALU.mult, reads=[BT["b"][kb], BT["s"][kb]], writes=[BT["b"][kb]])
        return kb

    def scan(ti, kb, init, Binit, d):
        t0, n, st = TT[ti]
        o = hbuf[d][:, t0:t0 + n]
        a_, b_ = T["a"][kb][:, :n], T["b"][kb][:, :n]
        if d == 1:
            o, a_, b_ = o[:, ::-1], a_[:, ::-1], b_[:, ::-1]
        p.I("dve", "tensor_tensor_scan", o, a_, b_, init, ALU.mult, ALU.add,
            reads=[BT["a"][kb], BT["b"][kb]] + Binit, writes=[Bh[d][ti]])

    def endcol(ti, d):
        t0, n, st = TT[ti]
        c = t0 + n - 1 if d == 0 else t0
        return hbuf[d][:, c:c + 1]

    for oc in range(8):
        xb = oc % 2
        p.dma("sp", xcf[xb][:], xcv[:, oc, :], writes=[Bxcf[xb]])
        if full:
            p.dma("sp", gch[xb][:], fm(G_d)[:, oc, :], writes=[Bgch[xb]])
        for d in range(2):
            order = [0, 1, 2, 3] if d == 0 else [3, 2, 1, 0]
            if full:
                kb = coeffs(4, d, oc)
                scan(4, kb, 0.0, [], d)
                e0 = endcol(4, d)
                p.I("dve", "tensor_copy", ext[:, 0:1], e0, reads=[Bh[d][4]], writes=[Bext])
                Aall, Ball = ca[:, oc, :, 2 * d], ca[:, oc, :, 2 * d + 1]
                if d == 1:
                    Aall, Ball = Aall[:, ::-1], Ball[:, ::-1]
                p.I("dve", "tensor_tensor_scan", ext[:, 1:9], Aall, Ball, e0, ALU.mult, ALU.add,
                    reads=[Bca, Bh[d][4], Bext], writes=[Bext])
                p.I("dve", "tensor_tensor", ext2[:], ext[:], oh[:, d, :], ALU.mult, reads=[Bext, Bca], writes=[Bext])
                p.I("dve", "reduce_sum", car[:], ext2[:], AX.X, reads=[Bext], writes=[Bext])
                init, Binit = car[:, 0:1], [Bext]
            else:
                init, Binit = 0.0, []
            prevA, BprevA = 1.0, []
            for ti in order:
                kb = coeffs(ti, d, oc)
                scan(ti, kb, init, Binit, d)
                init, Binit = endcol(ti, d), [Bh[d][ti]]
                if not full:
                    n = TT[ti][1]
                    p.I("dve", "tensor_tensor_scan", junk[:, :n], T["a"][kb][:, :n], zeros[:, :n], prevA, ALU.mult, ALU.add,
                        reads=[BT["a"][kb], Bz, Bjunk] + BprevA, writes=[Bjunk])
                    p.I("dve", "tensor_copy", comp[:, oc, 2 * d:2 * d + 1], junk[:, n - 1:n], reads=[Bjunk], writes=[Bcomp])
                    prevA, BprevA = comp[:, oc, 2 * d:2 * d + 1], [Bcomp]
            if not full:
                p.I("dve", "tensor_copy", comp[:, oc, 2 * d + 1:2 * d + 2], init, reads=Binit, writes=[Bcomp])
        if full:
            for ti, (t0, n, st) in enumerate(TT):
                p.I("dve", "tensor_tensor", hbuf[0][:, t0:t0 + n], hbuf[0][:, t0:t0 + n], hbuf[1][:, t0:t0 + n], ALU.add,
                    reads=[Bh[0][ti], Bh[1][ti]], writes=[Bh[0][ti]])
                p.I("dve", "tensor_tensor", yin[:, oc, t0:t0 + n], hbuf[0][:, t0:t0 + n], gch[xb][:, t0:t0 + n], ALU.mult,
                    reads=[Bh[0][ti], Bgch[xb]], writes=[Byin[oc][ti]])
    Bout = []
    if full:
        xt = [p.sbuf([128, 512], F32, "xt%d" % i) for i in range(3)]; Bxt = [Buf() for _ in range(3)]
        xi = 0
        for ti, (t0, n, st) in enumerate(TT):
            for oc in range(8):
                b = xi % 3
                bk = 6 + xi % 2
                xi += 1
                p.dma("sp", xt[b][:, :n], fm(xT)[:, oc, t0:t0 + n], writes=[Bxt[b]])
                for k in range(8):
                    p.op("pe", lambda e, k=k, oc=oc, bk=bk, t0=t0, n=n: e.matmul(
                        ps[bk][:, :n], wo[:, k, oc * 128:(oc + 1) * 128], yin[:, k, t0:t0 + n], start=(k == 0), stop=(k == 7)),
                        reads=[Bwo[k], Byin[k][ti]], writes=[Bps[bk]])
                p.op("dve", lambda e, b=b, bk=bk, oc=oc, n=n, st=st: e.scalar_tensor_tensor(
                    xt[b][:, :n], ps[bk][:, :n], cm.mod[:, 16 + oc, st:st + 1], xt[b][:, :n], ALU.mult, ALU.add),
                    reads=[Bps[bk], cm.Bmod, Bxt[b]], writes=[Bxt[b]])
                b_ = Buf()
                p.dma("sp", fm(xo)[:, oc, t0:t0 + n], xt[b][:, :n], reads=[Bxt[b]], writes=[b_])
                Bout.append(b_)
    else:
        b_ = Buf()
        p.dma("sp", comp_o, comp[:], reads=[Bcomp], writes=[b_])
        Bout.append(b_)
    p.wait_all("sp", Bout)
    return p.finalize()
def build_naa():
    p = Prog()
    xT = p.dram_in("xT", [1024, NT])
    g_d = p.dram_in("g", [128, 8])
    w_d = p.dram_in("w_qkv", [1024, 3072])
    qkg_d = p.dram_in("qkg", [128, 2])
    QT = p.dram_out("QT", [1024, NT], BF16)
    KT = p.dram_out("KT", [1024, NT], BF16)
    V = p.dram_out("V", [NT, 1024], BF16)
    cm = Common(p)
    s1, Bs1 = cm.scale_vec(g_d, 1, "s1")
    qkg = p.sbuf([128, 2], F32, "qkg"); Bqkg = Buf()
    p.dma("sp", qkg[:], qkg_d, writes=[Bqkg])
    p.I("dve", "tensor_scalar", qkg[:, 0:1], qkg[:, 0:1], 0.125, None, ALU.mult, reads=[Bqkg], writes=[Bqkg])
    bones = p.sbuf([128, 128], BF16, "bones"); Bbo = Buf()
    p.I("dve", "memset", bones[:], 0.0, writes=[Bbo])
    p.I("dve", "memset", bones[0:64, 0:64], 1.0, writes=[Bbo])
    p.I("dve", "memset", bones[64:128, 64:128], 1.0, writes=[Bbo])
    ps = [p.psum([128, 512], F32, "bank%d" % i) for i in range(8)]
    Bps = [Buf() for _ in range(8)]
    hb = p.sbuf([128, 8, NT], BF16, "hb")
    Bhb = [[Buf() for _ in TT] for _ in range(8)]
    wsb = p.sbuf([128, 8, 3072], BF16, "wqkv")
    Bw = [[Buf() for _ in range(3)] for _ in range(8)]
    cst = Caster(p, 3, 1024)
    wv = w_d.rearrange("(k p) m -> p k m", p=128)
    xt = [p.sbuf([128, 8, 512], F32, "xt%d" % i) for i in range(2)]
    Bxt = [[Buf() for _ in range(8)] for _ in range(2)]
    nm = NormMod(p, cm, s1, Bs1, 0, ps[7], Bps[7])
    xv = fm(xT)
    for ti, (t0, n, st) in enumerate(TT):
        b = ti % 2
        for c in range(8):
            p.dma("sp", xt[b][:, c, :n], xv[:, c, t0:t0 + n], writes=[Bxt[b][c]])
        nm.run([xt[b][:, c, :n] for c in range(8)], Bxt[b], n, st,
               [[hb[:, c, t0:t0 + n] for c in range(8)]], [[Bhb[c][ti] for c in range(8)]])
        if ti == 0:
            for k in range(8):
                for hh in range(3):
                    cst.load(wsb[:, k, hh * 1024:(hh + 1) * 1024], wv[:, k, hh * 1024:(hh + 1) * 1024], Bw[k][hh])
    sq = [p.sbuf([128, 512], BF16, "sq%d" % i) for i in range(2)]; Bsq = [Buf() for _ in range(2)]
    rs = [p.sbuf([128, 512], F32, "rs%d" % i) for i in range(2)]; Brs = [Buf() for _ in range(2)]
    ob = [p.sbuf([128, 512], BF16, "ob%d" % i) for i in range(3)]; Bob = [Buf() for _ in range(3)]
    Bout = []
    i2 = 0
    i3 = 0
    for oc in range(16):
        which = oc // 8
        dst = fm(QT if which == 0 else KT)
        for ti, (t0, n, st) in enumerate(TT):
            a = i2 % 2
            i2 += 1
            pq, pm = ps[a * 2], ps[a * 2 + 1]
            Bpq, Bpm = Bps[a * 2], Bps[a * 2 + 1]
            for k in range(8):
                p.I("pe", "matmul", pq[:, :n], wsb[:, k, oc * 128:(oc + 1) * 128], hb[:, k, t0:t0 + n], start=(k == 0), stop=(k == 7),
                    reads=[Bw[k][which], Bhb[k][ti]], writes=[Bpq])
            p.I("act", "activation", sq[a][:, :n], pq[:, :n], AF.Square, reads=[Bpq], writes=[Bsq[a]])
            p.I("pe", "matmul", pm[:, :n], bones[:], sq[a][:, :n], start=True, stop=True, reads=[Bbo, Bsq[a]], writes=[Bpm])
            p.I("act", "activation", rs[a][:, :n], pm[:, :n], AF.Sqrt, bias=RMS_EPS, scale=1.0 / 64.0, reads=[Bpm], writes=[Brs[a]])
            p.I("dve", "reciprocal", rs[a][:, :n], rs[a][:, :n], reads=[Brs[a]], writes=[Brs[a]])
            o = i3 % 3
            i3 += 1
            p.I("dve", "scalar_tensor_tensor", ob[o][:, :n], pq[:, :n], qkg[:, which:which + 1], rs[a][:, :n], ALU.mult, ALU.mult,
                reads=[Bpq, Bqkg, Brs[a]], writes=[Bob[o]])
            b_ = Buf()
            p.dma("sp", dst[:, oc % 8, t0:t0 + n], ob[o][:, :n], reads=[Bob[o]], writes=[b_])
            Bout.append(b_)
    vb = [p.sbuf([128, 512], BF16, "vb%d" % i) for i in range(3)]; Bvb = [Buf() for _ in range(3)]
    i4 = 0
    for sub in range(NT // 128):
        ti = min(sub // 4, 4)
        for half in range(2):
            a = 4 + i4 % 2
            o = i4 % 3
            i4 += 1
            for k in range(8):
                p.I("pe", "matmul", ps[a][:, :], hb[:, k, sub * 128:(sub + 1) * 128], wsb[:, k, 2048 + half * 512:2048 + (half + 1) * 512],
                    start=(k == 0), stop=(k == 7), reads=[Bw[k][2], Bhb[k][ti]], writes=[Bps[a]])
            p.I("act", "activation", vb[o][:], ps[a][:], AF.Identity, reads=[Bps[a]], writes=[Bvb[o]])
            b_ = Buf()
            p.dma("sp", V[sub * 128:(sub + 1) * 128, half * 512:(half + 1) * 512], vb[o][:], reads=[Bvb[o]], writes=[b_])
            Bout.append(b_)
    p.wait_all("sp", Bout)
    return p.finalize()


NSLOT = 20
NKT = NSLOT + 2
KW = NKT * 128


def build_nab():
    p = Prog()
    QT = p.dram_in("QT", [1024, NT], BF16)
    KT = p.dram_in("KText", [1024, KW], BF16)
    Vp = p.dram_in("Vp", [8, 128, NKT, 128], BF16)
    tab = p.dram_in("tab", [16, 128, 25, 128])
    wo_d = p.dram_in("w_o", [1024, 1024])
    xT = p.dram_in("xT", [1024, NT])
    xo = p.dram_out("xo", [1024, NT])
    cm = Common(p)
    ones = cm.ones
    cst = Caster(p, 3, 1024)
    wo = p.sbuf([128, 8, 1024], BF16, "wo"); Bwo = [Buf() for _ in range(8)]
    for k in range(8):
        cst.load(wo[:, k, :], wo_d.rearrange("(k p) m -> p k m", p=128)[:, k, :], Bwo[k])
    oT = p.sbuf([128, 8, NT], BF16, "oT"); BoT = [[Buf() for _ in range(18)] for _ in range(8)]
    qs = [p.sbuf([128, NT], BF16, "qs%d" % i) for i in range(2)]; Bqs = [Buf() for _ in range(2)]
    ks = [p.sbuf([128, KW], BF16, "ks%d" % i) for i in range(2)]; Bks = [Buf() for _ in range(2)]
    vs = [p.sbuf([128, NKT, 128], BF16, "vs%d" % i) for i in range(2)]; Bvs = [Buf() for _ in range(2)]
    tf = [p.sbuf([128, 25 * 128], F32, "tf%d" % i) for i in range(1)]; Btf = [Buf() for _ in range(1)]
    Eh = [p.sbuf([128, 25 * 128], BF16, "Eh%d" % i) for i in range(2)]; BEh = [Buf() for _ in range(2)]
    pe_ = [p.sbuf([128, 640], F32, "pexp%d" % i) for i in range(2)]; Bpe = [Buf() for _ in range(2)]
    pb = [p.sbuf([128, 896], BF16, "pbf%d" % i) for i in range(2)]; Bpb = [Buf() for _ in range(2)]
    rd = [p.sbuf([128, 128], F32, "rd%d" % i) for i in range(2)]; Brd = [Buf() for _ in range(2)]
    S = [p.psum([128, 1024], F32, "S%d" % i) for i in range(2)]; BS = [Buf() for _ in range(2)]
    O = [p.psum([128, 512], F32, "O%d" % i) for i in range(2)]; BO = [Buf() for _ in range(2)]
    Y = [p.psum([128, 512], F32, "Y%d" % i) for i in range(2)]; BY = [Buf() for _ in range(2)]
    it = 0
    for hc in range(8):
        b = hc % 2
        p.dma("sp", qs[b][:], fm(QT)[:, hc, :], writes=[Bqs[b]])
        p.dma("sp", ks[b][:], fm(KT)[:, hc, :], writes=[Bks[b]])
        p.dma("sp", vs[b][:], Vp[hc], writes=[Bvs[b]])
        for hh in range(2):
            h = 2 * hc + hh
            hp = 64 * hh
            eb = h % 2
            p.dma("sp", tf[0][:], tab[h].rearrange("p a b -> p (a b)"), writes=[Btf[0]])
            p.I("act", "activation", Eh[eb][:], tf[0][:], AF.Exp, reads=[Btf[0]], writes=[BEh[eb]])
            for blk in range(18):
                a = it % 2
                it += 1
                q0 = blk * 128
                if blk < 16:
                    st_ = 0 if blk == 0 else (1 if blk == 1 else (3 if blk == 14 else (4 if blk == 15 else 2)))
                    tiles = [blk + d for d in range(5)] + [NSLOT, NSLOT + 1]
                else:
                    tiles = [NSLOT, NSLOT + 1]
                nt = len(tiles)
                for j, tl in enumerate(tiles):
                    p.I("pe", "matmul", S[a][:, j * 128:(j + 1) * 128], ks[b][hp:hp + 64, tl * 128:(tl + 1) * 128],
                        qs[b][hp:hp + 64, q0:q0 + 128], start=True, stop=True, reads=[Bks[b], Bqs[b]], writes=[BS[a]])
                if blk < 16:
                    p.I("act", "activation", pe_[a][:], S[a][:, 0:640], AF.Exp, reads=[BS[a]], writes=[Bpe[a]])
                    p.I("act", "activation", pb[a][:, 640:896], S[a][:, 640:896], AF.Exp, reads=[BS[a]], writes=[Bpb[a]])
                    p.I("dve", "tensor_tensor", pb[a][:, 0:640], pe_[a][:], Eh[eb][:, st_ * 640:(st_ + 1) * 640], ALU.mult,
                        reads=[Bpe[a], BEh[eb]], writes=[Bpb[a]])
                else:
                    p.I("act", "activation", pb[a][:, 0:256], S[a][:, 0:256], AF.Exp, reads=[BS[a]], writes=[Bpb[a]])
                for j, tl in enumerate(tiles):
                    p.I("pe", "matmul", O[a][:, 0:128], vs[b][:, tl, :], pb[a][:, j * 128:(j + 1) * 128], start=(j == 0), stop=(j == nt - 1),
                        reads=[Bvs[b], Bpb[a]], writes=[BO[a]])
                for j, tl in enumerate(tiles):
                    p.I("pe", "matmul", O[a][:, 128:256], ones[:], pb[a][:, j * 128:(j + 1) * 128], start=(j == 0), stop=(j == nt - 1),
                        reads=[cm.Bones, Bpb[a]], writes=[BO[a]])
                p.I("dve", "reciprocal", rd[a][hp:hp + 64, :], O[a][hp:hp + 64, 128:256], reads=[BO[a]], writes=[Brd[a]])
                p.I("dve", "tensor_tensor", oT[hp:hp + 64, hc, q0:q0 + 128], O[a][hp:hp + 64, 0:128], rd[a][hp:hp + 64, :], ALU.mult,
                    reads=[BO[a], Brd[a]], writes=[BoT[hc][blk]])
    xt = [p.sbuf([128, 512], F32, "xt%d" % i) for i in range(3)]; Bxt = [Buf() for _ in range(3)]
    Bout = []
    xi = 0
    for ti, (t0, n, st) in enumerate(TT):
        for oc in range(8):
            b = xi % 3
            a = xi % 2
            xi += 1
            p.dma("sp", xt[b][:, :n], fm(xT)[:, oc, t0:t0 + n], writes=[Bxt[b]])
            for k in range(8):
                p.I("pe", "matmul", Y[a][:, :n], wo[:, k, oc * 128:(oc + 1) * 128], oT[:, k, t0:t0 + n], start=(k == 0), stop=(k == 7),
                    reads=[Bwo[k]] + BoT[k][t0 // 128:(t0 + n) // 128], writes=[BY[a]])
            p.I("dve", "scalar_tensor_tensor", xt[b][:, :n], Y[a][:, :n], cm.mod[:, 16 + oc, st:st + 1], xt[b][:, :n], ALU.mult, ALU.add,
                reads=[BY[a], cm.Bmod, Bxt[b]], writes=[Bxt[b]])
            b_ = Buf()
            p.dma("sp", fm(xo)[:, oc, t0:t0 + n], xt[b][:, :n], reads=[Bxt[b]], writes=[b_])
            Bout.append(b_)
    p.wait_all("sp", Bout)
    return p.finalize()
def build_fta():
    p = Prog()
    xT = p.dram_in("xT", [1024, NT])
    g_d = p.dram_in("g", [128, 8])
    cw_d = p.dram_in("cwsw", [2, 256, 256])
    ZR = p.dram_out("ZR", [1024, NT])
    ZI = p.dram_out("ZI", [1024, NT])
    cm = Common(p)
    s1, Bs1 = cm.scale_vec(g_d, 1, "s1")
    tb = p.sbuf([128, 2, 2, 256], F32, "tb"); Btb = Buf()
    for i in range(2):
        p.dma("sp", tb[:, i], cw_d[i].rearrange("(kk p) m -> p kk m", p=128), writes=[Btb])
    ps = [p.psum([128, 512], F32, "bank%d" % i) for i in range(8)]
    Bps = [Buf() for _ in range(8)]
    xt = [p.sbuf([128, 8, 512], F32, "xt%d" % i) for i in range(2)]
    Bxt = [[Buf() for _ in range(8)] for _ in range(2)]
    h32 = [p.sbuf([128, 8, 512], F32, "h32_%d" % i) for i in range(2)]
    Bh = [[Buf() for _ in range(8)] for _ in range(2)]
    ob = [p.sbuf([128, 512], F32, "ob%d" % i) for i in range(3)]; Bob = [Buf() for _ in range(3)]
    nm = NormMod(p, cm, s1, Bs1, 0, ps[7], Bps[7])
    xv = fm(xT)
    Bout = []
    i2 = 0
    for ti, (t0, n, st) in enumerate(TT):
        b = ti % 2
        for c in range(8):
            p.dma("sp", xt[b][:, c, :n], xv[:, c, t0:t0 + n], writes=[Bxt[b][c]])
        nm.run([xt[b][:, c, :n] for c in range(8)], Bxt[b], n, st, [[h32[b][:, c, :n] for c in range(8)]], [Bh[b]])
        for oc in range(8):
            gI = oc // 2
            for ri, dst in enumerate((ZR, ZI)):
                a = i2 % 6
                o = i2 % 3
                i2 += 1
                for kk in range(2):
                    p.I("pe", "matmul", ps[a][:, :n], tb[:, ri, kk, (oc % 2) * 128:(oc % 2 + 1) * 128], h32[b][:, 2 * gI + kk, :n],
                        start=(kk == 0), stop=(kk == 1), reads=[Btb, Bh[b][2 * gI + kk]], writes=[Bps[a]])
                p.I("act", "activation", ob[o][:, :n], ps[a][:, :n], AF.Identity, reads=[Bps[a]], writes=[Bob[o]])
                b_ = Buf()
                p.dma("sp", fm(dst)[:, oc, t0:t0 + n], ob[o][:, :n], reads=[Bob[o]], writes=[b_])
                Bout.append(b_)
    p.wait_all("sp", Bout)
    return p.finalize()


def build_ftb():
    p = Prog()
    Zr_d = p.dram_in("Zr", [128, 128, 128])
    Zi_d = p.dram_in("Zi", [128, 128, 128])
    Zc_d = p.dram_in("Zc", [2, 256, 128])
    R_d = p.dram_in("R12", [2, 128, 256])
    CS_d = p.dram_in("CS1", [2, 128, 128])
    TW_d = p.dram_in("TW", [2, 128, 128])
    F256_d = p.dram_in("F256", [2, 256, 256])
    F_o = p.dram_out("F", [128, 128, 128])
    Fc_o = p.dram_out("Fc", [256, 128])
    R = p.sbuf([128, 2, 256], F32, "R"); CS = p.sbuf([128, 2, 128], F32, "CS"); TW = p.sbuf([128, 2, 128], F32, "TW")
    Bt = Buf()
    for i in range(2):
        p.dma("sp", R[:, i], R_d[i], writes=[Bt]); p.dma("sp", CS[:, i], CS_d[i], writes=[Bt]); p.dma("sp", TW[:, i], TW_d[i], writes=[Bt])
    F2 = p.sbuf([128, 2, 2, 256], F32, "F2")
    for i in range(2):
        p.dma("sp", F2[:, i], F256_d[i].rearrange("(tc p) k -> p tc k", p=128), writes=[Bt])
    zc = p.sbuf([128, 2, 2, 128], F32, "zc")
    for i in range(2):
        p.dma("sp", zc[:, i], Zc_d[i].rearrange("(tc p) c -> p tc c", p=128), writes=[Bt])
    CG = 32
    zr = [p.sbuf([128, CG, 128], F32, "zr%d" % i) for i in range(2)]; zi = [p.sbuf([128, CG, 128], F32, "zi%d" % i) for i in range(2)]
    Bz = [Buf() for _ in range(2)]
    ar = [p.sbuf([128, CG, 128], F32, "ar%d" % i) for i in range(2)]; ai = [p.sbuf([128, CG, 128], F32, "ai%d" % i) for i in range(2)]
    Ba = [[Buf() for _ in range(CG // 4)] for _ in range(2)]
    tmp = [p.sbuf([128, 128], F32, "tw_t%d" % i) for i in range(4)]; Btmp = [Buf() for _ in range(4)]
    ob = [p.sbuf([128, 512], F32, "ob%d" % i) for i in range(3)]; Bob = [Buf() for _ in range(3)]
    ps = [p.psum([128, 512], F32, "bank%d" % i) for i in range(8)]
    Bps = [Buf() for _ in range(8)]
    Bout = []
    i1 = 0
    i3 = 0
    for gi in range(128 // CG):
        b = gi % 2
        c0 = gi * CG
        p.dma("sp", zr[b][:], Zr_d[:, c0:c0 + CG, :], writes=[Bz[b]])
        p.dma("sp", zi[b][:], Zi_d[:, c0:c0 + CG, :], writes=[Bz[b]])
        for c in range(CG):
            a = i1 % 4
            i1 += 1
            A = ps[a][:, 0:256]
            p.I("pe", "matmul", A, zr[b][:, c, :], R[:, 0, :], start=True, stop=False, reads=[Bz[b], Bt], writes=[Bps[a]])
            p.I("pe", "matmul", A, zi[b][:, c, :], R[:, 1, :], start=False, stop=True, reads=[Bz[b], Bt], writes=[Bps[a]])
            Ar, Ai = ps[a][:, 0:128], ps[a][:, 128:256]
            Tc, Ts = TW[:, 0, :], TW[:, 1, :]
            Bq = Ba[b][c // 4]
            p.I("dve", "tensor_tensor", tmp[0][:], Ar, Tc, ALU.mult, reads=[Bps[a], Bt], writes=[Btmp[0]])
            p.I("dve", "tensor_tensor", tmp[1][:], Ai, Ts, ALU.mult, reads=[Bps[a], Bt], writes=[Btmp[1]])
            p.I("pool", "tensor_tensor", ar[b][:, c, :], tmp[0][:], tmp[1][:], ALU.add, reads=[Btmp[0], Btmp[1]], writes=[Bq])
            p.I("dve", "tensor_tensor", tmp[2][:], Ai, Tc, ALU.mult, reads=[Bps[a], Bt], writes=[Btmp[2]])
            p.I("dve", "tensor_tensor", tmp[3][:], Ar, Ts, ALU.mult, reads=[Bps[a], Bt], writes=[Btmp[3]])
            p.I("pool", "tensor_tensor", ai[b][:, c, :], tmp[2][:], tmp[3][:], ALU.subtract, reads=[Btmp[2], Btmp[3]], writes=[Bq])
        for q in range(CG // 4):
            a = 4 + i3 % 3
            o = i3 % 3
            i3 += 1
            rr = ar[b][:, 4 * q:4 * q + 4, :].rearrange("p c k -> p (c k)")
            ii = ai[b][:, 4 * q:4 * q + 4, :].rearrange("p c k -> p (c k)")
            p.I("pe", "matmul", ps[a][:], CS[:, 0, :], rr, start=True, stop=False, reads=[Bt, Ba[b][q]], writes=[Bps[a]])
            p.I("pe", "matmul", ps[a][:], CS[:, 1, :], ii, start=False, stop=True, reads=[Bt, Ba[b][q]], writes=[Bps[a]])
            p.I("act", "activation", ob[o][:], ps[a][:], AF.Identity, reads=[Bps[a]], writes=[Bob[o]])
            b_ = Buf()
            cc = c0 + 4 * q
            p.dma("sp", F_o[:, cc:cc + 4, :], ob[o][:].rearrange("p (c k) -> p c k", c=4), reads=[Bob[o]], writes=[b_])
            Bout.append(b_)
    for kc in range(2):
        a = 7
        first = True
        for tc in range(2):
            for ri in range(2):
                p.I("pe", "matmul", ps[a][:, 0:128], F2[:, ri, tc, kc * 128:(kc + 1) * 128], zc[:, ri, tc, :],
                    start=first, stop=(tc == 1 and ri == 1), reads=[Bt], writes=[Bps[a]])
                first = False
        o = i3 % 3
        i3 += 1
        p.I("act", "activation", ob[o][:, 0:128], ps[a][:, 0:128], AF.Identity, reads=[Bps[a]], writes=[Bob[o]])
        b_ = Buf()
        p.dma("sp", Fc_o[kc * 128:(kc + 1) * 128, :], ob[o][:, 0:128], reads=[Bob[o]], writes=[b_])
        Bout.append(b_)
    p.wait_all("sp", Bout)
    return p.finalize()


def build_proj():
    p = Prog()
    fT = p.dram_in("fT", [1024, NT])
    w_d = p.dram_in("w", [1024, 1024])
    xT = p.dram_in("xT", [1024, NT])
    xo = p.dram_out("xo", [1024, NT])
    cm = Common(p)
    cst = Caster(p, 3, 1152)
    wo = p.sbuf([128, 8, 1024], BF16, "wo"); Bwo = [Buf() for _ in range(8)]
    for k in range(8):
        cst.load(wo[:, k, :], w_d.rearrange("(k p) m -> p k m", p=128)[:, k, :], Bwo[k])
    fb = p.sbuf([128, 8, NT], BF16, "fb"); Bfb = [Buf() for _ in range(8)]
    for c in range(8):
        for hh in range(2):
            cst.load(fb[:, c, hh * 1152:(hh + 1) * 1152], fm(fT)[:, c, hh * 1152:(hh + 1) * 1152], Bfb[c], eng="act" if c % 2 else "pool")
    Y = [p.psum([128, 512], F32, "Y%d" % i) for i in range(2)]; BY = [Buf() for _ in range(2)]
    xt = [p.sbuf([128, 512], F32, "xt%d" % i) for i in range(3)]; Bxt = [Buf() for _ in range(3)]
    Bout = []
    xi = 0
    for ti, (t0, n, st) in enumerate(TT):
        for oc in range(8):
            b = xi % 3
            a = xi % 2
            xi += 1
            p.dma("sp", xt[b][:, :n], fm(xT)[:, oc, t0:t0 + n], writes=[Bxt[b]])
            for k in range(8):
                p.I("pe", "matmul", Y[a][:, :n], wo[:, k, oc * 128:(oc + 1) * 128], fb[:, k, t0:t0 + n], start=(k == 0), stop=(k == 7),
                    reads=[Bwo[k], Bfb[k]], writes=[BY[a]])
            p.I("dve", "scalar_tensor_tensor", xt[b][:, :n], Y[a][:, :n], cm.mod[:, 16 + oc, st:st + 1], xt[b][:, :n], ALU.mult, ALU.add,
                reads=[BY[a], cm.Bmod, Bxt[b]], writes=[Bxt[b]])
            b_ = Buf()
            p.dma("sp", fm(xo)[:, oc, t0:t0 + n], xt[b][:, :n], reads=[Bxt[b]], writes=[b_])
            Bout.append(b_)
    p.wait_all("sp", Bout)
    return p.finalize()
NCORES = 8
_PROGS = {}
DEBUG = {}


def _prog(name, fn, *a):
    key = (name,) + a
    if key not in _PROGS:
        _PROGS[key] = fn(*a)
    return _PROGS[key]


def _run(nc, maps):
    if DEBUG.get("trace"):
        res = run_bass_kernel_spmd(nc, maps, core_ids=list(range(NCORES)), trace=True)
        print("EXEC_NS", res.exec_time_ns, flush=True)
        return res.results
    res = run_bass_kernel_spmd(nc, maps, core_ids=list(range(NCORES)))
    return res.results


def to_fm(v):
    return np.ascontiguousarray(np.asarray(v, np.float32).reshape(-1, 128).T)


def _split_x(xTs):
    lat = np.concatenate([r[:, :2048].T for r in xTs], 0)
    return np.ascontiguousarray(lat), np.ascontiguousarray(xTs[0][:, 2048:].T)


def _core_xT(lat, ctx):
    return [np.ascontiguousarray(np.concatenate([lat[2048 * k:2048 * (k + 1)], ctx], 0).T) for k in range(NCORES)]


def run_ada(inp):
    cv = np.stack([to_fm(inp["c"][0]), to_fm(inp["c_ctx"])], -1)
    maps = []
    for k in range(NCORES):
        w = np.ascontiguousarray(inp["ada_w"][:, :, 768 * k:768 * (k + 1)])
        b = np.ascontiguousarray(inp["ada_b"][:, 768 * k:768 * (k + 1)].reshape(4, 6, 128).transpose(0, 2, 1))
        maps.append({"w": w, "b": b, "cv": cv})
    res = _run(_prog("ada", build_ada), maps)
    return [np.ascontiguousarray(np.concatenate([r["o"][l] for r in res], 1)) for l in range(4)]


def run_ffn(xTs, mod, g, wgu, wdn, router=None):
    E = wgu.shape[0]
    maps = []
    for k in range(NCORES):
        m = {"xT": xTs[k], "mod": mod, "g": to_fm(g), "wgu": wgu, "wdn": wdn}
        if E > 1:
            sel = np.zeros((8, 8, 128), np.float32)
            for e in range(8):
                sel[e, e, :] = 1
            m.update({"router": router, "ident": np.eye(128, dtype=np.float32), "sel": sel})
        maps.append(m)
    res = _run(_prog("ffn", build_ffn, E), maps)
    return [r["xo"] for r in res]


def run_rg(xTs, mod, g, inp, j):
    lat, ctx = _split_x(xTs)
    cw = np.ascontiguousarray(inp["rg_conv_w"][j].reshape(4, 8, 128).transpose(2, 1, 0))
    maps = []
    for k in range(NCORES):
        xh = np.zeros((3, 1024), np.float32)
        hm = np.zeros((128, 3), np.float32)
        for i, t in enumerate((2048 * k - 2, 2048 * k - 1, 2048 * (k + 1))):
            if 0 <= t < 16384:
                xh[i] = lat[t]
                hm[:, i] = 1.0
        maps.append({"xT": xTs[k], "xh": np.ascontiguousarray(xh.T), "hmask": hm, "g": to_fm(g), "mod": mod,
                     "w_in": inp["rg_w_in"][j], "conv_w": cw, "conv_b": to_fm(inp["rg_conv_b"][j])})
    ra = _run(_prog("rga", build_rga), maps)
    gb = np.zeros((128, 3, 2, 8), np.float32)
    for i, nm_ in enumerate(("rg_ba", "rg_bi", "rg_lambda")):
        for d in range(2):
            gb[:, i, d, :] = to_fm(inp[nm_][j, d])
    maps = [{"XC": ra[k]["XC"], "wa": inp["rg_wa"][j], "wi": inp["rg_wi"][j], "gbias": gb} for k in range(NCORES)]
    rb = _run(_prog("rgs", build_rgs, False), maps)
    comp_all = np.ascontiguousarray(np.stack([r["comp"] for r in rb], 2))
    DEBUG["comp_all"] = comp_all
    maps2 = []
    for k in range(NCORES):
        oh = np.zeros((128, 2, 9), np.float32)
        oh[:, 0, k] = 1.0
        oh[:, 1, 7 - k] = 1.0
        m = dict(maps[k])
        m.update({"comp_all": comp_all, "onehot": oh, "G": ra[k]["G"], "w_out": inp["rg_w_out"][j], "xT": xTs[k], "mod": mod})
        maps2.append(m)
    rc = _run(_prog("rgs", build_rgs, True), maps2)
    return [r["xo"] for r in rc]
import ml_dtypes


def _na_slot_pairs(k):
    pairs = []
    for lp in range(NSLOT):
        P = 16 * k - 2 + lp
        pairs.append(P if 0 <= P < 128 else None)
    sub = {}
    if k == 0:
        pairs[1] = 3
        sub[1] = 0
    if k == NCORES - 1:
        pairs[18] = 124
        sub[18] = 15
    return pairs, sub


def _na_tables(rpb, k):
    pairs, sub = _na_slot_pairs(k)
    tab = np.full((16, 128, 25, 128), -30000.0, np.float32)
    ki = np.arange(128)
    qi = np.arange(128)
    for st_, jl in enumerate((0, 1, 5, 14, 15)):
        for d in range(5):
            lp = jl + d
            P = pairs[lp]
            if P is None or (lp in sub and sub[lp] != jl):
                continue
            kr = 2 * P + ki // 64
            kc = ki % 64
            r = 2 * (16 * k + jl) + qi // 64
            qc = qi % 64
            rs = np.clip(r - 4, 0, 248)
            cs = np.clip(qc - 8, 0, 48)
            ok = ((kr[:, None] >= rs[None, :]) & (kr[:, None] < rs[None, :] + 8) &
                  (kc[:, None] >= cs[None, :]) & (kc[:, None] < cs[None, :] + 16))
            ri = np.clip(kr[:, None] - r[None, :] + 7, 0, 14)
            ci = np.clip(kc[:, None] - qc[None, :] + 15, 0, 30)
            vals = rpb[:, ri, ci]
            tab[:, :, st_ * 5 + d, :] = np.where(ok[None], vals, np.float32(-30000.0))
    return tab


def run_na(xTs, mod, g, inp):
    qkg = np.zeros((128, 2), np.float32)
    qkg[:, 0] = np.tile(inp["na_q_g"][0], 2)
    qkg[:, 1] = np.tile(inp["na_k_g"][0], 2)
    sc = np.zeros((128, 2), np.float32)
    maps = [{"xT": xTs[k], "mod": mod, "g": to_fm(g), "w_qkv": inp["na_w_qkv"][0], "qkg": qkg, "qscale": sc} for k in range(NCORES)]
    for m in maps:
        m.pop("qscale")
    ra = _run(_prog("naa", build_naa), maps)
    Kfull = np.concatenate([r["KT"][:, :2048] for r in ra], 1)
    Kctx = ra[0]["KT"][:, 2048:]
    Vfull = np.concatenate([r["V"][:2048] for r in ra], 0)
    Vctx = ra[0]["V"][2048:]
    maps2 = []
    for k in range(NCORES):
        pairs, _ = _na_slot_pairs(k)
        kt = np.zeros((1024, KW), Kfull.dtype)
        vt = np.zeros((KW, 1024), Vfull.dtype)
        for lp, P in enumerate(pairs):
            if P is not None:
                kt[:, lp * 128:(lp + 1) * 128] = Kfull[:, P * 128:(P + 1) * 128]
                vt[lp * 128:(lp + 1) * 128] = Vfull[P * 128:(P + 1) * 128]
        kt[:, NSLOT * 128:] = Kctx
        vt[NSLOT * 128:] = Vctx
        vp = np.ascontiguousarray(vt.reshape(NKT, 128, 8, 128).transpose(2, 1, 0, 3))
        maps2.append({"QT": ra[k]["QT"], "KText": kt, "Vp": vp, "tab": _na_tables(inp["na_rpb"][0], k),
                      "w_o": inp["na_w_o"][0], "xT": xTs[k], "mod": mod})
    rb = _run(_prog("nab", build_nab), maps2)
    return [r["xo"] for r in rb]
def _dft_tables():
    k = np.arange(256)
    ang = 2 * np.pi * np.outer(k, k) / 256.0
    cwsw = np.stack([np.cos(ang) / 16.0, -np.sin(ang) / 16.0]).astype(np.float32)
    F256 = np.stack([np.cos(ang) / 16.0, np.sin(ang) / 16.0]).astype(np.float32)
    j = np.arange(128)
    a1 = 2 * np.pi * np.outer(j, j) / 128.0
    s = 1.0 / np.sqrt(128.0)
    C1, S1 = np.cos(a1) * s, np.sin(a1) * s
    R12 = np.stack([np.concatenate([C1, -S1], 1), np.concatenate([S1, C1], 1)]).astype(np.float32)
    CS1 = np.stack([C1, S1]).astype(np.float32)
    at = 2 * np.pi * np.outer(j, j) / 16384.0
    TW = np.stack([np.cos(at), np.sin(at)]).astype(np.float32)
    return cwsw, F256, R12, CS1, TW


def run_ft(xTs, mod, g, inp):
    cwsw, F256, R12, CS1, TW = _dft_tables()
    maps = [{"xT": xTs[k], "mod": mod, "g": to_fm(g), "cwsw": cwsw} for k in range(NCORES)]
    ra = _run(_prog("fta", build_fta), maps)
    Z = []
    Zc = []
    for nm_ in ("ZR", "ZI"):
        Z.append(np.concatenate([r[nm_][:, :2048].T for r in ra], 0))
        Zc.append(ra[0][nm_][:, 2048:].T)
    maps2 = []
    for k in range(NCORES):
        sl = slice(128 * k, 128 * (k + 1))
        m = {"R12": R12, "CS1": CS1, "TW": TW, "F256": F256,
             "Zc": np.ascontiguousarray(np.stack([Zc[0][:, sl], Zc[1][:, sl]]))}
        for nm_, z in (("Zr", Z[0]), ("Zi", Z[1])):
            m[nm_] = np.ascontiguousarray(z[:, sl].reshape(128, 128, 128).transpose(0, 2, 1))
        maps2.append(m)
    rb = _run(_prog("ftb", build_ftb), maps2)
    f = np.concatenate([r["F"].transpose(0, 2, 1).reshape(16384, 128) for r in rb], 1)
    fc = np.concatenate([r["Fc"] for r in rb], 1)
    DEBUG["f"] = f
    fTs = _core_xT(f, fc)
    maps3 = [{"fT": fTs[k], "w": inp["ft_w_out"][0], "xT": xTs[k], "mod": mod} for k in range(NCORES)]
    rc = _run(_prog("proj", build_proj), maps3)
    return [r["xo"] for r in rc]
def kernel(**inp):
    inp = {k: np.asarray(v) for k, v in inp.items()}
    mods = run_ada(inp)
    xTs = _core_xT(inp["x"][0], inp["ctx"][0])
    ng = inp["norm_g"]
    xTs = run_rg(xTs, mods[0], ng[0, 0], inp, 0)
    xTs = run_ffn(xTs, mods[0], ng[0, 1], inp["ffn_w_gu"][0:1], inp["ffn_w_down"][0:1])
    xTs = run_na(xTs, mods[1], ng[1, 0], inp)
    xTs = run_ffn(xTs, mods[1], ng[1, 1], inp["moe_w_gu"][0], inp["moe_w_down"][0], inp["moe_router"][0])
    xTs = run_ft(xTs, mods[2], ng[2, 0], inp)
    xTs = run_ffn(xTs, mods[2], ng[2, 1], inp["ffn_w_gu"][1:2], inp["ffn_w_down"][1:2])
    xTs = run_rg(xTs, mods[3], ng[3, 0], inp, 1)
    xTs = run_ffn(xTs, mods[3], ng[3, 1], inp["moe_w_gu"][1], inp["moe_w_down"][1], inp["moe_router"][1])
    lat, _ = _split_x(xTs)
    return np.ascontiguousarray(lat[None].astype(np.float32))
```

```python
import contextlib
import numpy as np
import concourse.bass as bass
import concourse.mybir as mybir
from concourse.bass_utils import run_bass_kernel_spmd

F32 = mybir.dt.float32
BF16 = mybir.dt.bfloat16
AF = mybir.ActivationFunctionType
ALU = mybir.AluOpType
AX = mybir.AxisListType


class Buf:
    __slots__ = ("w", "r", "name")

    def __init__(self, name=""):
        self.w = None
        self.r = []
        self.name = name


class Prog:
    NDSEM = 24

    def __init__(self):
        self.nc = bass.Bass("TRN2", target_bir_lowering=False)
        self.es = contextlib.ExitStack()
        self.recs = {k: [] for k in ("pe", "act", "dve", "pool", "sp")}
        self.cnt = {k: 0 for k in self.recs}
        self.waited = {k: {} for k in self.recs}
        self.sems = {}
        for i in range(self.NDSEM):
            self._sem("d%d" % i)
        self.ndma = 0
        self.dma_hist = []
        self.nbuf = 0

    EPOCH = 30000

    def _sem(self, key):
        if key not in self.sems:
            self.sems[key] = self.es.enter_context(self.nc.semaphore("s_" + key.replace("#", "_")))
        return self.sems[key]

    def sbuf(self, shape, dtype, name=None):
        self.nbuf += 1
        t = self.es.enter_context(self.nc.sbuf_tensor((name + "_sb") if name else ("sb%d" % self.nbuf), list(shape), dtype))
        return t

    def psum(self, shape, dtype=F32, name=None):
        self.nbuf += 1
        t = self.es.enter_context(self.nc.psum_tensor((name + "_ps") if name else ("ps%d" % self.nbuf), list(shape), dtype))
        return t

    def dram_in(self, name, shape, dtype=F32):
        return self.nc.dram_tensor(name, list(shape), dtype, kind="ExternalInput").ap()

    def dram_out(self, name, shape, dtype=F32):
        return self.nc.dram_tensor(name, list(shape), dtype, kind="ExternalOutput").ap()

    def _need(self, eng, ev, waits):
        if ev is None:
            return
        key, val = ev
        if self.waited[eng].get(key, 0) >= val:
            return
        if eng == "pe" and key.startswith("pe#"):
            return
        self.waited[eng][key] = val
        waits[key] = max(waits.get(key, 0), val)

    def _deps(self, eng, reads, writes):
        waits = {}
        for b in reads:
            self._need(eng, b.w, waits)
        for b in writes:
            self._need(eng, b.w, waits)
            for ev in b.r:
                self._need(eng, ev, waits)
        return waits

    def _commit(self, ev, reads, writes):
        for b in reads:
            b.r.append(ev)
        for b in writes:
            b.w = ev
            b.r = []

    def op(self, eng, fn, reads=(), writes=()):
        waits = self._deps(eng, reads, writes)
        c = self.cnt[eng]
        self.cnt[eng] += 1
        key = "%s#%d" % (eng, c // self.EPOCH)
        self._sem(key)
        ev = (key, c % self.EPOCH + 1)
        self.recs[eng].append((list(waits.items()), fn, (key, 1)))
        self._commit(ev, reads, writes)
        return ev

    def I(self, eng, name, *args, reads=(), writes=(), **kw):
        return self.op(eng, lambda e: getattr(e, name)(*args, **kw), reads, writes)

    def dma(self, q, out, in_, reads=(), writes=(), **kw):
        waits = self._deps(q, reads, writes)
        n = self.ndma
        self.ndma += 1
        key = "d%d" % (n % self.NDSEM)
        val = 16 * (n // self.NDSEM + 1)
        if n >= self.NDSEM:
            pk, pv = self.dma_hist[n - self.NDSEM]
            if self.waited[q].get(pk, 0) < pv:
                self.waited[q][pk] = pv
                waits[pk] = max(waits.get(pk, 0), pv)
        ev = (key, val)
        self.dma_hist.append(ev)

        def fn(e, out=out, in_=in_, kw=kw):
            return e.dma_start(out=out, in_=in_, **kw)
        self.recs[q].append((list(waits.items()), fn, (key, 16)))
        self._commit(ev, reads, writes)
        return ev

    def coll(self, kind, ins, outs, reads=(), writes=(), groups=None):
        q = "pool"
        waits = self._deps(q, reads, writes)
        n = self.ndma
        self.ndma += 1
        key = "d%d" % (n % self.NDSEM)
        val = 16 * (n // self.NDSEM + 1)
        if n >= self.NDSEM:
            pk, pv = self.dma_hist[n - self.NDSEM]
            if self.waited[q].get(pk, 0) < pv:
                self.waited[q][pk] = pv
                waits[pk] = max(waits.get(pk, 0), pv)
        ev = (key, val)
        self.dma_hist.append(ev)
        groups = groups or [list(range(8))]

        def fn(e):
            return e.collective_compute(kind, ALU.bypass, replica_groups=groups, ins=list(ins), outs=list(outs))
        self.recs[q].append((list(waits.items()), fn, (key, 16)))
        self._commit(ev, reads, writes)
        return ev

    def wait_all(self, eng, bufs):
        waits = {}
        for b in bufs:
            self._need(eng, b.w, waits)
        if waits:
            self.recs[eng].append((list(waits.items()), None, None))

    def finalize(self):
        nc = self.nc
        sems = self.sems
        recs = self.recs

        def replay(e, lst):
            for waits, fn, inc in lst:
                for key, val in waits:
                    e.wait_ge(sems[key], val)
                if fn is not None:
                    ins = fn(e)
                    ins.then_inc(sems[inc[0]], inc[1])

        with nc.Block() as block:
            @block.sync
            def _(e):
                replay(e, recs["sp"])

            @block.tensor
            def _(e):
                replay(e, recs["pe"])

            @block.scalar
            def _(e):
                replay(e, recs["act"])

            @block.vector
            def _(e):
                replay(e, recs["dve"])

            @block.gpsimd
            def _(e):
                replay(e, recs["pool"])
        self.es.close()
        return nc
NT = 2304
TT = [(0, 512, 0), (512, 512, 0), (1024, 512, 0), (1536, 512, 0), (2048, 256, 1)]
RMS_EPS = 1e-6


def fm(ap):
    return ap.rearrange("(c p) t -> p c t", p=128)


class Common:
    def __init__(self, p, need_ident=False):
        self.p = p
        self.mod_d = p.dram_in("mod", [128, 48, 2])
        self.mod = p.sbuf([128, 48, 2], F32, "mod_sb")
        self.Bmod = Buf()
        p.dma("sp", self.mod[:], self.mod_d, writes=[self.Bmod])
        self.ones = p.sbuf([128, 128], BF16, "ones_bf")
        self.Bones = Buf()
        p.op("dve", lambda e: e.memset(self.ones[:], 1.0), writes=[self.Bones])

    def scale_vec(self, g_d, j, name):
        p = self.p
        g = p.sbuf([128, 8], F32, name + "_g")
        Bg = Buf()
        p.dma("sp", g[:], g_d, writes=[Bg])
        s = p.sbuf([128, 8, 2], F32, name + "_s")
        Bs = Buf()
        p.op("dve", lambda e: e.tensor_scalar(s[:], self.mod[:, j * 8:(j + 1) * 8, :], 1.0, None, ALU.add),
             reads=[self.Bmod], writes=[Bs])
        for st in range(2):
            p.op("dve", lambda e, st=st: e.tensor_tensor(s[:, :, st], s[:, :, st], g[:], ALU.mult),
                 reads=[Bg, Bs], writes=[Bs])
        return s, Bs


class NormMod:
    def __init__(self, p, cm, s, Bs, jshift, ps_bank, Bps, width=512):
        self.p, self.cm, self.s, self.Bs, self.jshift = p, cm, s, Bs, jshift
        self.ps, self.Bps = ps_bank, Bps
        self.sq = [p.sbuf([128, width], BF16, "nm_sq%d" % i) for i in range(2)]
        self.Bsq = [Buf() for _ in range(2)]
        self.rstd = [p.sbuf([128, width], F32, "nm_rstd%d" % i) for i in range(2)]
        self.Brstd = [Buf() for _ in range(2)]
        self.tmp = [p.sbuf([128, width], F32, "nm_tmp%d" % i) for i in range(2)]
        self.Btmp = [Buf() for _ in range(2)]
        self.k = 0

    def run(self, xs, Bxs, n, stream, outs, Bouts):
        p, cm = self.p, self.cm
        ps = self.ps
        for c in range(8):
            i = self.k % 2
            self.k += 1
            p.op("act", lambda e, i=i, c=c: e.activation(self.sq[i][:, :n], xs[c], AF.Square),
                 reads=[Bxs[c]], writes=[self.Bsq[i]])
            p.op("pe", lambda e, i=i, c=c: e.matmul(ps[:, :n], cm.ones[:], self.sq[i][:, :n], start=(c == 0), stop=(c == 7)),
                 reads=[cm.Bones, self.Bsq[i]], writes=[self.Bps])
        r = self.k % 2
        p.op("act", lambda e: e.activation(self.rstd[r][:, :n], ps[:, :n], AF.Sqrt, bias=RMS_EPS, scale=1.0 / 1024.0),
             reads=[self.Bps], writes=[self.Brstd[r]])
        p.op("dve", lambda e: e.reciprocal(self.rstd[r][:, :n], self.rstd[r][:, :n]),
             reads=[self.Brstd[r]], writes=[self.Brstd[r]])
        js = self.jshift
        for c in range(8):
            i = self.k % 2
            self.k += 1
            p.op("dve", lambda e, i=i, c=c: e.tensor_tensor(self.tmp[i][:, :n], xs[c], self.rstd[r][:, :n], ALU.mult),
                 reads=[Bxs[c], self.Brstd[r]], writes=[self.Btmp[i]])
            for oi, (o, Bo) in enumerate(zip(outs, Bouts)):
                p.op("act", lambda e, i=i, c=c, o=o: e.activation(
                    o[c], self.tmp[i][:, :n], AF.Identity,
                    bias=cm.mod[:, js * 8 + c, stream:stream + 1], scale=self.s[:, c, stream:stream + 1]),
                    reads=[self.Btmp[i], cm.Bmod, self.Bs], writes=[Bo[c]])
def build_ada():
    p = Prog()
    w_d = p.dram_in("w", [4, 1024, 768])
    b_d = p.dram_in("b", [4, 128, 6])
    c_d = p.dram_in("cv", [128, 8, 2])
    o_d = p.dram_out("o", [4, 128, 6, 2])
    cv = p.sbuf([128, 8, 2], F32); Bcv = Buf()
    p.dma("sp", cv[:], c_d, writes=[Bcv])
    p.op("act", lambda e: e.activation(cv[:], cv[:], AF.Silu), reads=[Bcv], writes=[Bcv])
    bb = p.sbuf([128, 4, 6], F32); Bbb = Buf()
    p.dma("sp", bb[:], b_d.rearrange("l p i -> p l i"), writes=[Bbb])
    ob = p.sbuf([128, 4, 6, 2], F32); Bob = Buf()
    ws = [p.sbuf([128, 8, 768], F32, "adaw%d" % l) for l in range(4)]
    Bws = [Buf() for _ in range(4)]
    ps = [p.psum([128, 512], F32, "adaps%d" % i) for i in range(2)]
    Bps = [Buf() for _ in range(2)]
    for l in range(4):
        p.dma("sp" if l % 2 == 0 else "act", ws[l][:], w_d[l].rearrange("(k p) m -> p k m", p=128), writes=[Bws[l]])
    n = 0
    for l in range(4):
        for i in range(6):
            b = n % 2
            n += 1
            for k in range(8):
                p.op("pe", lambda e, l=l, i=i, k=k, b=b: e.matmul(
                    ps[b][:, 0:2], ws[l][:, k, i * 128:(i + 1) * 128], cv[:, k, :], start=(k == 0), stop=(k == 7)),
                    reads=[Bws[l], Bcv], writes=[Bps[b]])
            p.op("act", lambda e, l=l, i=i, b=b: e.activation(ob[:, l, i, :], ps[b][:, 0:2], AF.Identity,
                                                           bias=bb[:, l, i:i + 1], scale=1.0),
                 reads=[Bps[b], Bbb], writes=[Bob])
    Bo = Buf()
    p.dma("sp", o_d.rearrange("l p i s -> p l i s"), ob[:], reads=[Bob], writes=[Bo])
    p.wait_all("sp", [Bo])
    return p.finalize()


DEV_NG = 99
FFN_DMACAST = True
NYB = 4


def build_ffn(E):
    p = Prog()
    xT = p.dram_in("xT", [1024, NT])
    g_d = p.dram_in("g", [128, 8])
    wgu = p.dram_in("wgu", [E, 1024, 7168])
    wdn = p.dram_in("wdn", [E, 3584, 1024])
    xo = p.dram_out("xo", [1024, NT])
    cm = Common(p)
    s4, Bs4 = cm.scale_vec(g_d, 4, "s4")
    moe = E > 1
    if moe:
        rt_d = p.dram_in("router", [1024, 8])
        id_d = p.dram_in("ident", [128, 128])
        sel_d = p.dram_in("sel", [8, 8, 128])
        rt = p.sbuf([128, 8, 8], F32, "rt"); Brt = Buf()
        p.dma("sp", rt[:], rt_d.rearrange("(k p) e -> p k e", p=128), writes=[Brt])
        ident = p.sbuf([128, 128], F32, "ident_sb"); Bid = Buf()
        p.dma("sp", ident[:], id_d, writes=[Bid])
        sel = p.sbuf([8, 8, 128], F32, "sel_sb"); Bsel = Buf()
        p.dma("sp", sel[:], sel_d, writes=[Bsel])
        GT = p.sbuf([8, NT], F32, "GT"); BGT = [Buf() for _ in TT]
        h32 = p.sbuf([128, 8, 512], F32, "h32"); Bh32 = [Buf() for _ in range(8)]
        gbc = [p.sbuf([128, NT], F32, "gbc%d" % i) for i in range(2)]
        Bgbc = [[Buf() for _ in TT] for _ in range(2)]
        swt = [p.sbuf([128, 512], F32, "swt%d" % i) for i in range(2)]
        Bswt = [Buf() for _ in range(2)]
        sm = {k: p.sbuf([128, 8], F32, "sm_" + k) for k in ("l", "eq1", "l2", "eq2", "G")}
        sv = {k: p.sbuf([128, 1], F32, "sv_" + k) for k in ("m1", "m2", "d", "w1", "w2")}
        Bsm = Buf()
    x = p.sbuf([128, 8, NT], F32, "x")
    Bx = [[Buf() for _ in TT] for _ in range(8)]
    hb = p.sbuf([128, 8, NT], BF16, "hb")
    Bhb = [[Buf() for _ in TT] for _ in range(8)]
    ps = [p.psum([128, 512], F32, "bank%d" % i) for i in range(8)]
    Bps = [Buf() for _ in range(8)]
    xv = fm(xT)
    for ti, (t0, n, st) in enumerate(TT):
        for c in range(8):
            p.dma("sp", x[:, c, t0:t0 + n], xv[:, c, t0:t0 + n], writes=[Bx[c][ti]])
    nm = NormMod(p, cm, s4, Bs4, 3, ps[7], Bps[7])
    for ti, (t0, n, st) in enumerate(TT):
        xs = [x[:, c, t0:t0 + n] for c in range(8)]
        Bxs = [Bx[c][ti] for c in range(8)]
        outs = [[hb[:, c, t0:t0 + n] for c in range(8)]]
        Bouts = [[Bhb[c][ti] for c in range(8)]]
        if moe:
            outs.append([h32[:, c, :n] for c in range(8)])
            Bouts.append(Bh32)
        nm.run(xs, Bxs, n, st, outs, Bouts)
        if moe:
            for sub in range(n // 128):
                lg = ps[6][:, 0:8]
                for k in range(8):
                    p.op("pe", lambda e, k=k, sub=sub: e.matmul(lg, h32[:, k, sub * 128:(sub + 1) * 128], rt[:, k, :],
                                                               start=(k == 0), stop=(k == 7)),
                         reads=[Bh32[k], Brt], writes=[Bps[6]])
                D = lambda fn, rd=(), wr=(): p.op("dve", fn, reads=[Bsm] + list(rd), writes=[Bsm] + list(wr))
                D(lambda e: e.tensor_copy(sm["l"][:], lg), rd=[Bps[6]])
                D(lambda e: e.reduce_max(sv["m1"][:], sm["l"][:], AX.X))
                D(lambda e: e.tensor_scalar(sm["eq1"][:], sm["l"][:], sv["m1"][:, 0:1], None, ALU.is_equal))
                D(lambda e: e.scalar_tensor_tensor(sm["l2"][:], sm["eq1"][:], -1e30, sm["l"][:], ALU.mult, ALU.add))
                D(lambda e: e.reduce_max(sv["m2"][:], sm["l2"][:], AX.X))
                D(lambda e: e.tensor_scalar(sm["eq2"][:], sm["l2"][:], sv["m2"][:, 0:1], None, ALU.is_equal))
                D(lambda e: e.tensor_tensor(sv["d"][:], sv["m2"][:], sv["m1"][:], ALU.subtract))
                p.op("act", lambda e: e.activation(sv["d"][:], sv["d"][:], AF.Exp), reads=[Bsm], writes=[Bsm])
                D(lambda e: e.tensor_scalar(sv["w1"][:], sv["d"][:], 1.0, None, ALU.add))
                D(lambda e: e.reciprocal(sv["w1"][:], sv["w1"][:]))
                D(lambda e: e.tensor_tensor(sv["w2"][:], sv["d"][:], sv["w1"][:], ALU.mult))
                D(lambda e: e.tensor_scalar(sm["G"][:], sm["eq1"][:], sv["w1"][:, 0:1], None, ALU.mult))
                D(lambda e: e.scalar_tensor_tensor(sm["G"][:], sm["eq2"][:], sv["w2"][:, 0:1], sm["G"][:], ALU.mult, ALU.add))
                tp = ps[6][0:8, 128:256]
                p.op("pe", lambda e: e.transpose(tp, sm["G"][:], ident[:]), reads=[Bsm, Bid], writes=[Bps[6]])
                p.op("act", lambda e, sub=sub, t0=t0: e.activation(GT[:, t0 + sub * 128:t0 + (sub + 1) * 128], tp, AF.Identity),
                     reads=[Bps[6]], writes=[BGT[ti]])
    GH = 2 if moe else 4
    GW = GH * 128
    NG = min(28 // GH, DEV_NG)
    wg_sb = [p.sbuf([128, 8, 2, GW], BF16, "wgu%d" % i) for i in range(2)]
    wd_sb = [p.sbuf([128, GH, 1024], BF16, "wdn%d" % i) for i in range(2)]
    Bwg = [[Buf() for _ in range(8)] for _ in range(2)]
    Bwd = [[Buf() for _ in range(GH)] for _ in range(2)]
    act = [p.sbuf([128, GH, 512], BF16, "act%d" % i) for i in range(2)]
    Bact = [[Buf() for _ in range(GH)] for _ in range(2)]
    sg = [p.sbuf([128, 512], F32, "sg%d" % i) for i in range(2)]
    Bsg = [Buf() for _ in range(2)]
    NSTG = 3 if moe else 6
    stg = [p.sbuf([128, 1024], F32, "stg%d" % i) for i in range(0 if FFN_DMACAST else NSTG)]
    Bstg = [Buf() for _ in range(NSTG)]
    cnt = {"si": 0, "hi": 0, "yi": 0}

    def load_parts(ex, j, wb):
        wv = wgu[ex].rearrange("(k p) (h m) -> p k h m", p=128, h=2)
        dv = wdn[ex].rearrange("(k p) m -> p k m", p=128)
        c0 = j * GW
        steps = []

        def gu(k):
            if FFN_DMACAST:
                p.dma("pool", wg_sb[wb][:, k, :, :], wv[:, k, :, c0:c0 + GW], writes=[Bwg[wb][k]])
                return
            sb_ = cnt["si"] % NSTG
            cnt["si"] += 1
            sv_ = stg[sb_][:, 0:2 * GW].rearrange("p (a b) -> p a b", a=2)
            p.dma("sp", sv_, wv[:, k, :, c0:c0 + GW], writes=[Bstg[sb_]])
            p.I("pool", "tensor_copy", wg_sb[wb][:, k, :, :], sv_, reads=[Bstg[sb_]], writes=[Bwg[wb][k]])

        def dn(hc):
            if FFN_DMACAST:
                p.dma("pool", wd_sb[wb][:, hc, :], dv[:, j * GH + hc, :], writes=[Bwd[wb][hc]])
                return
            sb_ = cnt["si"] % NSTG
            cnt["si"] += 1
            p.dma("sp", stg[sb_][:], dv[:, j * GH + hc, :], writes=[Bstg[sb_]])
            p.I("pool", "tensor_copy", wd_sb[wb][:, hc, :], stg[sb_][:], reads=[Bstg[sb_]], writes=[Bwd[wb][hc]])

        for k in range(8):
            steps.append(lambda k=k: gu(k))
        for hc in range(GH):
            steps.append(lambda hc=hc: dn(hc))
        return steps

    def gate_bc(ex):
        for ti, (t0, n, st) in enumerate(TT):
            p.I("pe", "matmul", ps[7][:, :n], sel[:, ex, :], GT[:, t0:t0 + n], start=True, stop=True,
                reads=[Bsel, BGT[ti]], writes=[Bps[7]])
            p.I("act", "activation", gbc[ex % 2][:, t0:t0 + n], ps[7][:, :n], AF.Identity, reads=[Bps[7]], writes=[Bgbc[ex % 2][ti]])

    def stage1(item, hcs=None):
        ex, j, wb, ti, ab = item
        t0, n, st = TT[ti]
        for hc in (range(GH) if hcs is None else hcs):
            s_ = cnt["hi"] % 2
            cnt["hi"] += 1
            pg, pu = ps[2 * s_], ps[2 * s_ + 1]
            for half, pp, Bpp in ((0, pg, Bps[2 * s_]), (1, pu, Bps[2 * s_ + 1])):
                for k in range(8):
                    p.I("pe", "matmul", pp[:, :n], wg_sb[wb][:, k, half, hc * 128:(hc + 1) * 128], hb[:, k, t0:t0 + n],
                        start=(k == 0), stop=(k == 7), reads=[Bwg[wb][k], Bhb[k][ti]], writes=[Bpp])
            p.I("act", "activation", sg[s_][:, :n], pg[:, :n], AF.Silu, reads=[Bps[2 * s_]], writes=[Bsg[s_]])
            if moe:
                p.I("dve", "tensor_tensor", swt[s_][:, :n], sg[s_][:, :n], pu[:, :n], ALU.mult,
                    reads=[Bsg[s_], Bps[2 * s_ + 1]], writes=[Bswt[s_]])
                p.I("pool", "tensor_tensor", act[ab][:, hc, :n], swt[s_][:, :n], gbc[ex % 2][:, t0:t0 + n], ALU.mult,
                    reads=[Bswt[s_], Bgbc[ex % 2][ti]], writes=[Bact[ab][hc]])
            else:
                p.I("dve", "tensor_tensor", act[ab][:, hc, :n], sg[s_][:, :n], pu[:, :n], ALU.mult,
                    reads=[Bsg[s_], Bps[2 * s_ + 1]], writes=[Bact[ab][hc]])

    def stage2(item, ocs=None):
        ex, j, wb, ti, ab = item
        t0, n, st = TT[ti]
        for oc in (range(8) if ocs is None else ocs):
            yb = 4 + (cnt["yi"] % NYB)
            cnt["yi"] += 1
            for hc in range(GH):
                p.I("pe", "matmul", ps[yb][:, :n], wd_sb[wb][:, hc, oc * 128:(oc + 1) * 128], act[ab][:, hc, :n],
                    start=(hc == 0), stop=(hc == GH - 1), reads=[Bwd[wb][hc], Bact[ab][hc]], writes=[Bps[yb]])
            xs = x[:, oc, t0:t0 + n]
            p.I("dve", "scalar_tensor_tensor", xs, ps[yb][:, :n], cm.mod[:, 40 + oc, st:st + 1], xs, ALU.mult, ALU.add,
                reads=[Bps[yb], cm.Bmod, Bx[oc][ti]], writes=[Bx[oc][ti]])

    groups = [(ex, j) for ex in range(E) for j in range(NG)]
    nT = len(TT)
    for st_ in load_parts(groups[0][0], groups[0][1], 0):
        st_()
    prev = None
    n_items = 0
    for gidx, (ex, j) in enumerate(groups):
        wb = gidx % 2
        nxt = load_parts(groups[gidx + 1][0], groups[gidx + 1][1], (gidx + 1) % 2) if gidx + 1 < len(groups) else []
        per = -(-len(nxt) // nT)
        for ti in range(nT):
            item = (ex, j, wb, ti, n_items % 2)
            n_items += 1
            if ti == 0 and moe and j == 0:
                gate_bc(ex)
            for hcs, ocs in ((range(0, GH // 2), range(0, 4)), (range(GH // 2, GH), range(4, 8))):
                stage1(item, hcs)
                if prev is not None:
                    stage2(prev, ocs)
            prev = item
            for st_ in nxt[ti * per:(ti + 1) * per]:
                st_()
    stage2(prev)
    ov = fm(xo)
    Bo = []
    for ti, (t0, n, st) in enumerate(TT):
        for c in range(8):
            b = Buf()
            p.dma("sp", ov[:, c, t0:t0 + n], x[:, c, t0:t0 + n], reads=[Bx[c][ti]], writes=[b])
            Bo.append(b)
    p.wait_all("sp", Bo)
    return p.finalize()
class Caster:
    def __init__(self, p, n=3, width=1024):
        self.p = p
        self.w = width
        self.stg = [p.sbuf([128, width], F32, "cst%d" % i) for i in range(n)]
        self.B = [Buf() for _ in range(n)]
        self.i = 0

    def load(self, dst, src, Bdst, eng="pool"):
        p = self.p
        m = dst.shape[-1]
        k = self.i % len(self.stg)
        self.i += 1
        p.dma("sp", self.stg[k][:, :m], src, writes=[self.B[k]])
        if eng == "pool":
            p.op("pool", lambda e: e.tensor_copy(dst, self.stg[k][:, :m]), reads=[self.B[k]], writes=[Bdst])
        else:
            p.op("act", lambda e: e.activation(dst, self.stg[k][:, :m], AF.Identity), reads=[self.B[k]], writes=[Bdst])


XZW = 2310


def build_rga():
    p = Prog()
    xT = p.dram_in("xT", [1024, NT])
    xh = p.dram_in("xh", [1024, 3])
    hm_d = p.dram_in("hmask", [128, 3])
    g_d = p.dram_in("g", [128, 8])
    win = p.dram_in("w_in", [1024, 2048])
    cw_d = p.dram_in("conv_w", [128, 8, 4])
    cb_d = p.dram_in("conv_b", [128, 8])
    XC = p.dram_out("XC", [1024, NT])
    GO = p.dram_out("G", [1024, NT])
    cm = Common(p)
    s1, Bs1 = cm.scale_vec(g_d, 1, "s1")
    cw = p.sbuf([128, 8, 4], F32, "cw"); cb = p.sbuf([128, 8], F32, "cb"); hm = p.sbuf([128, 3], F32, "hm")
    Bc = Buf()
    p.dma("sp", cw[:], cw_d, writes=[Bc]); p.dma("sp", cb[:], cb_d, writes=[Bc]); p.dma("sp", hm[:], hm_d, writes=[Bc])
    ps = [p.psum([128, 512], F32, "bank%d" % i) for i in range(8)]
    Bps = [Buf() for _ in range(8)]
    NH = NT + 3
    hb = p.sbuf([128, 8, NH], BF16, "hb")
    tiles = TT + [(NT, 3, 0)]
    Bhb = [[Buf() for _ in tiles] for _ in range(8)]
    wsb = p.sbuf([128, 8, 2048], BF16, "w_in_sb")
    Bw = [[Buf() for _ in range(2)] for _ in range(8)]
    cst = Caster(p, 3, 1024)
    wv = win.rearrange("(k p) m -> p k m", p=128)
    xt = [p.sbuf([128, 8, 512], F32, "xt%d" % i) for i in range(2)]
    Bxt = [[Buf() for _ in range(8)] for _ in range(2)]
    nm = NormMod(p, cm, s1, Bs1, 0, ps[7], Bps[7])
    xv = fm(xT)
    xhv = fm(xh)
    for ti, (t0, n, st) in enumerate(tiles):
        b = ti % 2
        for c in range(8):
            src = xv[:, c, t0:t0 + n] if ti < 5 else xhv[:, c, :]
            p.dma("sp", xt[b][:, c, :n], src, writes=[Bxt[b][c]])
        nm.run([xt[b][:, c, :n] for c in range(8)], Bxt[b], n, st,
               [[hb[:, c, t0:t0 + n] for c in range(8)]], [[Bhb[c][ti] for c in range(8)]])
        if ti == 0:
            for k in range(8):
                for hh in range(2):
                    cst.load(wsb[:, k, hh * 1024:(hh + 1) * 1024], wv[:, k, hh * 1024:(hh + 1) * 1024], Bw[k][hh])
    xz = [p.sbuf([128, XZW], F32, "xz%d" % i) for i in range(2)]
    Bxz = [Buf() for _ in range(2)]
    xco = [p.sbuf([128, NT], F32, "xco%d" % i) for i in range(2)]
    Bxco = [Buf() for _ in range(2)]
    gsb = [p.sbuf([128, 512], F32, "gsb%d" % i) for i in range(2)]
    Bgsb = [Buf() for _ in range(2)]
    for i in range(2):
        p.op("dve", lambda e, i=i: e.memset(xz[i][:], 0.0), writes=[Bxz[i]])
    Bout = []
    pi = 0
    gi = 0
    for oc in range(16):
        hh = oc // 8
        zb = oc % 2
        for ti, (t0, n, st) in enumerate(tiles):
            bk = pi % 6
            pi += 1
            for k in range(8):
                p.op("pe", lambda e, k=k, oc=oc, bk=bk, t0=t0, n=n: e.matmul(
                    ps[bk][:, :n], wsb[:, k, oc * 128:(oc + 1) * 128], hb[:, k, t0:t0 + n], start=(k == 0), stop=(k == 7)),
                    reads=[Bw[k][hh], Bhb[k][ti]], writes=[Bps[bk]])
            if hh == 0:
                if ti < 4:
                    p.op("act", lambda e, zb=zb, bk=bk, t0=t0, n=n: e.activation(xz[zb][:, 2 + t0:2 + t0 + n], ps[bk][:, :n], AF.Identity),
                         reads=[Bps[bk]], writes=[Bxz[zb]])
                elif ti == 4:
                    p.op("act", lambda e, zb=zb, bk=bk, n=n: e.activation(xz[zb][:, 2053:2053 + n], ps[bk][:, :n], AF.Identity),
                         reads=[Bps[bk]], writes=[Bxz[zb]])
                else:
                    p.op("dve", lambda e, zb=zb, bk=bk: e.tensor_tensor(xz[zb][:, 0:2], ps[bk][:, 0:2], hm[:, 0:2], ALU.mult),
                         reads=[Bps[bk], Bc], writes=[Bxz[zb]])
                    p.op("dve", lambda e, zb=zb, bk=bk: e.tensor_tensor(xz[zb][:, 2050:2051], ps[bk][:, 2:3], hm[:, 2:3], ALU.mult),
                         reads=[Bps[bk], Bc], writes=[Bxz[zb]])
            elif ti < 5:
                gb = gi % 2
                gi += 1
                p.op("act", lambda e, gb=gb, bk=bk, n=n: e.activation(gsb[gb][:, :n], ps[bk][:, :n], AF.Gelu_apprx_tanh),
                     reads=[Bps[bk]], writes=[Bgsb[gb]])
                b_ = Buf()
                p.dma("sp", fm(GO)[:, oc - 8, t0:t0 + n], gsb[gb][:, :n], reads=[Bgsb[gb]], writes=[b_])
                Bout.append(b_)
        if hh == 0:
            for (o0, ln, z0) in ((0, 2048, 0), (2048, 256, 2051)):
                p.op("dve", lambda e, zb=zb, oc=oc, o0=o0, ln=ln, z0=z0: e.tensor_scalar(
                    xco[zb][:, o0:o0 + ln], xz[zb][:, z0:z0 + ln], cw[:, oc, 0:1], cb[:, oc:oc + 1], ALU.mult, ALU.add),
                    reads=[Bxz[zb], Bc], writes=[Bxco[zb]])
                for j in range(1, 4):
                    p.op("dve", lambda e, zb=zb, oc=oc, o0=o0, ln=ln, z0=z0, j=j: e.scalar_tensor_tensor(
                        xco[zb][:, o0:o0 + ln], xz[zb][:, z0 + j:z0 + j + ln], cw[:, oc, j:j + 1], xco[zb][:, o0:o0 + ln],
                        ALU.mult, ALU.add),
                        reads=[Bxz[zb], Bc, Bxco[zb]], writes=[Bxco[zb]])
            b_ = Buf()
            p.dma("sp", fm(XC)[:, oc, :], xco[zb][:], reads=[Bxco[zb]], writes=[b_])
            Bout.append(b_)
    p.wait_all("sp", Bout)
    return p.finalize()
def build_rgs(full):
    p = Prog()
    XC = p.dram_in("XC", [1024, NT])
    wa_d = p.dram_in("wa", [2, 4, 256, 256])
    wi_d = p.dram_in("wi", [2, 4, 256, 256])
    gb_d = p.dram_in("gbias", [128, 3, 2, 8])
    ps = [p.psum([128, 512], F32, "bank%d" % i) for i in range(8)]
    Bps = [Buf() for _ in range(8)]
    gbs = p.sbuf([128, 3, 2, 8], F32, "gbs"); Bgb = Buf()
    p.dma("sp", gbs[:], gb_d, writes=[Bgb])
    cl = p.sbuf([128, 2, 2, 8], F32, "cl"); Bcl = Buf()
    p.op("act", lambda e: e.activation(cl[:, 0], gbs[:, 2], AF.Exp, scale=-1.0), reads=[Bgb], writes=[Bcl])
    p.op("act", lambda e: e.activation(cl[:, 0], cl[:, 0], AF.Ln, bias=1.0), reads=[Bcl], writes=[Bcl])
    p.op("dve", lambda e: e.tensor_scalar(cl[:, 1], cl[:, 0], -8.0, None, ALU.mult), reads=[Bcl], writes=[Bcl])
    p.op("dve", lambda e: e.tensor_scalar(cl[:, 0], cl[:, 0], -4.0, None, ALU.mult), reads=[Bcl], writes=[Bcl])
    hbias = p.sbuf([128, 2, 2, 8], F32, "hbias")
    p.op("dve", lambda e: e.tensor_scalar(hbias[:], gbs[:, 0:2], 0.5, None, ALU.mult), reads=[Bgb], writes=[Bcl])
    cst = Caster(p, 2 if full else 3, 1152)
    gw = p.sbuf([128, 2, 2, 4, 2, 256], BF16, "gw"); Bgw = Buf()
    for d in range(2):
        for gi_, src in enumerate((wa_d, wi_d)):
            for n_ in range(4):
                cst.load(gw[:, d, gi_, n_].rearrange("p a b -> p (a b)"),
                         src[d, n_].rearrange("(kk p) m -> p kk m", p=128), Bgw)
    xcb = p.sbuf([128, 8, NT], BF16, "xcb"); Bxcb = [Buf() for _ in range(8)]
    xcv = fm(XC)
    for c in range(8):
        for hh in range(2):
            cst.load(xcb[:, c, hh * 1152:(hh + 1) * 1152], xcv[:, c, hh * 1152:(hh + 1) * 1152], Bxcb[c], eng="act" if c % 2 else "pool")
    xcf = [p.sbuf([128, NT], F32, "xcf%d" % i) for i in range(2)]; Bxcf = [Buf() for _ in range(2)]
    SW = 1024 if full else 2048
    T = {k: [p.sbuf([128, SW], F32, "t_%s%d" % (k, i)) for i in range(2)] for k in ("a", "s", "ib")}
    BT = {k: [[Buf() for _ in range(SW // 512)] for _ in range(2)] for k in T}
    Tr = [p.sbuf([128, 512], F32, "t_r%d" % i) for i in range(2)]
    BTr = [Buf() for _ in range(2)]
    if not full:
        zeros = p.sbuf([128, 512], F32, "zeros"); Bz = Buf()
        p.op("dve", lambda e: e.memset(zeros[:], 0.0), writes=[Bz])
    hbuf = [p.sbuf([128, NT], F32, "hf%d" % i) for i in range(2)]
    Bh = [[Buf() for _ in TT] for _ in range(2)]
    if not full:
        junk = p.sbuf([128, 512], F32, "junk"); Bjunk = Buf()
    comp = p.sbuf([128, 8, 4], F32, "comp"); Bcomp = Buf()
    if full:
        ca_d = p.dram_in("comp_all", [128, 8, 8, 4])
        oh_d = p.dram_in("onehot", [128, 2, 9])
        G_d = p.dram_in("G", [1024, NT])
        wo_d = p.dram_in("w_out", [1024, 1024])
        xT = p.dram_in("xT", [1024, NT])
        xo = p.dram_out("xo", [1024, NT])
        cm = Common(p)
        ca = p.sbuf([128, 8, 8, 4], F32, "ca"); oh = p.sbuf([128, 2, 9], F32, "oh"); Bca = Buf()
        p.dma("sp", ca[:], ca_d, writes=[Bca]); p.dma("sp", oh[:], oh_d, writes=[Bca])
        ext = p.sbuf([128, 9], F32, "ext"); ext2 = p.sbuf([128, 9], F32, "ext2"); car = p.sbuf([128, 1], F32, "car")
        Bext = Buf()
        wo = p.sbuf([128, 8, 1024], BF16, "wo"); Bwo = [Buf() for _ in range(8)]
        for k in range(8):
            cst.load(wo[:, k, :], wo_d.rearrange("(k p) m -> p k m", p=128)[:, k, :], Bwo[k])
        yin = p.sbuf([128, 8, NT], BF16, "yin"); Byin = [[Buf() for _ in TT] for _ in range(8)]
        gch = [p.sbuf([128, NT], F32, "gch%d" % i) for i in range(2)]; Bgch = [Buf() for _ in range(2)]
    else:
        comp_o = p.dram_out("comp", [128, 8, 4])
    st_ = {"ki": 0, "pi": 0, "ri": 0}

    def coeffs(tiles, d, oc):
        blk = oc // 2
        kb = st_["ki"] % 2
        st_["ki"] += 1
        where = {}
        for j, ti in enumerate(tiles):
            t0, n, st = TT[ti]
            sl = j if full else ti
            where[ti] = (kb, sl)
            rb = st_["ri"] % 2
            st_["ri"] += 1
            q = st_["pi"] % 3
            st_["pi"] += 1
            pr, pq = ps[q * 2], ps[q * 2 + 1]
            Bpr, Bpq = Bps[q * 2], Bps[q * 2 + 1]
            for gi_, pp, Bpp in ((0, pr, Bpr), (1, pq, Bpq)):
                for kk in range(2):
                    p.I("pe", "matmul", pp[:, :n], gw[:, d, gi_, blk, kk, (oc % 2) * 128:(oc % 2 + 1) * 128],
                        xcb[:, 2 * blk + kk, t0:t0 + n], start=(kk == 0), stop=(kk == 1),
                        reads=[Bgw, Bxcb[2 * blk + kk]], writes=[Bpp])
            r_ = Tr[rb][:, :n]
            a_, s_, i_ = (T[k][kb][:, sl * 512:sl * 512 + n] for k in ("a", "s", "ib"))
            p.I("act", "activation", r_, pr[:, :n], AF.Tanh, bias=hbias[:, 0, d, oc:oc + 1], scale=0.5, reads=[Bpr, Bcl], writes=[BTr[rb]])
            p.I("act", "activation", i_, pq[:, :n], AF.Tanh, bias=hbias[:, 1, d, oc:oc + 1], scale=0.5, reads=[Bpq, Bcl], writes=[BT["ib"][kb][sl]])
            p.I("act", "activation", a_, r_, AF.Exp, bias=cl[:, 0, d, oc:oc + 1], scale=cl[:, 0, d, oc:oc + 1],
                reads=[BTr[rb], Bcl], writes=[BT["a"][kb][sl]])
            p.I("act", "activation", s_, r_, AF.Exp, bias=cl[:, 1, d, oc:oc + 1], scale=cl[:, 1, d, oc:oc + 1],
                reads=[BTr[rb], Bcl], writes=[BT["s"][kb][sl]])
        for ti in tiles:
            t0, n, st = TT[ti]
            kb, sl = where[ti]
            s_ = T["s"][kb][:, sl * 512:sl * 512 + n]
            p.I("act", "activation", s_, s_, AF.Sqrt, bias=0.25, scale=-0.25, reads=[BT["s"][kb][sl]], writes=[BT["s"][kb][sl]])
        return where

    def scan(ti, wh, init, Binit, d, oc):
        t0, n, st = TT[ti]
        kb, sl = wh
        xb = oc % 2
        a_, s_, b_ = (T[k][kb][:, sl * 512:sl * 512 + n] for k in ("a", "s", "ib"))
        Ba_, Bs_, Bb_ = (BT[k][kb][sl] for k in ("a", "s", "ib"))
        p.I("dve", "scalar_tensor_tensor", b_, b_, 1.0, xcf[xb][:, t0:t0 + n], ALU.add, ALU.mult, reads=[Bb_, Bxcf[xb]], writes=[Bb_])
        p.I("dve", "tensor_tensor", b_, b_, s_, ALU.mult, reads=[Bb_, Bs_], writes=[Bb_])
        o = hbuf[d][:, t0:t0 + n]
        if d == 1:
            o, a_, b_ = o[:, ::-1], a_[:, ::-1], b_[:, ::-1]
        p.I("dve", "tensor_tensor_scan", o, a_, b_, init, ALU.mult, ALU.add, reads=[Ba_, Bb_] + Binit, writes=[Bh[d][ti]])

    def endcol(ti, d):
        t0, n, st = TT[ti]
        c = t0 + n - 1 if d == 0 else t0
        return hbuf[d][:, c:c + 1]

    for oc in range(8):
        xb = oc % 2
        p.dma("sp", xcf[xb][:], xcv[:, oc, :], writes=[Bxcf[xb]])
        if full:
            p.dma("sp", gch[xb][:], fm(G_d)[:, oc, :], writes=[Bgch[xb]])
        for d in range(2):
            order = [0, 1, 2, 3] if d == 0 else [3, 2, 1, 0]
            if full:
                wh = coeffs([4], d, oc)
                scan(4, wh[4], 0.0, [], d, oc)
                e0 = endcol(4, d)
                p.I("dve", "tensor_copy", ext[:, 0:1], e0, reads=[Bh[d][4]], writes=[Bext])
                Aall, Ball = ca[:, oc, :, 2 * d], ca[:, oc, :, 2 * d + 1]
                if d == 1:
                    Aall, Ball = Aall[:, ::-1], Ball[:, ::-1]
                p.I("dve", "tensor_tensor_scan", ext[:, 1:9], Aall, Ball, e0, ALU.mult, ALU.add,
                    reads=[Bca, Bh[d][4], Bext], writes=[Bext])
                p.I("dve", "tensor_tensor", ext2[:], ext[:], oh[:, d, :], ALU.mult, reads=[Bext, Bca], writes=[Bext])
                p.I("dve", "reduce_sum", car[:], ext2[:], AX.X, reads=[Bext], writes=[Bext])
                init, Binit = car[:, 0:1], [Bext]
            else:
                init, Binit = 0.0, []
            prevA, BprevA = 1.0, []
            if not full:
                wh = coeffs(order, d, oc)
            for idx, ti in enumerate(order):
                if full and idx % 2 == 0:
                    wh = coeffs(order[idx:idx + 2], d, oc)
                scan(ti, wh[ti], init, Binit, d, oc)
                init, Binit = endcol(ti, d), [Bh[d][ti]]
                if not full:
                    n = TT[ti][1]
                    kb, sl = wh[ti]
                    p.I("dve", "tensor_tensor_scan", junk[:, :n], T["a"][kb][:, sl * 512:sl * 512 + n], zeros[:, :n], prevA, ALU.mult, ALU.add,
                        reads=[BT["a"][kb][sl], Bz, Bjunk] + BprevA, writes=[Bjunk])
                    p.I("dve", "tensor_copy", comp[:, oc, 2 * d:2 * d + 1], junk[:, n - 1:n], reads=[Bjunk], writes=[Bcomp])
                    prevA, BprevA = comp[:, oc, 2 * d:2 * d + 1], [Bcomp]
            if not full:
                p.I("dve", "tensor_copy", comp[:, oc, 2 * d + 1:2 * d + 2], init, reads=Binit, writes=[Bcomp])
        if full:
            for ti, (t0, n, st) in enumerate(TT):
                p.I("dve", "tensor_tensor", hbuf[0][:, t0:t0 + n], hbuf[0][:, t0:t0 + n], hbuf[1][:, t0:t0 + n], ALU.add,
                    reads=[Bh[0][ti], Bh[1][ti]], writes=[Bh[0][ti]])
                p.I("dve", "tensor_tensor", yin[:, oc, t0:t0 + n], hbuf[0][:, t0:t0 + n], gch[xb][:, t0:t0 + n], ALU.mult,
                    reads=[Bh[0][ti], Bgch[xb]], writes=[Byin[oc][ti]])
    Bout = []
    if full:
        xt = [p.sbuf([128, 512], F32, "xt%d" % i) for i in range(3)]; Bxt = [Buf() for _ in range(3)]
        xi = 0
        for ti, (t0, n, st) in enumerate(TT):
            for oc in range(8):
                b = xi % 3
                bk = 6 + xi % 2
                xi += 1
                p.dma("sp", xt[b][:, :n], fm(xT)[:, oc, t0:t0 + n], writes=[Bxt[b]])
                for k in range(8):
                    p.op("pe", lambda e, k=k, oc=oc, bk=bk, t0=t0, n=n: e.matmul(
                        ps[bk][:, :n], wo[:, k, oc * 128:(oc + 1) * 128], yin[:, k, t0:t0 + n], start=(k == 0), stop=(k == 7)),
                        reads=[Bwo[k], Byin[k][ti]], writes=[Bps[bk]])
                p.op("dve", lambda e, b=b, bk=bk, oc=oc, n=n, st=st: e.scalar_tensor_tensor(
                    xt[b][:, :n], ps[bk][:, :n], cm.mod[:, 16 + oc, st:st + 1], xt[b][:, :n], ALU.mult, ALU.add),
                    reads=[Bps[bk], cm.Bmod, Bxt[b]], writes=[Bxt[b]])
                b_ = Buf()
                p.dma("sp", fm(xo)[:, oc, t0:t0 + n], xt[b][:, :n], reads=[Bxt[b]], writes=[b_])
                Bout.append(b_)
    else:
        b_ = Buf()
        p.dma("sp", comp_o, comp[:], reads=[Bcomp], writes=[b_])
        Bout.append(b_)
    p.wait_all("sp", Bout)
    return p.finalize()
def build_naa():
    p = Prog()
    xT = p.dram_in("xT", [1024, NT])
    g_d = p.dram_in("g", [128, 8])
    w_d = p.dram_in("w_qkv", [1024, 3072])
    qkg_d = p.dram_in("qkg", [128, 2])
    QT = p.dram_out("QT", [1024, NT], BF16)
    KT = p.dram_out("KT", [1024, NT], BF16)
    V = p.dram_out("V", [NT, 1024], BF16)
    cm = Common(p)
    s1, Bs1 = cm.scale_vec(g_d, 1, "s1")
    qkg = p.sbuf([128, 2], F32, "qkg"); Bqkg = Buf()
    p.dma("sp", qkg[:], qkg_d, writes=[Bqkg])
    p.I("dve", "tensor_scalar", qkg[:, 0:1], qkg[:, 0:1], 0.125, None, ALU.mult, reads=[Bqkg], writes=[Bqkg])
    bones = p.sbuf([128, 128], BF16, "bones"); Bbo = Buf()
    p.I("dve", "memset", bones[:], 0.0, writes=[Bbo])
    p.I("dve", "memset", bones[0:64, 0:64], 1.0, writes=[Bbo])
    p.I("dve", "memset", bones[64:128, 64:128], 1.0, writes=[Bbo])
    ps = [p.psum([128, 512], F32, "bank%d" % i) for i in range(8)]
    Bps = [Buf() for _ in range(8)]
    hb = p.sbuf([128, 8, NT], BF16, "hb")
    Bhb = [[Buf() for _ in TT] for _ in range(8)]
    wsb = p.sbuf([128, 8, 3072], BF16, "wqkv")
    Bw = [[Buf() for _ in range(3)] for _ in range(8)]
    cst = Caster(p, 3, 1024)
    wv = w_d.rearrange("(k p) m -> p k m", p=128)
    xt = [p.sbuf([128, 8, 512], F32, "xt%d" % i) for i in range(2)]
    Bxt = [[Buf() for _ in range(8)] for _ in range(2)]
    nm = NormMod(p, cm, s1, Bs1, 0, ps[7], Bps[7])
    xv = fm(xT)
    for ti, (t0, n, st) in enumerate(TT):
        b = ti % 2
        for c in range(8):
            p.dma("sp", xt[b][:, c, :n], xv[:, c, t0:t0 + n], writes=[Bxt[b][c]])
        nm.run([xt[b][:, c, :n] for c in range(8)], Bxt[b], n, st,
               [[hb[:, c, t0:t0 + n] for c in range(8)]], [[Bhb[c][ti] for c in range(8)]])
        if ti == 0:
            for k in range(8):
                for hh in range(3):
                    cst.load(wsb[:, k, hh * 1024:(hh + 1) * 1024], wv[:, k, hh * 1024:(hh + 1) * 1024], Bw[k][hh])
    sq = [p.sbuf([128, 512], BF16, "sq%d" % i) for i in range(2)]; Bsq = [Buf() for _ in range(2)]
    rs = [p.sbuf([128, 512], F32, "rs%d" % i) for i in range(2)]; Brs = [Buf() for _ in range(2)]
    ob = [p.sbuf([128, 512], BF16, "ob%d" % i) for i in range(3)]; Bob = [Buf() for _ in range(3)]
    Bout = []
    vb = [p.sbuf([128, 512], BF16, "vb%d" % i) for i in range(3)]; Bvb = [Buf() for _ in range(3)]
    i4 = [0]

    def v_unit(sub, half):
        ti = min(sub // 4, 4)
        a = 4 + i4[0] % 2
        o = i4[0] % 3
        i4[0] += 1
        for k in range(8):
            p.I("pe", "matmul", ps[a][:, :], hb[:, k, sub * 128:(sub + 1) * 128], wsb[:, k, 2048 + half * 512:2048 + (half + 1) * 512],
                start=(k == 0), stop=(k == 7), reads=[Bw[k][2], Bhb[k][ti]], writes=[Bps[a]])
        p.I("act", "activation", vb[o][:], ps[a][:], AF.Identity, reads=[Bps[a]], writes=[Bvb[o]])
        b_ = Buf()
        p.dma("sp", V[sub * 128:(sub + 1) * 128, half * 512:(half + 1) * 512], vb[o][:], reads=[Bvb[o]], writes=[b_])
        Bout.append(b_)

    v_units = [(sub, half) for sub in range(NT // 128) for half in range(2)]
    i2 = 0
    i3 = 0
    for oc in range(16):
        which = oc // 8
        dst = fm(QT if which == 0 else KT)
        for ti, (t0, n, st) in enumerate(TT):
            a = i2 % 2
            i2 += 1
            pq, pm = ps[a * 2], ps[a * 2 + 1]
            Bpq, Bpm = Bps[a * 2], Bps[a * 2 + 1]
            for k in range(8):
                p.I("pe", "matmul", pq[:, :n], wsb[:, k, oc * 128:(oc + 1) * 128], hb[:, k, t0:t0 + n], start=(k == 0), stop=(k == 7),
                    reads=[Bw[k][which], Bhb[k][ti]], writes=[Bpq])
            p.I("act", "activation", sq[a][:, :n], pq[:, :n], AF.Square, reads=[Bpq], writes=[Bsq[a]])
            p.I("pe", "matmul", pm[:, :n], bones[:], sq[a][:, :n], start=True, stop=True, reads=[Bbo, Bsq[a]], writes=[Bpm])
            p.I("act", "activation", rs[a][:, :n], pm[:, :n], AF.Sqrt, bias=RMS_EPS, scale=1.0 / 64.0, reads=[Bpm], writes=[Brs[a]])
            p.I("dve", "reciprocal", rs[a][:, :n], rs[a][:, :n], reads=[Brs[a]], writes=[Brs[a]])
            o = i3 % 3
            i3 += 1
            p.I("dve", "scalar_tensor_tensor", ob[o][:, :n], pq[:, :n], qkg[:, which:which + 1], rs[a][:, :n], ALU.mult, ALU.mult,
                reads=[Bpq, Bqkg, Brs[a]], writes=[Bob[o]])
            b_ = Buf()
            p.dma("sp", dst[:, oc % 8, t0:t0 + n], ob[o][:, :n], reads=[Bob[o]], writes=[b_])
            Bout.append(b_)
            if i2 % 2 == 0 and v_units:
                v_unit(*v_units.pop(0))
    while v_units:
        v_unit(*v_units.pop(0))
    p.wait_all("sp", Bout)
    return p.finalize()


NSLOT = 20
NKT = NSLOT + 2
KW = NKT * 128


def build_nab():
    p = Prog()
    QT = p.dram_in("QT", [1024, NT], BF16)
    KT = p.dram_in("KText", [1024, KW], BF16)
    Vp = p.dram_in("Vp", [8, 128, NKT, 128], BF16)
    tab = p.dram_in("tab", [16, 128, 25, 128])
    wo_d = p.dram_in("w_o", [1024, 1024])
    xT = p.dram_in("xT", [1024, NT])
    xo = p.dram_out("xo", [1024, NT])
    cm = Common(p)
    ones = cm.ones
    cst = Caster(p, 3, 1024)
    wo = p.sbuf([128, 8, 1024], BF16, "wo"); Bwo = [Buf() for _ in range(8)]
    for k in range(8):
        cst.load(wo[:, k, :], wo_d.rearrange("(k p) m -> p k m", p=128)[:, k, :], Bwo[k])
    oT = p.sbuf([128, 8, NT], BF16, "oT"); BoT = [[Buf() for _ in range(18)] for _ in range(8)]
    qs = [p.sbuf([128, NT], BF16, "qs%d" % i) for i in range(2)]; Bqs = [Buf() for _ in range(2)]
    ks = [p.sbuf([128, KW], BF16, "ks%d" % i) for i in range(2)]; Bks = [Buf() for _ in range(2)]
    vs = [p.sbuf([128, NKT, 2, 65], BF16, "vs%d" % i) for i in range(2)]; Bvs = [Buf() for _ in range(2)]
    for i in range(2):
        p.I("dve", "memset", vs[i][:, :, :, 64:65], 1.0, writes=[Bvs[i]])
    id_d = p.dram_in("ident", [128, 128])
    idf = p.sbuf([128, 128], F32, "idf"); identb = p.sbuf([128, 128], BF16, "identb"); Bid = Buf()
    p.dma("sp", idf[:], id_d, writes=[Bid])
    p.I("dve", "tensor_copy", identb[:], idf[:], reads=[Bid], writes=[Bid])
    oq = [p.sbuf([128, 18, 128], BF16, "oq%d" % i) for i in range(2)]; Boq = [[Buf() for _ in range(18)] for _ in range(2)]
    tf = [p.sbuf([128, 25 * 128], F32, "tf%d" % i) for i in range(1)]; Btf = [Buf() for _ in range(1)]
    Eh = [p.sbuf([128, 25 * 128], BF16, "Eh%d" % i) for i in range(2)]; BEh = [Buf() for _ in range(2)]
    pe_ = [p.sbuf([128, 640], F32, "pexp%d" % i) for i in range(2)]; Bpe = [Buf() for _ in range(2)]
    pb = [p.sbuf([128, 896], BF16, "pbf%d" % i) for i in range(2)]; Bpb = [Buf() for _ in range(2)]
    rd = [p.sbuf([128, 128], F32, "rd%d" % i) for i in range(2)]; Brd = [Buf() for _ in range(2)]
    S = [p.psum([128, 1024], F32, "S%d" % i) for i in range(2)]; BS = [Buf() for _ in range(2)]
    O = [p.psum([128, 512], F32, "O%d" % i) for i in range(2)]; BO = [Buf() for _ in range(2)]
    Y = [p.psum([128, 512], F32, "Y%d" % i) for i in range(2)]; BY = [Buf() for _ in range(2)]
    def blk_tiles(blk):
        if blk < 16:
            return [blk + d for d in range(5)] + [NSLOT, NSLOT + 1]
        return [NSLOT, NSLOT + 1]

    def stage_s(item):
        hc, hh, blk, a = item
        b, hp, eb, q0 = hc % 2, 64 * hh, (2 * hc + hh) % 2, blk * 128
        tiles = blk_tiles(blk)
        for j, tl in enumerate(tiles):
            p.I("pe", "matmul", S[a][:, j * 128:(j + 1) * 128], ks[b][hp:hp + 64, tl * 128:(tl + 1) * 128],
                qs[b][hp:hp + 64, q0:q0 + 128], start=True, stop=True, reads=[Bks[b], Bqs[b]], writes=[BS[a]])
        if blk < 16:
            st_ = 0 if blk == 0 else (1 if blk == 1 else (3 if blk == 14 else (4 if blk == 15 else 2)))
            p.I("act", "activation", pe_[a][:], S[a][:, 0:640], AF.Exp, reads=[BS[a]], writes=[Bpe[a]])
            p.I("act", "activation", pb[a][:, 640:896], S[a][:, 640:896], AF.Exp, reads=[BS[a]], writes=[Bpb[a]])
            p.I("dve", "tensor_tensor", pb[a][:, 0:640], pe_[a][:], Eh[eb][:, st_ * 640:(st_ + 1) * 640], ALU.mult,
                reads=[Bpe[a], BEh[eb]], writes=[Bpb[a]])
        else:
            p.I("act", "activation", pb[a][:, 0:256], S[a][:, 0:256], AF.Exp, reads=[BS[a]], writes=[Bpb[a]])

    def stage_pv(item):
        hc, hh, blk, a = item
        b, hp = hc % 2, 64 * hh
        tiles = blk_tiles(blk)
        nt = len(tiles)
        for j, tl in enumerate(tiles):
            p.I("pe", "matmul", O[a][:, 0:65], pb[a][:, j * 128:(j + 1) * 128], vs[b][:, tl, hh, :], start=(j == 0), stop=(j == nt - 1),
                reads=[Bvs[b], Bpb[a]], writes=[BO[a]])
        p.I("dve", "reciprocal", rd[a][:, 0:1], O[a][:, 64:65], reads=[BO[a]], writes=[Brd[a]])
        p.I("dve", "tensor_scalar", oq[b][:, blk, hp:hp + 64], O[a][:, 0:64], rd[a][:, 0:1], None, ALU.mult,
            reads=[BO[a], Brd[a]], writes=[Boq[b][blk]])
        if hh == 1 and blk == 17:
            for k2 in range(18):
                x_ = k2 % 2
                p.I("pe", "matmul", Y[x_][:, 0:128], oq[b][:, k2, :], identb[:], start=True, stop=True, reads=[Boq[b][k2], Bid], writes=[BY[x_]])
                p.I("act", "activation", oT[:, hc, k2 * 128:(k2 + 1) * 128], Y[x_][:, 0:128], AF.Identity, reads=[BY[x_]], writes=[BoT[hc][k2]])

    items = [(hc, hh, blk) for hc in range(8) for hh in range(2) for blk in range(18)]
    prev = None
    for idx, (hc, hh, blk) in enumerate(items):
        b = hc % 2
        if hh == 0 and blk == 0:
            p.dma("sp", qs[b][:], fm(QT)[:, hc, :], writes=[Bqs[b]])
            p.dma("sp", ks[b][:], fm(KT)[:, hc, :], writes=[Bks[b]])
            p.dma("sp", vs[b][:, :, :, 0:64], Vp[hc].rearrange("p t (h f) -> p t h f", h=2), writes=[Bvs[b]])
        if blk == 0:
            h = 2 * hc + hh
            p.dma("sp", tf[0][:], tab[h].rearrange("p a b -> p (a b)"), writes=[Btf[0]])
            p.I("act", "activation", Eh[h % 2][:], tf[0][:], AF.Exp, reads=[Btf[0]], writes=[BEh[h % 2]])
        item = (hc, hh, blk, idx % 2)
        stage_s(item)
        if prev is not None:
            stage_pv(prev)
        prev = item
    stage_pv(prev)
    xt = [p.sbuf([128, 512], F32, "xt%d" % i) for i in range(3)]; Bxt = [Buf() for _ in range(3)]
    Bout = []
    xi = 0
    for ti, (t0, n, st) in enumerate(TT):
        for oc in range(8):
            b = xi % 3
            a = xi % 2
            xi += 1
            p.dma("sp", xt[b][:, :n], fm(xT)[:, oc, t0:t0 + n], writes=[Bxt[b]])
            for k in range(8):
                p.I("pe", "matmul", Y[a][:, :n], wo[:, k, oc * 128:(oc + 1) * 128], oT[:, k, t0:t0 + n], start=(k == 0), stop=(k == 7),
                    reads=[Bwo[k]] + BoT[k][t0 // 128:(t0 + n) // 128], writes=[BY[a]])
            p.I("dve", "scalar_tensor_tensor", xt[b][:, :n], Y[a][:, :n], cm.mod[:, 16 + oc, st:st + 1], xt[b][:, :n], ALU.mult, ALU.add,
                reads=[BY[a], cm.Bmod, Bxt[b]], writes=[Bxt[b]])
            b_ = Buf()
            p.dma("sp", fm(xo)[:, oc, t0:t0 + n], xt[b][:, :n], reads=[Bxt[b]], writes=[b_])
            Bout.append(b_)
    p.wait_all("sp", Bout)
    return p.finalize()
def build_fta():
    p = Prog()
    xT = p.dram_in("xT", [1024, NT])
    g_d = p.dram_in("g", [128, 8])
    cw_d = p.dram_in("cwsw", [2, 256, 256])
    ZR = p.dram_out("ZR", [1024, NT])
    ZI = p.dram_out("ZI", [1024, NT])
    cm = Common(p)
    s1, Bs1 = cm.scale_vec(g_d, 1, "s1")
    tb = p.sbuf([128, 2, 2, 256], F32, "tb"); Btb = Buf()
    for i in range(2):
        p.dma("sp", tb[:, i], cw_d[i].rearrange("(kk p) m -> p kk m", p=128), writes=[Btb])
    ps = [p.psum([128, 512], F32, "bank%d" % i) for i in range(8)]
    Bps = [Buf() for _ in range(8)]
    xt = [p.sbuf([128, 8, 512], F32, "xt%d" % i) for i in range(2)]
    Bxt = [[Buf() for _ in range(8)] for _ in range(2)]
    h32 = [p.sbuf([128, 8, 512], F32, "h32_%d" % i) for i in range(2)]
    Bh = [[Buf() for _ in range(8)] for _ in range(2)]
    ob = [p.sbuf([128, 512], F32, "ob%d" % i) for i in range(3)]; Bob = [Buf() for _ in range(3)]
    nm = NormMod(p, cm, s1, Bs1, 0, ps[7], Bps[7])
    xv = fm(xT)
    Bout = []
    i2 = 0
    for ti, (t0, n, st) in enumerate(TT):
        b = ti % 2
        for c in range(8):
            p.dma("sp", xt[b][:, c, :n], xv[:, c, t0:t0 + n], writes=[Bxt[b][c]])
        nm.run([xt[b][:, c, :n] for c in range(8)], Bxt[b], n, st, [[h32[b][:, c, :n] for c in range(8)]], [Bh[b]])
        for oc in range(8):
            gI = oc // 2
            for ri, dst in enumerate((ZR, ZI)):
                a = i2 % 6
                o = i2 % 3
                i2 += 1
                for kk in range(2):
                    p.I("pe", "matmul", ps[a][:, :n], tb[:, ri, kk, (oc % 2) * 128:(oc % 2 + 1) * 128], h32[b][:, 2 * gI + kk, :n],
                        start=(kk == 0), stop=(kk == 1), reads=[Btb, Bh[b][2 * gI + kk]], writes=[Bps[a]])
                p.I("act", "activation", ob[o][:, :n], ps[a][:, :n], AF.Identity, reads=[Bps[a]], writes=[Bob[o]])
                b_ = Buf()
                p.dma("sp", fm(dst)[:, oc, t0:t0 + n], ob[o][:, :n], reads=[Bob[o]], writes=[b_])
                Bout.append(b_)
    p.wait_all("sp", Bout)
    return p.finalize()


def build_ftb():
    p = Prog()
    Zr_d = p.dram_in("Zr", [128, 128, 128])
    Zi_d = p.dram_in("Zi", [128, 128, 128])
    Zc_d = p.dram_in("Zc", [2, 256, 128])
    R_d = p.dram_in("R12", [2, 128, 256])
    CS_d = p.dram_in("CS1", [2, 128, 128])
    TW_d = p.dram_in("TW", [2, 128, 128])
    F256_d = p.dram_in("F256", [2, 256, 256])
    F_o = p.dram_out("F", [128, 128, 128])
    Fc_o = p.dram_out("Fc", [256, 128])
    R = p.sbuf([128, 2, 256], F32, "R"); CS = p.sbuf([128, 2, 128], F32, "CS"); TW = p.sbuf([128, 2, 128], F32, "TW")
    Bt = Buf()
    for i in range(2):
        p.dma("sp", R[:, i], R_d[i], writes=[Bt]); p.dma("sp", CS[:, i], CS_d[i], writes=[Bt]); p.dma("sp", TW[:, i], TW_d[i], writes=[Bt])
    F2 = p.sbuf([128, 2, 2, 256], F32, "F2")
    for i in range(2):
        p.dma("sp", F2[:, i], F256_d[i].rearrange("(tc p) k -> p tc k", p=128), writes=[Bt])
    zc = p.sbuf([128, 2, 2, 128], F32, "zc")
    for i in range(2):
        p.dma("sp", zc[:, i], Zc_d[i].rearrange("(tc p) c -> p tc c", p=128), writes=[Bt])
    CG = 32
    zr = [p.sbuf([128, CG, 128], F32, "zr%d" % i) for i in range(2)]; zi = [p.sbuf([128, CG, 128], F32, "zi%d" % i) for i in range(2)]
    Bz = [Buf() for _ in range(2)]
    ar = [p.sbuf([128, CG, 128], F32, "ar%d" % i) for i in range(2)]; ai = [p.sbuf([128, CG, 128], F32, "ai%d" % i) for i in range(2)]
    Ba = [[Buf() for _ in range(CG // 4)] for _ in range(2)]
    tmp = [p.sbuf([128, 128], F32, "tw_t%d" % i) for i in range(4)]; Btmp = [Buf() for _ in range(4)]
    ob = [p.sbuf([128, 512], F32, "ob%d" % i) for i in range(3)]; Bob = [Buf() for _ in range(3)]
    ps = [p.psum([128, 512], F32, "bank%d" % i) for i in range(8)]
    Bps = [Buf() for _ in range(8)]
    Bout = []
    i1 = 0
    i3 = 0
    for gi in range(128 // CG):
        b = gi % 2
        c0 = gi * CG
        p.dma("sp", zr[b][:], Zr_d[:, c0:c0 + CG, :], writes=[Bz[b]])
        p.dma("sp", zi[b][:], Zi_d[:, c0:c0 + CG, :], writes=[Bz[b]])
        for c in range(CG):
            a = i1 % 4
            i1 += 1
            A = ps[a][:, 0:256]
            p.I("pe", "matmul", A, zr[b][:, c, :], R[:, 0, :], start=True, stop=False, reads=[Bz[b], Bt], writes=[Bps[a]])
            p.I("pe", "matmul", A, zi[b][:, c, :], R[:, 1, :], start=False, stop=True, reads=[Bz[b], Bt], writes=[Bps[a]])
            Ar, Ai = ps[a][:, 0:128], ps[a][:, 128:256]
            Tc, Ts = TW[:, 0, :], TW[:, 1, :]
            Bq = Ba[b][c // 4]
            p.I("dve", "tensor_tensor", tmp[0][:], Ar, Tc, ALU.mult, reads=[Bps[a], Bt], writes=[Btmp[0]])
            p.I("dve", "tensor_tensor", tmp[1][:], Ai, Ts, ALU.mult, reads=[Bps[a], Bt], writes=[Btmp[1]])
            p.I("pool", "tensor_tensor", ar[b][:, c, :], tmp[0][:], tmp[1][:], ALU.add, reads=[Btmp[0], Btmp[1]], writes=[Bq])
            p.I("dve", "tensor_tensor", tmp[2][:], Ai, Tc, ALU.mult, reads=[Bps[a], Bt], writes=[Btmp[2]])
            p.I("dve", "tensor_tensor", tmp[3][:], Ar, Ts, ALU.mult, reads=[Bps[a], Bt], writes=[Btmp[3]])
            p.I("pool", "tensor_tensor", ai[b][:, c, :], tmp[2][:], tmp[3][:], ALU.subtract, reads=[Btmp[2], Btmp[3]], writes=[Bq])
        for q in range(CG // 4):
            a = 4 + i3 % 3
            o = i3 % 3
            i3 += 1
            rr = ar[b][:, 4 * q:4 * q + 4, :].rearrange("p c k -> p (c k)")
            ii = ai[b][:, 4 * q:4 * q + 4, :].rearrange("p c k -> p (c k)")
            p.I("pe", "matmul", ps[a][:], CS[:, 0, :], rr, start=True, stop=False, reads=[Bt, Ba[b][q]], writes=[Bps[a]])
            p.I("pe", "matmul", ps[a][:], CS[:, 1, :], ii, start=False, stop=True, reads=[Bt, Ba[b][q]], writes=[Bps[a]])
            p.I("act", "activation", ob[o][:], ps[a][:], AF.Identity, reads=[Bps[a]], writes=[Bob[o]])
            b_ = Buf()
            cc = c0 + 4 * q
            p.dma("sp", F_o[:, cc:cc + 4, :], ob[o][:].rearrange("p (c k) -> p c k", c=4), reads=[Bob[o]], writes=[b_])
            Bout.append(b_)
    for kc in range(2):
        a = 7
        first = True
        for tc in range(2):
            for ri in range(2):
                p.I("pe", "matmul", ps[a][:, 0:128], F2[:, ri, tc, kc * 128:(kc + 1) * 128], zc[:, ri, tc, :],
                    start=first, stop=(tc == 1 and ri == 1), reads=[Bt], writes=[Bps[a]])
                first = False
        o = i3 % 3
        i3 += 1
        p.I("act", "activation", ob[o][:, 0:128], ps[a][:, 0:128], AF.Identity, reads=[Bps[a]], writes=[Bob[o]])
        b_ = Buf()
        p.dma("sp", Fc_o[kc * 128:(kc + 1) * 128, :], ob[o][:, 0:128], reads=[Bob[o]], writes=[b_])
        Bout.append(b_)
    p.wait_all("sp", Bout)
    return p.finalize()


def build_proj():
    p = Prog()
    fT = p.dram_in("fT", [1024, NT])
    w_d = p.dram_in("w", [1024, 1024])
    xT = p.dram_in("xT", [1024, NT])
    xo = p.dram_out("xo", [1024, NT])
    cm = Common(p)
    cst = Caster(p, 3, 1152)
    wo = p.sbuf([128, 8, 1024], BF16, "wo"); Bwo = [Buf() for _ in range(8)]
    for k in range(8):
        cst.load(wo[:, k, :], w_d.rearrange("(k p) m -> p k m", p=128)[:, k, :], Bwo[k])
    fb = p.sbuf([128, 8, NT], BF16, "fb"); Bfb = [Buf() for _ in range(8)]
    for c in range(8):
        for hh in range(2):
            cst.load(fb[:, c, hh * 1152:(hh + 1) * 1152], fm(fT)[:, c, hh * 1152:(hh + 1) * 1152], Bfb[c], eng="act" if c % 2 else "pool")
    Y = [p.psum([128, 512], F32, "Y%d" % i) for i in range(2)]; BY = [Buf() for _ in range(2)]
    xt = [p.sbuf([128, 512], F32, "xt%d" % i) for i in range(3)]; Bxt = [Buf() for _ in range(3)]
    Bout = []
    xi = 0
    for ti, (t0, n, st) in enumerate(TT):
        for oc in range(8):
            b = xi % 3
            a = xi % 2
            xi += 1
            p.dma("sp", xt[b][:, :n], fm(xT)[:, oc, t0:t0 + n], writes=[Bxt[b]])
            for k in range(8):
                p.I("pe", "matmul", Y[a][:, :n], wo[:, k, oc * 128:(oc + 1) * 128], fb[:, k, t0:t0 + n], start=(k == 0), stop=(k == 7),
                    reads=[Bwo[k], Bfb[k]], writes=[BY[a]])
            p.I("dve", "scalar_tensor_tensor", xt[b][:, :n], Y[a][:, :n], cm.mod[:, 16 + oc, st:st + 1], xt[b][:, :n], ALU.mult, ALU.add,
                reads=[BY[a], cm.Bmod, Bxt[b]], writes=[Bxt[b]])
            b_ = Buf()
            p.dma("sp", fm(xo)[:, oc, t0:t0 + n], xt[b][:, :n], reads=[Bxt[b]], writes=[b_])
            Bout.append(b_)
    p.wait_all("sp", Bout)
    return p.finalize()
NCORES = 8
_PROGS = {}
DEBUG = {}


def _prog(name, fn, *a):
    key = (name,) + a
    if key not in _PROGS:
        _PROGS[key] = fn(*a)
    return _PROGS[key]


def _run(nc, maps):
    if DEBUG.get("trace"):
        res = run_bass_kernel_spmd(nc, maps, core_ids=list(range(NCORES)), trace=True)
        print("EXEC_NS", res.exec_time_ns, flush=True)
        return res.results
    res = run_bass_kernel_spmd(nc, maps, core_ids=list(range(NCORES)))
    return res.results


def to_fm(v):
    return np.ascontiguousarray(np.asarray(v, np.float32).reshape(-1, 128).T)


def _split_x(xTs):
    lat = np.concatenate([r[:, :2048].T for r in xTs], 0)
    return np.ascontiguousarray(lat), np.ascontiguousarray(xTs[0][:, 2048:].T)


def _core_xT(lat, ctx):
    return [np.ascontiguousarray(np.concatenate([lat[2048 * k:2048 * (k + 1)], ctx], 0).T) for k in range(NCORES)]


def run_ada(inp):
    cv = np.stack([to_fm(inp["c"][0]), to_fm(inp["c_ctx"])], -1)
    maps = []
    for k in range(NCORES):
        w = np.ascontiguousarray(inp["ada_w"][:, :, 768 * k:768 * (k + 1)])
        b = np.ascontiguousarray(inp["ada_b"][:, 768 * k:768 * (k + 1)].reshape(4, 6, 128).transpose(0, 2, 1))
        maps.append({"w": w, "b": b, "cv": cv})
    res = _run(_prog("ada", build_ada), maps)
    return [np.ascontiguousarray(np.concatenate([r["o"][l] for r in res], 1)) for l in range(4)]


def run_ffn(xTs, mod, g, wgu, wdn, router=None):
    E = wgu.shape[0]
    maps = []
    for k in range(NCORES):
        m = {"xT": xTs[k], "mod": mod, "g": to_fm(g), "wgu": wgu, "wdn": wdn}
        if E > 1:
            sel = np.zeros((8, 8, 128), np.float32)
            for e in range(8):
                sel[e, e, :] = 1
            m.update({"router": router, "ident": np.eye(128, dtype=np.float32), "sel": sel})
        maps.append(m)
    res = _run(_prog("ffn", build_ffn, E), maps)
    return [r["xo"] for r in res]


def run_rg(xTs, mod, g, inp, j):
    lat, ctx = _split_x(xTs)
    cw = np.ascontiguousarray(inp["rg_conv_w"][j].reshape(4, 8, 128).transpose(2, 1, 0))
    maps = []
    for k in range(NCORES):
        xh = np.zeros((3, 1024), np.float32)
        hm = np.zeros((128, 3), np.float32)
        for i, t in enumerate((2048 * k - 2, 2048 * k - 1, 2048 * (k + 1))):
            if 0 <= t < 16384:
                xh[i] = lat[t]
                hm[:, i] = 1.0
        maps.append({"xT": xTs[k], "xh": np.ascontiguousarray(xh.T), "hmask": hm, "g": to_fm(g), "mod": mod,
                     "w_in": inp["rg_w_in"][j], "conv_w": cw, "conv_b": to_fm(inp["rg_conv_b"][j])})
    ra = _run(_prog("rga", build_rga), maps)
    gb = np.zeros((128, 3, 2, 8), np.float32)
    for i, nm_ in enumerate(("rg_ba", "rg_bi", "rg_lambda")):
        for d in range(2):
            gb[:, i, d, :] = to_fm(inp[nm_][j, d])
    maps = [{"XC": ra[k]["XC"], "wa": inp["rg_wa"][j], "wi": inp["rg_wi"][j], "gbias": gb} for k in range(NCORES)]
    rb = _run(_prog("rgs", build_rgs, False), maps)
    comp_all = np.ascontiguousarray(np.stack([r["comp"] for r in rb], 2))
    DEBUG["comp_all"] = comp_all
    maps2 = []
    for k in range(NCORES):
        oh = np.zeros((128, 2, 9), np.float32)
        oh[:, 0, k] = 1.0
        oh[:, 1, 7 - k] = 1.0
        m = dict(maps[k])
        m.update({"comp_all": comp_all, "onehot": oh, "G": ra[k]["G"], "w_out": inp["rg_w_out"][j], "xT": xTs[k], "mod": mod})
        maps2.append(m)
    rc = _run(_prog("rgs", build_rgs, True), maps2)
    return [r["xo"] for r in rc]
import ml_dtypes


def _na_slot_pairs(k):
    pairs = []
    for lp in range(NSLOT):
        P = 16 * k - 2 + lp
        pairs.append(P if 0 <= P < 128 else None)
    sub = {}
    if k == 0:
        pairs[1] = 3
        sub[1] = 0
    if k == NCORES - 1:
        pairs[18] = 124
        sub[18] = 15
    return pairs, sub


def _na_tables(rpb, k):
    pairs, sub = _na_slot_pairs(k)
    tab = np.full((16, 128, 25, 128), -30000.0, np.float32)
    ki = np.arange(128)
    qi = np.arange(128)
    for st_, jl in enumerate((0, 1, 5, 14, 15)):
        for d in range(5):
            lp = jl + d
            P = pairs[lp]
            if P is None or (lp in sub and sub[lp] != jl):
                continue
            kr = 2 * P + ki // 64
            kc = ki % 64
            r = 2 * (16 * k + jl) + qi // 64
            qc = qi % 64
            rs = np.clip(r - 4, 0, 248)
            cs = np.clip(qc - 8, 0, 48)
            ok = ((kr[:, None] >= rs[None, :]) & (kr[:, None] < rs[None, :] + 8) &
                  (kc[:, None] >= cs[None, :]) & (kc[:, None] < cs[None, :] + 16))
            ri = np.clip(kr[:, None] - r[None, :] + 7, 0, 14)
            ci = np.clip(kc[:, None] - qc[None, :] + 15, 0, 30)
            vals = rpb[:, ri, ci]
            tab[:, :, st_ * 5 + d, :] = np.where(ok[None], vals, np.float32(-30000.0))
    return tab


def run_na(xTs, mod, g, inp):
    qkg = np.zeros((128, 2), np.float32)
    qkg[:, 0] = np.tile(inp["na_q_g"][0], 2)
    qkg[:, 1] = np.tile(inp["na_k_g"][0], 2)
    sc = np.zeros((128, 2), np.float32)
    maps = [{"xT": xTs[k], "mod": mod, "g": to_fm(g), "w_qkv": inp["na_w_qkv"][0], "qkg": qkg, "qscale": sc} for k in range(NCORES)]
    for m in maps:
        m.pop("qscale")
    ra = _run(_prog("naa", build_naa), maps)
    Kfull = np.concatenate([r["KT"][:, :2048] for r in ra], 1)
    Kctx = ra[0]["KT"][:, 2048:]
    Vfull = np.concatenate([r["V"][:2048] for r in ra], 0)
    Vctx = ra[0]["V"][2048:]
    maps2 = []
    for k in range(NCORES):
        pairs, _ = _na_slot_pairs(k)
        kt = np.zeros((1024, KW), Kfull.dtype)
        vt = np.zeros((KW, 1024), Vfull.dtype)
        for lp, P in enumerate(pairs):
            if P is not None:
                kt[:, lp * 128:(lp + 1) * 128] = Kfull[:, P * 128:(P + 1) * 128]
                vt[lp * 128:(lp + 1) * 128] = Vfull[P * 128:(P + 1) * 128]
        kt[:, NSLOT * 128:] = Kctx
        vt[NSLOT * 128:] = Vctx
        vp = np.ascontiguousarray(vt.reshape(NKT, 128, 8, 128).transpose(2, 1, 0, 3))
        maps2.append({"QT": ra[k]["QT"], "KText": kt, "Vp": vp, "tab": _na_tables(inp["na_rpb"][0], k), "ident": np.eye(128, dtype=np.float32),
                      "w_o": inp["na_w_o"][0], "xT": xTs[k], "mod": mod})
    rb = _run(_prog("nab", build_nab), maps2)
    return [r["xo"] for r in rb]
def _dft_tables():
    k = np.arange(256)
    ang = 2 * np.pi * np.outer(k, k) / 256.0
    cwsw = np.stack([np.cos(ang) / 16.0, -np.sin(ang) / 16.0]).astype(np.float32)
    F256 = np.stack([np.cos(ang) / 16.0, np.sin(ang) / 16.0]).astype(np.float32)
    j = np.arange(128)
    a1 = 2 * np.pi * np.outer(j, j) / 128.0
    s = 1.0 / np.sqrt(128.0)
    C1, S1 = np.cos(a1) * s, np.sin(a1) * s
    R12 = np.stack([np.concatenate([C1, -S1], 1), np.concatenate([S1, C1], 1)]).astype(np.float32)
    CS1 = np.stack([C1, S1]).astype(np.float32)
    at = 2 * np.pi * np.outer(j, j) / 16384.0
    TW = np.stack([np.cos(at), np.sin(at)]).astype(np.float32)
    return cwsw, F256, R12, CS1, TW


def run_ft(xTs, mod, g, inp):
    cwsw, F256, R12, CS1, TW = _dft_tables()
    maps = [{"xT": xTs[k], "mod": mod, "g": to_fm(g), "cwsw": cwsw} for k in range(NCORES)]
    ra = _run(_prog("fta", build_fta), maps)
    Z = []
    Zc = []
    for nm_ in ("ZR", "ZI"):
        Z.append(np.concatenate([r[nm_][:, :2048].T for r in ra], 0))
        Zc.append(ra[0][nm_][:, 2048:].T)
    maps2 = []
    for k in range(NCORES):
        sl = slice(128 * k, 128 * (k + 1))
        m = {"R12": R12, "CS1": CS1, "TW": TW, "F256": F256,
             "Zc": np.ascontiguousarray(np.stack([Zc[0][:, sl], Zc[1][:, sl]]))}
        for nm_, z in (("Zr", Z[0]), ("Zi", Z[1])):
            m[nm_] = np.ascontiguousarray(z[:, sl].reshape(128, 128, 128).transpose(0, 2, 1))
        maps2.append(m)
    rb = _run(_prog("ftb", build_ftb), maps2)
    f = np.concatenate([r["F"].transpose(0, 2, 1).reshape(16384, 128) for r in rb], 1)
    fc = np.concatenate([r["Fc"] for r in rb], 1)
    DEBUG["f"] = f
    fTs = _core_xT(f, fc)
    maps3 = [{"fT": fTs[k], "w": inp["ft_w_out"][0], "xT": xTs[k], "mod": mod} for k in range(NCORES)]
    rc = _run(_prog("proj", build_proj), maps3)
    return [r["xo"] for r in rc]
def kernel(**inp):
    inp = {k: np.asarray(v) for k, v in inp.items()}
    mods = run_ada(inp)
    xTs = _core_xT(inp["x"][0], inp["ctx"][0])
    ng = inp["norm_g"]
    xTs = run_rg(xTs, mods[0], ng[0, 0], inp, 0)
    xTs = run_ffn(xTs, mods[0], ng[0, 1], inp["ffn_w_gu"][0:1], inp["ffn_w_down"][0:1])
    xTs = run_na(xTs, mods[1], ng[1, 0], inp)
    xTs = run_ffn(xTs, mods[1], ng[1, 1], inp["moe_w_gu"][0], inp["moe_w_down"][0], inp["moe_router"][0])
    xTs = run_ft(xTs, mods[2], ng[2, 0], inp)
    xTs = run_ffn(xTs, mods[2], ng[2, 1], inp["ffn_w_gu"][1:2], inp["ffn_w_down"][1:2])
    xTs = run_rg(xTs, mods[3], ng[3, 0], inp, 1)
    xTs = run_ffn(xTs, mods[3], ng[3, 1], inp["moe_w_gu"][1], inp["moe_w_down"][1], inp["moe_router"][1])
    lat, _ = _split_x(xTs)
    return np.ascontiguousarray(lat[None].astype(np.float32))
```
